# Optimizing a Trainium2 kernel written in Bass

```python
import math
import jax, jax.numpy as jnp
from jax import lax
import numpy as np


D_MODEL = 2048
BATCH = 8
SEQ = 2048
DEPTH = 2

GRID_W = 64
CTX_LEN = 256
EPS = 1e-6
ROPE_THETA = 10000.0
Q_BLOCK = 128
D_FF = 5632
N_MOD = 9
ADA_STD = 0.5

MLA_HEADS = 8
MLA_Q_RANK = 512
MLA_KV_RANK = 256
MLA_NOPE = 128
MLA_ROPE = 64
MLA_V = 128
MLA_SCALE = (MLA_NOPE + MLA_ROPE) ** -0.5

DIFF_HEADS = 4
DIFF_QK = 64
DIFF_V = 2 * DIFF_QK
DIFF_SCALE = DIFF_QK ** -0.5
DIFF_LAMBDA_STD = 0.1

FOURIER_GROUPS = 4
FOURIER_CH = 128

MLA_WIDTH = MLA_HEADS * MLA_V
DIFF_WIDTH = DIFF_HEADS * DIFF_V
FOURIER_WIDTH = FOURIER_GROUPS * FOURIER_CH
MIX_WIDTH = MLA_WIDTH + DIFF_WIDTH + FOURIER_WIDTH

OFF_KV_LAT = 0
OFF_K_ROPE = OFF_KV_LAT + MLA_KV_RANK
OFF_DK = OFF_K_ROPE + MLA_ROPE
OFF_DV = OFF_DK + 2 * DIFF_HEADS * DIFF_QK
KV_COLS = OFF_DV + DIFF_HEADS * DIFF_V
OFF_Q_LAT = 0
OFF_DQ = OFF_Q_LAT + MLA_Q_RANK
OFF_FOUR = OFF_DQ + 2 * DIFF_HEADS * DIFF_QK
Q_COLS = OFF_FOUR + FOURIER_WIDTH
IN_COLS = KV_COLS + Q_COLS

kernel_name = 'hybrid_mla_diff_fourier_macaron_dit'


def rmsnorm(x, g):
    xf = x.astype(jnp.float32)
    y = xf * lax.rsqrt(jnp.mean(xf * xf, axis=-1, keepdims=True) + EPS)
    return y.astype(x.dtype) * g


def modulate(x, g, shift, scale):
    return rmsnorm(x, g) * (1 + scale) + shift


def swiglu(h, wg, wu, wd):
    return (jax.nn.silu(h @ wg) * (h @ wu)) @ wd


def ffn_sublayer(z, shift, scale, gate, g, wg, wu, wd):
    return z + 0.5 * gate * swiglu(modulate(z, g, shift, scale), wg, wu, wd)


def rope_1d(x, pos):
    d = x.shape[-1]
    half = d // 2
    inv = ROPE_THETA ** (-2.0 * jnp.arange(half, dtype=jnp.float32) / d)
    ang = pos.astype(jnp.float32)[:, None] * inv[None, :]
    cos = jnp.cos(ang)[:, None, :].astype(x.dtype)
    sin = jnp.sin(ang)[:, None, :].astype(x.dtype)
    x1, x2 = x[..., :half], x[..., half:]
    return jnp.concatenate([x1 * cos - x2 * sin, x1 * sin + x2 * cos], axis=-1)


def rope_2d(x, pos):
    if pos is None:
        return x
    row, col = pos
    half = x.shape[-1] // 2
    return jnp.concatenate([rope_1d(x[..., :half], row), rope_1d(x[..., half:], col)], axis=-1)


def kv_side(pk, kv_norm, w_ukv, pos):
    B, S = pk.shape[:2]
    c_kv = pk[..., OFF_KV_LAT:OFF_K_ROPE]
    k_rope = pk[..., OFF_K_ROPE:OFF_DK]
    dk = pk[..., OFF_DK:OFF_DV]
    dv = pk[..., OFF_DV:KV_COLS]
    kv = (rmsnorm(c_kv, kv_norm) @ w_ukv).reshape(B, S, MLA_HEADS, MLA_NOPE + MLA_V)
    k_nope, mla_v = kv[..., :MLA_NOPE], kv[..., MLA_NOPE:]
    k_rope = rope_2d(k_rope[:, :, None, :], pos)
    mla_k = jnp.concatenate([k_nope, jnp.broadcast_to(k_rope, (B, S, MLA_HEADS, MLA_ROPE))], axis=-1)
    dk = rope_2d(dk.reshape(B, S, 2 * DIFF_HEADS, DIFF_QK), pos).reshape(B, S, DIFF_HEADS, 2, DIFF_QK)
    dv = dv.reshape(B, S, DIFF_HEADS, DIFF_V)
    return (mla_k, mla_v, dk[..., 0, :], dk[..., 1, :], dv)


def q_side(pq, q_norm, w_uq, pos):
    B, S = pq.shape[:2]
    c_q = pq[..., OFF_Q_LAT:OFF_DQ]
    dq = pq[..., OFF_DQ:OFF_FOUR]
    u = pq[..., OFF_FOUR:Q_COLS]
    q = (rmsnorm(c_q, q_norm) @ w_uq).reshape(B, S, MLA_HEADS, MLA_NOPE + MLA_ROPE)
    mla_q = jnp.concatenate([q[..., :MLA_NOPE], rope_2d(q[..., MLA_NOPE:], pos)], axis=-1)
    dq = rope_2d(dq.reshape(B, S, 2 * DIFF_HEADS, DIFF_QK), pos).reshape(B, S, DIFF_HEADS, 2, DIFF_QK)
    return (mla_q, dq[..., 0, :], dq[..., 1, :], u)


def softmax_f32(s):
    return jax.nn.softmax(s.astype(jnp.float32), axis=-1)


def mla_block(q, k, v):
    s = jnp.einsum('bqhd,bkhd->bhqk', q, k) * MLA_SCALE
    p = softmax_f32(s).astype(v.dtype)
    return jnp.einsum('bhqk,bkhd->bqhd', p, v)


def diff_block(q1, q2, k1, k2, v, lam):
    s1 = jnp.einsum('bqhd,bkhd->bhqk', q1, k1) * DIFF_SCALE
    s2 = jnp.einsum('bqhd,bkhd->bhqk', q2, k2) * DIFF_SCALE
    p = (softmax_f32(s1) - lam * softmax_f32(s2)).astype(v.dtype)
    return jnp.einsum('bhqk,bkhd->bqhd', p, v)


def over_query_blocks(fn, qs):
    B, S = qs[0].shape[:2]
    blk = min(Q_BLOCK, S)
    nb = S // blk
    xs = tuple(jnp.swapaxes(q.reshape(B, nb, blk, *q.shape[2:]), 0, 1) for q in qs)
    out = lax.map(lambda t: fn(*t), xs)
    return jnp.swapaxes(out, 0, 1).reshape(B, S, *out.shape[3:])


def fourier_mix(u):
    B, S = u.shape[:2]
    g = u.reshape(B, S, FOURIER_GROUPS, FOURIER_CH).astype(jnp.float32)
    f = jnp.fft.fft2(g, axes=(1, 3), norm='ortho').real
    return f.reshape(B, S, FOURIER_WIDTH).astype(u.dtype)


def token_mix(qp, kvp, lam, lam_init, subln, w_out):
    mla_q, dq1, dq2, u = qp
    mla_k, mla_v, dk1, dk2, dv = kvp
    B, S = u.shape[:2]
    o_mla = over_query_blocks(lambda q: mla_block(q, mla_k, mla_v), (mla_q,))
    o_diff = over_query_blocks(lambda a, b: diff_block(a, b, dk1, dk2, dv, lam), (dq1, dq2))
    o_diff = rmsnorm(o_diff, subln) * (1.0 - lam_init)
    o = jnp.concatenate([o_mla.reshape(B, S, MLA_WIDTH), o_diff.reshape(B, S, DIFF_WIDTH), fourier_mix(u)], axis=-1)
    return o @ w_out


def setup_inputs(seed: int = 0) -> dict:
    key = jax.random.key(seed)
    ks = jax.random.split(key, 20)

    def nrm(k, shape, std):
        return jax.random.normal(k, shape, jnp.float32) * std

    return {
        'x': nrm(ks[0], (BATCH, SEQ, D_MODEL), 1.0),
        'c': nrm(ks[1], (BATCH, D_MODEL), 1.0),
        'ctx': nrm(ks[2], (BATCH, CTX_LEN, D_MODEL), 1.0),
        'c_ctx': nrm(ks[3], (D_MODEL,), 1.0),
        'ada_w': nrm(ks[4], (DEPTH, D_MODEL, N_MOD * D_MODEL), ADA_STD * D_MODEL ** -0.5),
        'ada_b': nrm(ks[5], (DEPTH, N_MOD * D_MODEL), 0.02),
        'norm_g': 1.0 + nrm(ks[6], (DEPTH, 3, D_MODEL), 0.02),
        'ffn_wg': nrm(ks[7], (DEPTH, 2, D_MODEL, D_FF), D_MODEL ** -0.5),
        'ffn_wu': nrm(ks[8], (DEPTH, 2, D_MODEL, D_FF), D_MODEL ** -0.5),
        'ffn_wd': nrm(ks[9], (DEPTH, 2, D_FF, D_MODEL), D_FF ** -0.5),
        'w_in': nrm(ks[10], (DEPTH, D_MODEL, IN_COLS), D_MODEL ** -0.5),
        'mla_q_norm': 1.0 + nrm(ks[11], (DEPTH, MLA_Q_RANK), 0.02),
        'mla_kv_norm': 1.0 + nrm(ks[12], (DEPTH, MLA_KV_RANK), 0.02),
        'mla_w_uq': nrm(ks[13], (DEPTH, MLA_Q_RANK, MLA_HEADS * (MLA_NOPE + MLA_ROPE)), MLA_Q_RANK ** -0.5),
        'mla_w_ukv': nrm(ks[14], (DEPTH, MLA_KV_RANK, MLA_HEADS * (MLA_NOPE + MLA_V)), MLA_KV_RANK ** -0.5),
        'diff_lambda': nrm(ks[15], (DEPTH, 4, DIFF_QK), DIFF_LAMBDA_STD),
        'diff_subln': 1.0 + nrm(ks[16], (DEPTH, DIFF_V), 0.02),
        'w_out': nrm(ks[17], (DEPTH, MIX_WIDTH, D_MODEL), MIX_WIDTH ** -0.5),
        'final_norm': 1.0 + nrm(ks[18], (D_MODEL,), 0.02),
    }


def reference(x, c, ctx, c_ctx, ada_w, ada_b, norm_g, ffn_wg, ffn_wu, ffn_wd, w_in,
              mla_q_norm, mla_kv_norm, mla_w_uq, mla_w_ukv, diff_lambda, diff_subln,
              w_out, final_norm):
    B, n, D = x.shape
    rows = n // GRID_W
    pos = (jnp.repeat(jnp.arange(rows), GRID_W), jnp.tile(jnp.arange(GRID_W), rows))
    xc = ctx
    s_lat = jax.nn.silu(c)
    s_ctx = jax.nn.silu(c_ctx)[None, :]
    for l in range(DEPTH):
        last = l == DEPTH - 1
        lam_init = 0.8 - 0.6 * math.exp(-0.3 * l)
        lf = diff_lambda[l].astype(jnp.float32)
        lam = jnp.exp(jnp.sum(lf[0] * lf[1])) - jnp.exp(jnp.sum(lf[2] * lf[3])) + lam_init
        m = (s_lat @ ada_w[l] + ada_b[l]).reshape(B, N_MOD, 1, D)
        mc = (s_ctx @ ada_w[l] + ada_b[l]).reshape(1, N_MOD, 1, D)

        x = ffn_sublayer(x, m[:, 0], m[:, 1], m[:, 2], norm_g[l, 0], ffn_wg[l, 0], ffn_wu[l, 0], ffn_wd[l, 0])
        xc = ffn_sublayer(xc, mc[:, 0], mc[:, 1], mc[:, 2], norm_g[l, 0], ffn_wg[l, 0], ffn_wu[l, 0], ffn_wd[l, 0])

        h = modulate(x, norm_g[l, 1], m[:, 3], m[:, 4])
        hc = modulate(xc, norm_g[l, 1], mc[:, 3], mc[:, 4])
        p = h @ w_in[l]
        if last:
            pc_kv = hc @ w_in[l][:, :KV_COLS]
        else:
            pc = hc @ w_in[l]
            pc_kv = pc[..., :KV_COLS]
        ctx_kv = kv_side(pc_kv, mla_kv_norm[l], mla_w_ukv[l], None)
        lat_kv = kv_side(p[..., :KV_COLS], mla_kv_norm[l], mla_w_ukv[l], pos)
        full_kv = tuple(jnp.concatenate([a, b], axis=1) for a, b in zip(ctx_kv, lat_kv))
        lat_q = q_side(p[..., KV_COLS:], mla_q_norm[l], mla_w_uq[l], pos)
        x = x + m[:, 5] * token_mix(lat_q, full_kv, lam, lam_init, diff_subln[l], w_out[l])
        if not last:
            ctx_q = q_side(pc[..., KV_COLS:], mla_q_norm[l], mla_w_uq[l], None)
            xc = xc + mc[:, 5] * token_mix(ctx_q, ctx_kv, lam, lam_init, diff_subln[l], w_out[l])

        x = ffn_sublayer(x, m[:, 6], m[:, 7], m[:, 8], norm_g[l, 2], ffn_wg[l, 1], ffn_wu[l, 1], ffn_wd[l, 1])
        if not last:
            xc = ffn_sublayer(xc, mc[:, 6], mc[:, 7], mc[:, 8], norm_g[l, 2], ffn_wg[l, 1], ffn_wu[l, 1], ffn_wd[l, 1])
    return rmsnorm(x, final_norm)
```

```python
import math
from contextlib import ExitStack

import numpy as np
import concourse.bass as bass
import concourse.mybir as mybir
from concourse.bass_utils import run_bass_kernel_spmd

F32 = mybir.dt.float32
BF16 = mybir.dt.bfloat16
AF = mybir.ActivationFunctionType
ALU = mybir.AluOpType

D = 2048
NTOK = 2304
NCTX = 256
NLAT = 2048
DFF = 5632
EPS = 1e-6
ENGS = ("tensor", "vector", "scalar", "gpsimd", "sync")


class Buf:
    __slots__ = ("name", "last_w", "readers", "sem")

    def __init__(self, name=""):
        self.name = name
        self.last_w = None
        self.readers = []
        self.sem = None


class Op:
    __slots__ = ("eng", "fn", "deps", "is_dma", "sem", "val", "signal")

    def __init__(self, eng, fn, is_dma):
        self.eng = eng
        self.fn = fn
        self.deps = []
        self.is_dma = is_dma
        self.sem = None
        self.val = None
        self.signal = is_dma


class Prog:
    def __init__(self, nc, n_dma_sems=64):
        self.nc = nc
        self.eng_sem = {}
        self.eng_cnt = {e: 0 for e in ENGS}
        self.dma_sems = []
        self.dma_cnt = []
        self.n_dma_sems = n_dma_sems
        self.ops = []
        self.barrier_deps = {}
        self.free_dma = {}
        self.dma_last = {}
        self.n_inst = 0
        self._waited = {e: {} for e in ENGS}
        self._phase_sembufs = []
        self.per_eng_inst = {}

    def open(self, stack):
        for e in ENGS:
            self.eng_sem[e] = stack.enter_context(self.nc.semaphore("es_" + e))
        for i in range(self.n_dma_sems):
            self.dma_sems.append(stack.enter_context(self.nc.semaphore("ds%d" % i)))
            self.dma_cnt.append(0)
        half = self.n_dma_sems // 2
        self.free_dma = {True: list(range(half)), False: list(range(half, self.n_dma_sems))}

    def _track(self, op, reads, writes):
        deps = op.deps
        for b in reads:
            if b.last_w is not None:
                deps.append(b.last_w)
        for b in writes:
            if b.last_w is not None:
                deps.append(b.last_w)
            deps.extend(b.readers)
        for b in writes:
            b.last_w = op
            b.readers = []
        for b in reads:
            b.readers.append(op)
        bd = self.barrier_deps.pop(op.eng, None)
        if bd:
            deps.extend(bd)
        self.ops.append(op)

    def op(self, eng, fn, reads=(), writes=()):
        o = Op(eng, fn, False)
        self._track(o, reads, writes)
        return o

    def dma(self, eng, out, in_, reads=(), writes=(), sembuf=None):
        sw = (eng == "gpsimd")
        if sembuf.sem is None:
            sembuf.sem = self.free_dma[sw].pop()
            self._phase_sembufs.append((sembuf, sw))
        s = sembuf.sem

        def fn(e, out=out, in_=in_):
            return e.dma_start(out=out, in_=in_)

        o = Op(eng, fn, True)
        o.sem = s
        self.dma_cnt[s] += 16
        o.val = self.dma_cnt[s]
        prev = self.dma_last.get(s)
        if prev is not None:
            o.deps.append(prev)
        self.dma_last[s] = o
        if sw and SW_WINDOW:
            if len(self.sw_hist) >= SW_WINDOW:
                o.deps.append(self.sw_hist[-SW_WINDOW])
            self.sw_hist.append(o)
        self._track(o, reads, writes)
        return o

    def wait(self, eng, reads=(), writes=()):
        o = Op(eng, None, False)
        self._track(o, reads, writes)
        return o

    def begin_phase(self):
        self.sw_hist = []
        self.ops = []
        self._phase_sembufs = []
        self.dma_last = {}

    def barrier(self):
        last = {}
        dmas = {}
        for o in self.ops:
            if o.fn is None:
                continue
            if o.is_dma:
                dmas[o.sem] = o
            last[o.eng] = o
        front = list(last.values()) + list(dmas.values())
        for o in front:
            o.signal = True
        for e in ENGS:
            self.barrier_deps.setdefault(e, []).extend(front)

    def end_phase(self, bufs=()):
        for b, sw in self._phase_sembufs:
            self.free_dma[sw].append(b.sem)
            b.sem = None
        for b in bufs:
            b.last_w = None
            b.readers = []

    def emit(self):
        nc = self.nc
        ops = self.ops
        for o in ops:
            for d in o.deps:
                d.signal = True
        for o in ops:
            if not o.is_dma and o.signal and o.val is None and o.fn is not None:
                self.eng_cnt[o.eng] += 1
                o.sem = ("E", o.eng)
                o.val = self.eng_cnt[o.eng]
        per = {e: [] for e in ENGS}
        for o in ops:
            per[o.eng].append(o)

        def semh(s):
            return self.eng_sem[s[1]] if isinstance(s, tuple) else self.dma_sems[s]

        def run(eng_name, e):
            n0 = nc.n_instructions()
            try:
                run_(eng_name, e)
            finally:
                self.per_eng_inst[eng_name] = self.per_eng_inst.get(eng_name, 0) + nc.n_instructions() - n0

        def run_(eng_name, e):
            w = self._waited[eng_name]
            for o in per[eng_name]:
                need = {}
                for d in o.deps:
                    if need.get(d.sem, 0) < d.val:
                        need[d.sem] = d.val
                for s, v in need.items():
                    if w.get(s, 0) < v:
                        e.wait_ge(semh(s), v)
                        self.n_inst += 1
                        w[s] = v
                if o.fn is None:
                    continue
                ins = o.fn(e)
                self.n_inst += 1
                if o.signal:
                    ins.then_inc(semh(o.sem), 16 if o.is_dma else 1)

        with nc.Block() as block:
            @block.tensor
            def _(e):
                run("tensor", e)

            @block.vector
            def _(e):
                run("vector", e)

            @block.scalar
            def _(e):
                run("scalar", e)

            @block.gpsimd
            def _(e):
                run("gpsimd", e)

            @block.sync
            def _(e):
                run("sync", e)


_uid = [0]


def uname(name):
    _uid[0] += 1
    return "%s_u%d" % (name, _uid[0])


def sbt(st, nc, name, shape, dtype):
    return st.enter_context(nc.sbuf_tensor(uname(name), shape, dtype))


class Ring:
    def __init__(self, st, nc, name, n, shape, dtype):
        self.t = [sbt(st, nc, "%s%d" % (name, i), shape, dtype) for i in range(n)]
        self.b = [Buf("%s%d" % (name, i)) for i in range(n)]
        self.i = 0

    def next(self):
        k = self.i % len(self.t)
        self.i += 1
        return self.t[k], self.b[k]


class K:
    pass


MM_COUNT = [0]


def mmc(e, *a, **kw):
    MM_COUNT[0] += 1
    return e.matmul(*a, **kw)


def rows(ap, p=128):
    return ap.rearrange("(kc p) n -> p kc n", p=p)


def psum_next(k):
    i = k.ps_i % k.ps_n
    k.ps_i += 1
    return k.ps[i], k.psb[i]


def ph_transpose_in(k):
    nc, P = k.nc, k.P
    with ExitStack() as st:
        xin = Ring(st, nc, "tx", 2, [128, 4, D], F32)
        stg = Ring(st, nc, "ts", 2, [128, 16, 512], F32)
        P.begin_phase()
        groups = [(k.ctx, 0, 0, 256)] + [(k.x, i * 512, 256 + i * 512, 512) for i in range(4)]
        cnt = 0
        for src, r0, t0, n in groups:
            nj = n // 128
            xt, xb = xin.next()
            P.dma("sync", xt[:, 0:nj, :], src[r0:r0 + n, :].rearrange("(j p) d -> p j d", p=128),
                  writes=[xb], sembuf=xb)
            sg, sgb = stg.next()
            for kc in range(16):
                ps, pb = psum_next(k)

                def f(e, ps=ps, xt=xt, kc=kc, nj=nj):
                    for j in range(nj):
                        ins = mmc(e, ps[:, j * 128:(j + 1) * 128], lhsT=xt[:, j, kc * 128:(kc + 1) * 128],
                                       rhs=k.ident[:, :], start=True, stop=True)
                    return ins
                P.op("tensor", f, reads=[xb, k.constb], writes=[pb])
                if cnt % 2 == 0:
                    P.op("vector", lambda e, sg=sg, ps=ps, kc=kc, n=n: e.tensor_copy(out=sg[:, kc, 0:n], in_=ps[:, 0:n]),
                         reads=[pb], writes=[sgb])
                else:
                    P.op("scalar", lambda e, sg=sg, ps=ps, kc=kc, n=n: e.copy(out=sg[:, kc, 0:n], in_=ps[:, 0:n]),
                         reads=[pb], writes=[sgb])
                cnt += 1
            P.dma("sync", rows(k.xT[:, t0:t0 + n]), sg[:, :, 0:n], reads=[sgb], sembuf=sgb)
        P.barrier()
        P.emit()
        P.end_phase()


def ph_ada(k, l, ntiles=36, subs=(0, 1, 2)):
    nc, P = k.nc, k.P
    with ExitStack() as st:
        slab = Ring(st, nc, "aw", 3, [128, 16, 512], BF16)
        mrow = sbt(st, nc, "mrow", [2, 18432], F32)
        mrowb = [Buf() for _ in range(ntiles)]
        P.begin_phase()
        if l == 0:
            P.op("scalar", lambda e: e.activation(out=k.sT[:], in_=k.cT[:], func=AF.Silu), reads=[k.constb], writes=[k.sTb])
        for nt in range(ntiles):
            sl, sb = slab.next()
            P.dma("gpsimd", sl[:], rows(k.ada_w[l, :, nt * 512:(nt + 1) * 512]), writes=[sb], sembuf=sb)
            ps, pb = psum_next(k)

            def f(e, ps=ps, sl=sl):
                for kc in range(16):
                    ins = mmc(e, ps[0:2, 0:512], lhsT=k.sT[:, kc, :], rhs=sl[:, kc, :], start=(kc == 0), stop=(kc == 15))
                return ins
            P.op("tensor", f, reads=[sb, k.sTb], writes=[pb])
            if nt % 2 == 0:
                P.op("vector", lambda e, ps=ps, nt=nt: e.tensor_copy(out=mrow[0:2, nt * 512:(nt + 1) * 512], in_=ps[0:2, 0:512]),
                     reads=[pb], writes=[mrowb[nt]])
            else:
                P.op("scalar", lambda e, ps=ps, nt=nt: e.copy(out=mrow[0:2, nt * 512:(nt + 1) * 512], in_=ps[0:2, 0:512]),
                     reads=[pb], writes=[mrowb[nt]])
        ps, pb = psum_next(k)

        def f(e, ps=ps):
            for j in range(4 * ntiles):
                ins = mmc(e, ps[:, 2 * j:2 * j + 2], lhsT=mrow[0:2, j * 128:(j + 1) * 128], rhs=k.ident[0:2, 0:2],
                               start=True, stop=True)
            return ins
        P.op("tensor", f, reads=mrowb + [k.constb], writes=[pb])
        mb = k.coefb
        psv = ps[:, 0:8 * ntiles].rearrange("p (j t) -> p j t", t=2)
        for t in range(2):
            P.op("vector", lambda e, t=t: e.tensor_tensor(out=k.mT[:, l, t, 0:4 * ntiles], in0=psv[:, :, t], in1=k.adabT[:, l, 0:4 * ntiles], op=ALU.add),
                 reads=[pb, k.constb], writes=[mb])
        ada_coefs(k, l, subs)
        P.barrier()
        P.emit()
        P.end_phase()


def ada_coefs(k, l, subs=(0, 1, 2)):
        P = k.P
        mb = k.coefb
        for t in range(2):
            for s in subs:
                P.op("vector", lambda e, t=t, s=s: e.scalar_tensor_tensor(
                    out=k.coef[:, l, t, 3 * s + 0, :], in0=k.mT[:, l, t, (3 * s + 1) * 16:(3 * s + 2) * 16], scalar=1.0,
                    in1=k.gT[:, l, s, :], op0=ALU.add, op1=ALU.mult), reads=[mb, k.constb], writes=[mb])
                P.op("vector", lambda e, t=t, s=s: e.tensor_copy(
                    out=k.coef[:, l, t, 3 * s + 1, :], in_=k.mT[:, l, t, (3 * s) * 16:(3 * s + 1) * 16]), reads=[mb], writes=[mb])
                P.op("vector", lambda e, t=t, s=s: e.tensor_scalar(
                    out=k.coef[:, l, t, 3 * s + 2, :], in0=k.mT[:, l, t, (3 * s + 2) * 16:(3 * s + 3) * 16],
                    scalar1=(1.0 if s == 1 else 0.5), scalar2=None, op0=ALU.mult), reads=[mb], writes=[mb])


def ada_bg_steps(k, l, st, bank, bankb, W=512, nslab=3, col0=0, ncols=9 * D, subs=(0, 1, 2)):
    nc, P = k.nc, k.P
    slab = Ring(st, nc, "awb", nslab, [128, 16, W], BF16)
    rowr = Ring(st, nc, "arow", 2, [2, W], F32)
    mb = k.coefb
    NJ = W // 128
    pend = []

    def step(nt):
        sl, sb = slab.next()
        P.dma("gpsimd", sl[:], rows(k.ada_w[l, :, col0 + nt * W:col0 + (nt + 1) * W]), writes=[sb], sembuf=sb)

        def f(e):
            for kc in range(16):
                ins = mmc(e, bank[0:2, 0:W], lhsT=k.sT[:, kc, :], rhs=sl[:, kc, :], start=(kc == 0), stop=(kc == 15))
            return ins
        P.op("tensor", f, reads=[sb, k.sTb], writes=[bankb])
        rw, rwb = rowr.next()
        P.op("vector", lambda e: e.tensor_copy(out=rw[0:2, 0:W], in_=bank[0:2, 0:W]), reads=[bankb], writes=[rwb])
        if pend:
            pend.pop(0)()

        def tail():
            def f2(e):
                for j in range(NJ):
                    ins = mmc(e, bank[:, 2 * j:2 * j + 2], lhsT=rw[0:2, j * 128:(j + 1) * 128], rhs=k.ident[0:2, 0:2], start=True, stop=True)
                return ins
            P.op("tensor", f2, reads=[rwb, k.constb], writes=[bankb])
            psv = bank[:, 0:2 * NJ].rearrange("p (j t) -> p j t", t=2)
            c0 = col0 // 128 + nt * NJ
            for t in range(2):
                P.op("vector", lambda e, t=t: e.tensor_tensor(out=k.mT[:, l, t, c0:c0 + NJ], in0=psv[:, :, t],
                                                              in1=k.adabT[:, l, c0:c0 + NJ], op=ALU.add),
                     reads=[bankb, k.constb], writes=[mb])
        pend.append(tail)

    def finish():
        while pend:
            pend.pop(0)()
        ada_coefs(k, l, subs)
    steps = [(lambda nt=nt: step(nt)) for nt in range(ncols // W)]
    return steps, finish


def norm_mod(k, st_rings, tiles, hT, hTb, l, s, g_only=None, dq="sync", split=False):
    P = k.P
    xin, sqr, tmpr, rsr = st_rings
    G = xin.t[0].shape[1]
    NQ = 16 // G

    def loads(t0, n):
        pend = []
        depth = len(xin.t)

        def issue(q):
            xt, xb = xin.next()
            qn_ = dq if isinstance(dq, str) else dq[q % len(dq)]
            P.dma(qn_, xt[:, :, 0:n], rows(k.xT[q * G * 128:(q + 1) * G * 128, t0:t0 + n]), writes=[xb], sembuf=xb)
            pend.append((xt, xb))
        for q in range(min(depth, NQ)):
            issue(q)
        for q in range(NQ):
            yield q, pend[q]
            if q + depth < NQ:
                issue(q + depth)

    def pass1(t0, n):
        pss, pssb = psum_next(k)
        for q, (xt, xb) in loads(t0, n):
            sq, sqb = sqr.next()
            P.op("scalar", lambda e, sq=sq, xt=xt: e.activation(out=sq[:, :, 0:n], in_=xt[:, :, 0:n], func=AF.Square),
                 reads=[xb], writes=[sqb])

            def f(e, sq=sq, q=q):
                for j in range(G):
                    ins = mmc(e, pss[:, 0:n], lhsT=k.ones_bf[:, :], rhs=sq[:, j, 0:n], start=(q == 0 and j == 0),
                              stop=(q == NQ - 1 and j == G - 1))
                return ins
            P.op("tensor", f, reads=[sqb, k.constb], writes=[pssb])
        rs, rsb = rsr.next()
        P.op("scalar", lambda e: e.activation(out=rs[:, 0:n], in_=pss[:, 0:n], func=AF.Sqrt, bias=k.epsc[:, 0:1], scale=1.0 / D),
             reads=[pssb, k.constb], writes=[rsb])
        P.op("vector", lambda e: e.reciprocal(out=rs[:, 0:n], in_=rs[:, 0:n]), reads=[rsb], writes=[rsb])
        return rs, rsb

    def pass2(ti, t0, n, typ, c0, rs, rsb):
        hTb1 = hTb[ti]
        for q, (xt, xb) in loads(t0, n):
            for j in range(G):
                kc = G * q + j
                tm, tmb = tmpr.next()
                P.op("vector", lambda e, tm=tm, xt=xt, j=j: e.tensor_tensor(
                    out=tm[:, 0:n], in0=xt[:, j, 0:n], in1=rs[:, 0:n], op=ALU.mult), reads=[xb, rsb], writes=[tmb])
                if g_only is None:
                    sc = k.coef[:, l, typ, 3 * s + 0, kc:kc + 1]
                    bi = k.coef[:, l, typ, 3 * s + 1, kc:kc + 1]
                    P.op("scalar", lambda e, tm=tm, kc=kc, sc=sc, bi=bi: e.activation(
                        out=hT[:, kc, c0:c0 + n], in_=tm[:, 0:n], func=AF.Identity, bias=bi, scale=sc),
                        reads=[tmb, k.coefb], writes=[hTb1])
                else:
                    sc = g_only[:, kc:kc + 1]
                    P.op("scalar", lambda e, tm=tm, kc=kc, sc=sc: e.activation(
                        out=hT[:, kc, c0:c0 + n], in_=tm[:, 0:n], func=AF.Identity, scale=sc),
                        reads=[tmb, k.constb], writes=[hTb1])

    if split:
        rss = [pass1(t0, n) for (t0, n, typ, c0) in tiles]
        for ti, (t0, n, typ, c0) in enumerate(tiles):
            pass2(ti, t0, n, typ, c0, *rss[ti])
    else:
        for ti, (t0, n, typ, c0) in enumerate(tiles):
            rs, rsb = pass1(t0, n)
            pass2(ti, t0, n, typ, c0, rs, rsb)


def mk_tiles(block):
    out = []
    c0 = 0
    for t0, n in block:
        out.append((t0, n, 1 if t0 < NCTX else 0, c0))
        c0 += n
    return out


FULL_BLOCKS = [[(0, 256), (256, 512)], [(768, 512), (1280, 256)], [(1536, 512), (2048, 256)]]
LAT_BLOCKS = [[(256, 512), (768, 256)], [(1024, 512), (1536, 256)], [(1792, 512)]]
XT_REGIONS = [(0, 256), (256, 512), (768, 512), (768, 256), (1024, 512), (1280, 256), (1536, 512), (1536, 256),
              (1792, 512), (2048, 256), (1280, 512)]


def ph_ffn(k, l, j, blocks, bg=False):
    nc, P = k.nc, k.P
    s = 0 if j == 0 else 2
    wg_d, wu_d, wd_d = k.ffn_wg[l, j], k.ffn_wu[l, j], k.ffn_wd[l, j]
    BT = 768
    with ExitStack() as st:
        hT = sbt(st, nc, "hT", [128, 16, BT], BF16)
        aT = sbt(st, nc, "aT", [128, 44, BT], BF16)
        wgu = Ring(st, nc, "wgu", 4, [128, 16, 256], BF16)
        wdr = Ring(st, nc, "wd", 2 if bg else 3, [128, 44, 128], BF16)
        xin = Ring(st, nc, "xin", 3, [128, 2, 512], F32)
        sqr = Ring(st, nc, "sq", 2, [128, 2, 512], BF16)
        tmpr = Ring(st, nc, "tmp", 2, [128, 512], F32)
        rsr = Ring(st, nc, "rs", 2, [128, 512], F32)
        sgr = Ring(st, nc, "sg", 2, [128, 512], F32)
        xor_ = Ring(st, nc, "xo", 3, [128, 512], F32)
        P.begin_phase()
        bg_steps, bg_finish = [], None
        if bg:
            k.ps_n = 7
            bg_steps, bg_finish = ada_bg_steps(k, l, st, k.ps[7], k.psb[7], W=256, nslab=2, col0=3 * D, ncols=6 * D, subs=(1, 2))
        hTb = [Buf("hT%d" % i) for i in range(4)]
        aTbs = [Buf() for _ in range(44)]
        tl = [mk_tiles(b) for b in blocks]
        norm_mod(k, (xin, sqr, tmpr, rsr), tl[0], hT, hTb, l, s, dq=("sync", "scalar"))
        for bi, tiles in enumerate(tl):
            for fp in range(22):
                if bg_steps:
                    bg_steps.pop(0)()
                wg, wgb = wgu.next()
                P.dma("gpsimd", wg[:], rows(wg_d[:, fp * 256:(fp + 1) * 256]), writes=[wgb], sembuf=wgb)
                wu, wub = wgu.next()
                P.dma("gpsimd", wu[:], rows(wu_d[:, fp * 256:(fp + 1) * 256]), writes=[wub], sembuf=wub)
                for fs in range(2):
                    fc = fp * 2 + fs
                    for ti, (t0, n, typ, c0) in enumerate(tiles):
                        pg, pgb = psum_next(k)
                        pu, pub = psum_next(k)

                        def f(e, ps=pg, w=wg, fs=fs, c0=c0, n=n):
                            for kc in range(16):
                                ins = mmc(e, ps[:, 0:n], lhsT=w[:, kc, fs * 128:(fs + 1) * 128], rhs=hT[:, kc, c0:c0 + n],
                                               start=(kc == 0), stop=(kc == 15))
                            return ins
                        P.op("tensor", f, reads=[wgb, hTb[ti]], writes=[pgb])

                        def f2(e, ps=pu, w=wu, fs=fs, c0=c0, n=n):
                            for kc in range(16):
                                ins = mmc(e, ps[:, 0:n], lhsT=w[:, kc, fs * 128:(fs + 1) * 128], rhs=hT[:, kc, c0:c0 + n],
                                               start=(kc == 0), stop=(kc == 15))
                            return ins
                        P.op("tensor", f2, reads=[wub, hTb[ti]], writes=[pub])
                        sg, sgb = sgr.next()
                        P.op("scalar", lambda e, sg=sg, pg=pg, n=n: e.activation(out=sg[:, 0:n], in_=pg[:, 0:n], func=AF.Silu),
                             reads=[pgb], writes=[sgb])
                        P.op("vector", lambda e, sg=sg, pu=pu, fc=fc, c0=c0, n=n: e.tensor_tensor(
                            out=aT[:, fc, c0:c0 + n], in0=sg[:, 0:n], in1=pu[:, 0:n], op=ALU.mult),
                            reads=[sgb, pub], writes=[aTbs[fc]])
            for dc in range(16):
                if dc == 3 and bi + 1 < len(tl):
                    norm_mod(k, (xin, sqr, tmpr, rsr), tl[bi + 1], hT, hTb, l, s, dq="scalar", split=True)
                wd, wdb = wdr.next()
                P.dma("gpsimd", wd[:, 0:22, :], rows(wd_d[0:2816, dc * 128:(dc + 1) * 128]), writes=[wdb], sembuf=wdb)
                P.dma("gpsimd", wd[:, 22:44, :], rows(wd_d[2816:5632, dc * 128:(dc + 1) * 128]), writes=[wdb], sembuf=wdb)
                for (t0, n, typ, c0) in tiles:
                    ps, pb = psum_next(k)

                    def f(e, ps=ps, wd=wd, c0=c0, n=n):
                        for kc in range(44):
                            ins = mmc(e, ps[:, 0:n], lhsT=wd[:, kc, :], rhs=aT[:, kc, c0:c0 + n], start=(kc == 0), stop=(kc == 43))
                        return ins
                    P.op("tensor", f, reads=[wdb] + aTbs, writes=[pb])
                    xo, xob = xor_.next()
                    P.dma("sync", xo[:, 0:n], k.xT[dc * 128:(dc + 1) * 128, t0:t0 + n], writes=[xob], sembuf=xob)
                    gcol = k.coef[:, l, typ, 3 * s + 2, dc:dc + 1]
                    P.op("vector", lambda e, xo=xo, ps=ps, gcol=gcol, n=n: e.scalar_tensor_tensor(
                        out=xo[:, 0:n], in0=ps[:, 0:n], scalar=gcol, in1=xo[:, 0:n], op0=ALU.mult, op1=ALU.add),
                        reads=[pb, xob, k.coefb], writes=[xob])
                    P.dma("sync", k.xT[dc * 128:(dc + 1) * 128, t0:t0 + n], xo[:, 0:n], reads=[xob], sembuf=xob)
        while bg_steps:
            bg_steps.pop(0)()
        if bg_finish is not None:
            bg_finish()
        k.ps_n = 8
        P.barrier()
        P.emit()
        P.end_phase()


ALL_TILES = [(0, 256), (256, 512), (768, 512), (1280, 512), (1792, 512)]
LAT_TILES = ALL_TILES[1:]
MLA_SCALE = 192.0 ** -0.5
DIFF_SCALE = 0.125


def evac(P, idx, out, in_, reads, writes):
    if idx % 2 == 0:
        P.op("vector", lambda e: e.tensor_copy(out=out, in_=in_), reads=reads, writes=writes)
    else:
        P.op("scalar", lambda e: e.copy(out=out, in_=in_), reads=reads, writes=writes)


def mm_group(P, ps_ap, pairs, reads, writes):
    def f(e):
        n = len(pairs)
        for i, (a, b) in enumerate(pairs):
            ins = mmc(e, ps_ap, lhsT=a, rhs=b, start=(i == 0), stop=(i == n - 1))
        return ins
    return P.op("tensor", f, reads=reads, writes=writes)


def rope_combine(k, P, rings, psA, pAb, psB, pBb, t0, n, dst_ap, stg, stgb):
    t1r, t2r = rings
    t1, t1b = t1r.next()
    t2, t2b = t2r.next()
    P.op("vector", lambda e: e.tensor_tensor(out=t1[:, 0:n], in0=psA[:, 0:n], in1=k.cosT[:, t0:t0 + n], op=ALU.mult),
         reads=[pAb, k.ropeb], writes=[t1b])
    P.op("vector", lambda e: e.tensor_tensor(out=t2[:, 0:n], in0=psB[:, 0:n], in1=k.sinT[:, t0:t0 + n], op=ALU.mult),
         reads=[pBb, k.ropeb], writes=[t2b])
    P.op("vector", lambda e: e.tensor_tensor(out=stg[:, 0:n], in0=t1[:, 0:n], in1=t2[:, 0:n], op=ALU.add),
         reads=[t1b, t2b], writes=[stgb])
    P.dma("sync", dst_ap, stg[:, 0:n], reads=[stgb], sembuf=stgb)


def ph_proj(k, l):
    nc, P = k.nc, k.P
    last = (l == 1)
    w = k.w_in2[l]
    with ExitStack() as st:
        hT = sbt(st, nc, "hTp", [128, 16, NTOK], BF16)
        slabr = Ring(st, nc, "wsl", 4, [128, 16, 256], BF16)
        dvs = sbt(st, nc, "dvs", [128, 16, 512], BF16)
        dvsb = Buf()
        xin = Ring(st, nc, "xin", 3, [128, 2, 512], F32)
        sqr = Ring(st, nc, "sq", 2, [128, 2, 512], BF16)
        xar = Ring(st, nc, "xa", 2, [128, 512], BF16)
        rperm = sbt(st, nc, "rperm", [128, 128], BF16)
        rpb = Buf()
        tmpr = Ring(st, nc, "tmp", 3, [128, 512], F32)
        rsr = Ring(st, nc, "rs", 5, [128, 512], F32)
        rs2r = Ring(st, nc, "rs2", 1, [128, 512], F32)
        rawr = Ring(st, nc, "raw", 5, [128, 512], F32)
        sq2r = Ring(st, nc, "sq2", 4, [128, 512], BF16)
        t1r = Ring(st, nc, "t1", 2, [128, 512], F32)
        t2r = Ring(st, nc, "t2", 2, [128, 512], F32)
        stgr = Ring(st, nc, "stg", 2, [128, 512], BF16)
        k.cosT = sbt(st, nc, "cosT", [128, NTOK], F32)
        k.sinT = sbt(st, nc, "sinT", [128, NTOK], F32)
        k.ropeb = Buf()
        P.begin_phase()
        P.dma("sync", k.cosT[:], k.cosT_d, writes=[k.ropeb], sembuf=k.ropeb)
        P.dma("sync", k.sinT[:], k.sinT_d, writes=[k.ropeb], sembuf=k.ropeb)
        P.dma("gpsimd", rperm[:], k.rperm_d, writes=[rpb], sembuf=rpb)
        hTb = [Buf() for _ in range(5)]
        tiles = mk_tiles(ALL_TILES)
        norm_mod(k, (xin, sqr, tmpr, rsr), tiles, hT, hTb, l, 1, split=True, dq=("sync", "scalar"))
        qtiles = [(ti, t) for ti, t in enumerate(tiles) if not (last and ti == 0)]
        atiles = list(enumerate(tiles))
        slabs = {}

        def get_slab(si):
            sl, sb_ = slabr.next()
            P.dma("gpsimd", sl[:], rows(w[:, si * 256:(si + 1) * 256]), writes=[sb_], sembuf=sb_)
            return sl, sb_

        def proj_mm(sl, sb_, sub, ti, c0, n):
            ps, pb = psum_next(k)
            mm_group(P, ps[:, 0:n], [(sl[:, kc, sub * 128:(sub + 1) * 128], hT[:, kc, c0:c0 + n]) for kc in range(16)],
                     reads=[sb_, hTb[ti]], writes=[pb])
            return ps, pb

        def normed(slab_ids, nch, gcol, dst, tl):
            sls = [get_slab(si) for si in slab_ids]
            for ti, (t0, n, typ, c0) in tl:
                raws = []
                sqs = []
                for ch in range(nch):
                    sl, sb_ = sls[ch // 2]
                    ps, pb = proj_mm(sl, sb_, ch % 2, ti, c0, n)
                    rw, rwb = rawr.next()
                    P.op("vector", lambda e, rw=rw, ps=ps, n=n: e.tensor_copy(out=rw[:, 0:n], in_=ps[:, 0:n]), reads=[pb], writes=[rwb])
                    sq, sqb = sq2r.next()
                    P.op("scalar", lambda e, sq=sq, rw=rw, n=n: e.activation(out=sq[:, 0:n], in_=rw[:, 0:n], func=AF.Square),
                         reads=[rwb], writes=[sqb])
                    raws.append((rw, rwb))
                    sqs.append((sq, sqb))
                pss, pssb = psum_next(k)
                for ch in range(nch):
                    sq, sqb = sqs[ch]
                    P.op("tensor", lambda e, pss=pss, sq=sq, n=n, ch=ch: mmc(e, pss[:, 0:n], lhsT=k.ones_bf[:, :], rhs=sq[:, 0:n],
                                                                                 start=(ch == 0), stop=(ch == nch - 1)),
                         reads=[sqb, k.constb], writes=[pssb])
                rs, rsb = rs2r.next()
                P.op("scalar", lambda e, rs=rs, pss=pss, n=n: e.activation(out=rs[:, 0:n], in_=pss[:, 0:n], func=AF.Sqrt,
                                                                          bias=k.epsc[:, 0:1], scale=1.0 / (128 * nch)),
                     reads=[pssb, k.constb], writes=[rsb])
                P.op("vector", lambda e, rs=rs, n=n: e.reciprocal(out=rs[:, 0:n], in_=rs[:, 0:n]), reads=[rsb], writes=[rsb])
                for ch in range(nch):
                    rw, rwb = raws[ch]
                    P.op("vector", lambda e, rw=rw, rs=rs, n=n: e.tensor_tensor(out=rw[:, 0:n], in0=rw[:, 0:n], in1=rs[:, 0:n], op=ALU.mult),
                         reads=[rwb, rsb], writes=[rwb])
                    sg, sgb = stgr.next()
                    P.op("scalar", lambda e, sg=sg, rw=rw, ch=ch, n=n: e.activation(out=sg[:, 0:n], in_=rw[:, 0:n], func=AF.Identity,
                                                                                   scale=gcol[:, ch:ch + 1]),
                         reads=[rwb, k.constb], writes=[sgb])
                    P.dma("sync", dst[ch * 128:(ch + 1) * 128, t0:t0 + n], sg[:, 0:n], reads=[sgb], sembuf=sgb)

        pend_rope = []

        def roped(main_slab, main_sub, dst_rows, tl):
            for ti, (t0, n, typ, c0) in tl:
                psA, pAb = proj_mm(main_slab[0], main_slab[1], main_sub, ti, c0, n)
                xa, xab = xar.next()
                P.op("scalar", lambda e, xa=xa, psA=psA, n=n: e.copy(out=xa[:, 0:n], in_=psA[:, 0:n]), reads=[pAb], writes=[xab])
                if pend_rope:
                    pend_rope.pop(0)()

                def tail(psA=psA, pAb=pAb, xa=xa, xab=xab, t0=t0, n=n):
                    psB, pBb = psum_next(k)
                    P.op("tensor", lambda e: mmc(e, psB[:, 0:n], lhsT=rperm[:, :], rhs=xa[:, 0:n], start=True, stop=True),
                         reads=[xab, rpb], writes=[pBb])
                    sg, sgb = stgr.next()
                    rope_combine(k, P, (t1r, t2r), psA, pAb, psB, pBb, t0, n, dst_rows[:, t0:t0 + n], sg, sgb)
                pend_rope.append(tail)

        normed([0], 2, k.kvnT[:, l, :], k.ckvnT, atiles)
        s1 = get_slab(1)
        roped(s1, 0, k.krT, atiles)
        for hp in range(2):
            sm = get_slab(2 + hp)
            for sub in range(2):
                h = hp * 2 + sub
                roped(sm, sub, k.dkT[h * 128:(h + 1) * 128, :], atiles)
        while pend_rope:
            pend_rope.pop(0)()
        normed([6, 7], 4, k.qnormT[:, l, :], k.cqnT, qtiles)
        for hp in range(2):
            sm = get_slab(8 + hp)
            for sub in range(2):
                h = hp * 2 + sub
                roped(sm, sub, k.dqT[h * 128:(h + 1) * 128, :], qtiles)
        while pend_rope:
            pend_rope.pop(0)()
        cnt = 0
        for hp in range(2):
            sm = get_slab(12 + hp)
            for sub in range(2):
                g = hp * 2 + sub
                for ti, (t0, n, typ, c0) in qtiles:
                    ps, pb = proj_mm(sm[0], sm[1], sub, ti, c0, n)
                    sg, sgb = stgr.next()
                    evac(P, cnt, sg[:, 0:n], ps[:, 0:n], [pb], [sgb])
                    cnt += 1
                    P.dma("sync", k.uT[g * 128:(g + 1) * 128, t0:t0 + n], sg[:, 0:n], reads=[sgb], sembuf=sgb)
        P.dma("gpsimd", dvs[:], rows(w[:, 28 * 128:32 * 128]), writes=[dvsb], sembuf=dvsb)
        for tc in range(18):
            ti = 0 if tc < 2 else 1 + (tc - 2) // 4
            ps, pb = psum_next(k)
            mm_group(P, ps[:, :], [(hT[:, kc, tc * 128:(tc + 1) * 128], dvs[:, kc, :]) for kc in range(16)],
                     reads=[dvsb, hTb[ti]], writes=[pb])
            sg, sgb = stgr.next()
            evac(P, tc, sg[:, :], ps[:, :], [pb], [sgb])
            P.dma("sync", k.DVs[tc * 128:(tc + 1) * 128, :], sg[:, :], reads=[sgb], sembuf=sgb)
        P.barrier()
        P.emit()
        P.end_phase()


def ph_proj2(k, l):
    nc, P = k.nc, k.P
    last = (l == 1)
    with ExitStack() as st:
        ckv = sbt(st, nc, "ckv", [128, 2, NTOK], BF16)
        cq = sbt(st, nc, "cq", [128, 4, NTOK], BF16)
        wukv = sbt(st, nc, "wukv", [128, 2, 2048], BF16)
        wuq = sbt(st, nc, "wuq", [128, 4, 2048], BF16)
        t1r = Ring(st, nc, "t1", 2, [128, 512], F32)
        t2r = Ring(st, nc, "t2", 2, [128, 512], F32)
        stgr = Ring(st, nc, "stg", 4, [128, 512], BF16)
        k.cosT = sbt(st, nc, "cosT", [128, NTOK], F32)
        k.sinT = sbt(st, nc, "sinT", [128, NTOK], F32)
        k.ropeb = Buf()
        ckvb, cqb, wkb, wqb = Buf(), Buf(), Buf(), Buf()
        P.begin_phase()
        P.dma("sync", ckv[:], rows(k.ckvnT[:, :]), writes=[ckvb], sembuf=ckvb)
        P.dma("sync", cq[:], rows(k.cqnT[:, :]), writes=[cqb], sembuf=cqb)
        P.dma("sync", k.cosT[:], k.cosT_d, writes=[k.ropeb], sembuf=k.ropeb)
        P.dma("sync", k.sinT[:], k.sinT_d, writes=[k.ropeb], sembuf=k.ropeb)
        P.dma("gpsimd", wukv[:], rows(k.w_ukv2[l]), writes=[wkb], sembuf=wkb)
        P.dma("gpsimd", wuq[:], rows(k.w_uq2[l]), writes=[wqb], sembuf=wqb)
        tiles = mk_tiles(ALL_TILES)
        qtiles = tiles[1:] if last else tiles
        cnt = 0
        for h in range(8):
            for (t0, n, typ, c0) in tiles:
                ps, pb = psum_next(k)
                mm_group(P, ps[:, 0:n], [(wukv[:, kc, h * 128:(h + 1) * 128], ckv[:, kc, t0:t0 + n]) for kc in range(2)],
                         reads=[wkb, ckvb], writes=[pb])
                sg, sgb = stgr.next()
                evac(P, cnt, sg[:, 0:n], ps[:, 0:n], [pb], [sgb])
                cnt += 1
                P.dma("sync", k.knT[h * 128:(h + 1) * 128, t0:t0 + n], sg[:, 0:n], reads=[sgb], sembuf=sgb)
        for tc in range(18):
            for half in range(2):
                ps, pb = psum_next(k)
                mm_group(P, ps[:, :], [(ckv[:, kc, tc * 128:(tc + 1) * 128], wukv[:, kc, 1024 + half * 512:1024 + (half + 1) * 512])
                                      for kc in range(2)], reads=[wkb, ckvb], writes=[pb])
                sg, sgb = stgr.next()
                evac(P, cnt, sg[:, :], ps[:, :], [pb], [sgb])
                cnt += 1
                P.dma("sync", k.Vs[tc * 128:(tc + 1) * 128, half * 512:(half + 1) * 512], sg[:, :], reads=[sgb], sembuf=sgb)
        for h in range(8):
            for (t0, n, typ, c0) in qtiles:
                ps, pb = psum_next(k)
                mm_group(P, ps[:, 0:n], [(wuq[:, kc, h * 128:(h + 1) * 128], cq[:, kc, t0:t0 + n]) for kc in range(4)],
                         reads=[wqb, cqb], writes=[pb])
                sg, sgb = stgr.next()
                evac(P, cnt, sg[:, 0:n], ps[:, 0:n], [pb], [sgb])
                cnt += 1
                P.dma("sync", k.qnT[h * 128:(h + 1) * 128, t0:t0 + n], sg[:, 0:n], reads=[sgb], sembuf=sgb)
        for r in range(4):
            for (t0, n, typ, c0) in qtiles:
                psA, pAb = psum_next(k)
                mm_group(P, psA[:, 0:n], [(wuq[:, kc, 1024 + r * 128:1024 + (r + 1) * 128], cq[:, kc, t0:t0 + n]) for kc in range(4)],
                         reads=[wqb, cqb], writes=[pAb])
                psB, pBb = psum_next(k)
                mm_group(P, psB[:, 0:n], [(wuq[:, kc, 1536 + r * 128:1536 + (r + 1) * 128], cq[:, kc, t0:t0 + n]) for kc in range(4)],
                         reads=[wqb, cqb], writes=[pBb])
                sg, sgb = stgr.next()
                rope_combine(k, P, (t1r, t2r), psA, pAb, psB, pBb, t0, n, k.qrT[r * 128:(r + 1) * 128, t0:t0 + n], sg, sgb)
        P.barrier()
        P.emit()
        P.end_phase()


def ph_mla(k, l, bg_layer=None):
    nc, P = k.nc, k.P
    with ExitStack() as st:
        kn = sbt(st, nc, "kn", [128, 8, NTOK], BF16)
        V = sbt(st, nc, "V", [128, 18, 1024], BF16)
        kr = sbt(st, nc, "kr", [128, NTOK], BF16)
        qnr = Ring(st, nc, "qn", 2, [128, 8, 512], BF16)
        qrr = Ring(st, nc, "qr", 2, [128, 8, 512], BF16)
        Er = Ring(st, nc, "E", 4, [128, 512], BF16)
        rzr = Ring(st, nc, "rz", 2, [128, 512], F32)
        stgr = Ring(st, nc, "stg", 3, [128, 512], BF16)
        knb, Vb, krb = Buf(), Buf(), Buf()
        P.begin_phase()
        bg_steps, bg_finish = [], None
        if bg_layer is not None:
            bg_steps, bg_finish = ada_bg_steps(k, bg_layer, st, k.ps[3], k.psb[3])
        NS = 3 if bg_layer is not None else 4
        for t_, b_ in zip(qrr.t, qrr.b):
            P.op("vector", lambda e, t_=t_: e.memset(t_[:], 0.0), writes=[b_])
        P.dma("sync", kn[:], rows(k.knT[:, :]), writes=[knb], sembuf=knb)
        P.dma("sync", kr[:], k.krT[:, :], writes=[krb], sembuf=krb)
        qt = [((256 + i * 512), 512, list(range(18))) for i in range(4)]
        if l == 0:
            qt = [(0, 256, [0, 1])] + qt
        si = 0
        hh = 0
        def load_q(t0, n):
            qn, qnb = qnr.next()
            qr, qrb = qrr.next()
            P.dma("sync", qn[:, :, 0:n], rows(k.qnT[:, t0:t0 + n]), writes=[qnb], sembuf=qnb)
            for h_ in range(8):
                hp_ = 64 * (h_ % 2)
                r0_ = (h_ // 2) * 128 + hp_
                P.dma("sync", qr[hp_:hp_ + 64, h_, 0:n], k.qrT[r0_:r0_ + 64, t0:t0 + n], writes=[qrb], sembuf=qrb)
            return qn, qnb, qr, qrb
        qloads = [load_q(qt[0][0], qt[0][1])]
        P.dma("sync", V[:], rows(k.Vs[:, :]), writes=[Vb], sembuf=Vb)
        for qi, (t0, n, keys) in enumerate(qt):
            if qi + 1 < len(qt):
                qloads.append(load_q(qt[qi + 1][0], qt[qi + 1][1]))
            qn, qnb, qr, qrb = qloads[qi]

            def do_head(h, t0, n, keys, qn, qnb, qr, qrb, O, Ob, Z, Zb):
                nonlocal si
                hp = 64 * (h % 2)

                def s_mm(j):
                    nonlocal si
                    S, Sb = k.ps[si % NS], k.psb[si % NS]
                    si += 1
                    mm_group(P, S[:, 0:n], [(kn[:, h, j * 128:(j + 1) * 128], qn[:, h, 0:n]),
                                            (kr[:, j * 128:(j + 1) * 128], qr[:, h, 0:n])],
                             reads=[knb, krb, qnb, qrb], writes=[Sb])
                    return S, Sb
                cur = s_mm(keys[0])
                for ji, j in enumerate(keys):
                    S, Sb = cur
                    if ji + 1 < len(keys):
                        cur = s_mm(keys[ji + 1])
                    E, Eb = Er.next()
                    P.op("scalar", lambda e, E=E, S=S: e.activation(out=E[:, 0:n], in_=S[:, 0:n], func=AF.Exp, scale=MLA_SCALE),
                         reads=[Sb], writes=[Eb])

                    def pv(e, E=E, j=j, ji=ji, O=O, Z=Z):
                        mmc(e, O[:, 0:n], lhsT=V[:, j, h * 128:(h + 1) * 128], rhs=E[:, 0:n], start=(ji == 0), stop=(ji == len(keys) - 1))
                        return mmc(e, Z[:, 0:n], lhsT=k.ones_bf[:, :], rhs=E[:, 0:n], start=(ji == 0), stop=(ji == len(keys) - 1))
                    P.op("tensor", pv, reads=[Eb, Vb, k.constb], writes=[Ob, Zb])
                rz, rzb = rzr.next()
                P.op("vector", lambda e, rz=rz, Z=Z: e.reciprocal(out=rz[:, 0:n], in_=Z[:, 0:n]), reads=[Zb], writes=[rzb])
                sg, sgb = stgr.next()
                P.op("vector", lambda e, sg=sg, O=O, rz=rz: e.tensor_tensor(out=sg[:, 0:n], in0=O[:, 0:n], in1=rz[:, 0:n], op=ALU.mult),
                     reads=[Ob, rzb], writes=[sgb])
                P.dma("sync", k.oT[h * 128:(h + 1) * 128, t0:t0 + n], sg[:, 0:n], reads=[sgb], sembuf=sgb)
            for h in range(8):
                if bg_steps:
                    bg_steps.pop(0)()
                do_head(h, t0, n, keys, qn, qnb, qr, qrb, k.ps[4 + hh % 2], k.psb[4 + hh % 2], k.ps[6 + hh % 2], k.psb[6 + hh % 2])
                hh += 1
        while bg_steps:
            bg_steps.pop(0)()
        if bg_finish is not None:
            bg_finish()
        P.barrier()
        P.emit()
        P.end_phase()


def ph_diff(k, l):
    nc, P = k.nc, k.P
    lam_init = 0.8 - 0.6 * math.exp(-0.3 * l)
    cfac = 1.0 - lam_init
    with ExitStack() as st:
        dk = sbt(st, nc, "dk", [128, 4, NTOK], BF16)
        DV = sbt(st, nc, "DV", [128, 18, 512], BF16)
        dqr = Ring(st, nc, "dq", 2, [128, 4, 512], BF16)
        Er = Ring(st, nc, "E", 6, [128, 512], BF16)
        fr = Ring(st, nc, "f", 12, [128, 512], F32)
        sqr = Ring(st, nc, "sq", 2, [128, 512], BF16)
        stgr = Ring(st, nc, "stg", 3, [128, 512], BF16)
        pending = []
        lt = sbt(st, nc, "lt", [128, 2, 64], F32)
        lr = sbt(st, nc, "lr", [128, 4], F32)
        dkb, DVb, lb = Buf(), Buf(), Buf()
        P.begin_phase()
        P.dma("sync", dk[:], rows(k.dkT[:, :]), writes=[dkb], sembuf=dkb)
        lam = k.lamrep
        P.op("vector", lambda e: e.tensor_tensor(out=lt[:, 0, :], in0=lam[:, l, 0, :], in1=lam[:, l, 1, :], op=ALU.mult), reads=[k.constb], writes=[lb])
        P.op("vector", lambda e: e.tensor_tensor(out=lt[:, 1, :], in0=lam[:, l, 2, :], in1=lam[:, l, 3, :], op=ALU.mult), reads=[k.constb, lb], writes=[lb])
        P.op("vector", lambda e: e.reduce_sum(out=lr[:, 0:2], in_=lt[:, :, :], axis=mybir.AxisListType.X), reads=[lb], writes=[lb])
        P.op("scalar", lambda e: e.activation(out=lr[:, 0:2], in_=lr[:, 0:2], func=AF.Exp), reads=[lb], writes=[lb])
        P.op("vector", lambda e: e.tensor_tensor(out=lr[:, 2:3], in0=lr[:, 1:2], in1=lr[:, 0:1], op=ALU.subtract), reads=[lb], writes=[lb])
        P.op("vector", lambda e: e.tensor_scalar(out=lr[:, 2:3], in0=lr[:, 2:3], scalar1=-lam_init, scalar2=None, op0=ALU.add), reads=[lb], writes=[lb])
        P.op("vector", lambda e: e.memset(lr[:, 3:4], EPS / (cfac * cfac)), reads=[lb], writes=[lb])
        neglam = lr[:, 2:3]
        epsc2 = lr[:, 3:4]
        qt = [((256 + i * 512), 512, list(range(18))) for i in range(4)]
        if l == 0:
            qt = [(0, 256, [0, 1])] + qt
        si = 0
        def load_q(t0, n):
            dq, dqb = dqr.next()
            P.dma("sync", dq[:, :, 0:n], rows(k.dqT[:, t0:t0 + n]), writes=[dqb], sembuf=dqb)
            return dq, dqb
        qloads = [load_q(qt[0][0], qt[0][1])]
        P.dma("sync", DV[:], rows(k.DVs[:, :]), writes=[DVb], sembuf=DVb)
        for qi, (t0, n, keys) in enumerate(qt):
            if qi + 1 < len(qt):
                qloads.append(load_q(qt[qi + 1][0], qt[qi + 1][1]))
            dq, dqb = qloads[qi]

            def do_head(h, t0, n, keys, dq, dqb):
                nonlocal si
                acc = [(k.ps[4 + i], k.psb[4 + i]) for i in range(4)]

                def s_mm(j):
                    nonlocal si
                    S1, S1b = k.ps[(2 * si) % 4], k.psb[(2 * si) % 4]
                    S2, S2b = k.ps[(2 * si) % 4 + 1], k.psb[(2 * si) % 4 + 1]
                    si += 1
                    P.op("tensor", lambda e: mmc(e, S1[:, 0:n], lhsT=dk[0:64, h, j * 128:(j + 1) * 128], rhs=dq[0:64, h, 0:n], start=True, stop=True),
                         reads=[dkb, dqb], writes=[S1b])
                    P.op("tensor", lambda e: mmc(e, S2[:, 0:n], lhsT=dk[64:128, h, j * 128:(j + 1) * 128], rhs=dq[64:128, h, 0:n], start=True, stop=True),
                         reads=[dkb, dqb], writes=[S2b])
                    return (S1, S1b, S2, S2b)
                cur = s_mm(keys[0])
                for ji, j in enumerate(keys):
                    S1, S1b, S2, S2b = cur
                    if ji + 1 < len(keys):
                        cur = s_mm(keys[ji + 1])
                    first, lastj = (ji == 0), (ji == len(keys) - 1)
                    for (S, Sb, (O, Ob), (Z, Zb)) in ((S1, S1b, acc[0], acc[1]), (S2, S2b, acc[2], acc[3])):
                        E, Eb = Er.next()
                        P.op("scalar", lambda e, E=E, S=S: e.activation(out=E[:, 0:n], in_=S[:, 0:n], func=AF.Exp, scale=DIFF_SCALE),
                             reads=[Sb], writes=[Eb])

                        def pv(e, E=E, O=O, Z=Z, j=j, first=first, lastj=lastj):
                            mmc(e, O[:, 0:n], lhsT=DV[:, j, h * 128:(h + 1) * 128], rhs=E[:, 0:n], start=first, stop=lastj)
                            return mmc(e, Z[:, 0:n], lhsT=k.ones_bf[:, :], rhs=E[:, 0:n], start=first, stop=lastj)
                        P.op("tensor", pv, reads=[Eb, DVb, k.constb], writes=[Ob, Zb])
                    if pending and (ji == 8 or lastj):
                        pending.pop(0)()
                cps = []
                for (A, Ab) in acc:
                    c, cb = fr.next()
                    P.op("vector", lambda e, c=c, A=A: e.tensor_copy(out=c[:, 0:n], in_=A[:, 0:n]), reads=[Ab], writes=[cb])
                    cps.append((c, cb))
                (o1, o1b), (z1, z1b), (o2, o2b), (z2, z2b) = cps
                P.op("vector", lambda e: e.reciprocal(out=z1[:, 0:n], in_=z1[:, 0:n]), reads=[z1b], writes=[z1b])
                P.op("vector", lambda e: e.reciprocal(out=z2[:, 0:n], in_=z2[:, 0:n]), reads=[z2b], writes=[z2b])
                P.op("vector", lambda e: e.tensor_tensor(out=o1[:, 0:n], in0=o1[:, 0:n], in1=z1[:, 0:n], op=ALU.mult), reads=[o1b, z1b], writes=[o1b])
                P.op("vector", lambda e: e.tensor_tensor(out=o2[:, 0:n], in0=o2[:, 0:n], in1=z2[:, 0:n], op=ALU.mult), reads=[o2b, z2b], writes=[o2b])
                o, ob = fr.next()
                P.op("vector", lambda e: e.scalar_tensor_tensor(out=o[:, 0:n], in0=o2[:, 0:n], scalar=neglam, in1=o1[:, 0:n],
                                                                op0=ALU.mult, op1=ALU.add), reads=[o1b, o2b, lb], writes=[ob])
                sq, sqb = sqr.next()
                P.op("vector", lambda e: e.tensor_tensor(out=sq[:, 0:n], in0=o[:, 0:n], in1=o[:, 0:n], op=ALU.mult), reads=[ob], writes=[sqb])

                def finish():
                    nonlocal si
                    SS, SSb = k.ps[(2 * si) % 4], k.psb[(2 * si) % 4]
                    P.op("tensor", lambda e: mmc(e, SS[:, 0:n], lhsT=k.ones_bf[:, :], rhs=sq[:, 0:n], start=True, stop=True),
                         reads=[sqb, k.constb], writes=[SSb])
                    rs, rsb = fr.next()
                    P.op("scalar", lambda e: e.activation(out=rs[:, 0:n], in_=SS[:, 0:n], func=AF.Ln, bias=epsc2,
                                                          scale=1.0 / (128.0 * cfac * cfac)), reads=[SSb, lb], writes=[rsb])
                    P.op("scalar", lambda e: e.activation(out=rs[:, 0:n], in_=rs[:, 0:n], func=AF.Exp, scale=-0.5), reads=[rsb], writes=[rsb])
                    P.op("vector", lambda e: e.tensor_tensor(out=o[:, 0:n], in0=o[:, 0:n], in1=rs[:, 0:n], op=ALU.mult),
                         reads=[ob, rsb], writes=[ob])
                    sg, sgb = stgr.next()
                    P.op("scalar", lambda e: e.activation(out=sg[:, 0:n], in_=o[:, 0:n], func=AF.Identity, scale=k.sublnT[:, l:l + 1]),
                         reads=[ob, k.constb], writes=[sgb])
                    P.dma("sync", k.oT[1024 + h * 128:1024 + (h + 1) * 128, t0:t0 + n], sg[:, 0:n], reads=[sgb], sembuf=sgb)
                pending.append(finish)
            for h in range(4):
                do_head(h, t0, n, keys, dq, dqb)
        while pending:
            pending.pop(0)()
        P.barrier()
        P.emit()
        P.end_phase()


def ph_fourier(k, l):
    nc, P = k.nc, k.P
    with ExitStack() as st:
        u = sbt(st, nc, "u", [128, 4, NTOK], BF16)
        ccsc = sbt(st, nc, "ccsc", [128, 256], BF16)
        AB = sbt(st, nc, "AB", [128, 16, 4, 256], BF16)
        ABc = sbt(st, nc, "ABc", [128, 2, 4, 256], BF16)
        csr = Ring(st, nc, "cs", 2, [128, 16, 512], BF16)
        ssr = Ring(st, nc, "ss", 2, [128, 16, 512], BF16)
        c256 = sbt(st, nc, "c256", [128, 2, 256], BF16)
        s256 = sbt(st, nc, "s256", [128, 2, 256], BF16)
        stgr = Ring(st, nc, "stg", 3, [128, 512], BF16)
        ub, cb_, c2b, s2b = Buf(), Buf(), Buf(), Buf()
        ABb = [Buf() for _ in range(16)]
        ABcb = [Buf() for _ in range(2)]
        P.begin_phase()
        P.dma("sync", u[:], rows(k.uT[:, :]), writes=[ub], sembuf=ub)
        P.dma("gpsimd", ccsc[:], k.ccsc_d, writes=[cb_], sembuf=cb_)
        cnt = 0

        def step1(tok0, ABt, ABtb, tc):
            nonlocal cnt
            for gp in range(2):
                ps, pb = psum_next(k)

                def f(e, ps=ps, gp=gp):
                    for gs in range(2):
                        ins = mmc(e, ps[:, gs * 256:(gs + 1) * 256], lhsT=u[:, gp * 2 + gs, tok0:tok0 + 128], rhs=ccsc[:, :], start=True, stop=True)
                    return ins
                P.op("tensor", f, reads=[ub, cb_], writes=[pb])
                evac(P, cnt, ABt[:, tc, gp * 2:gp * 2 + 2, :].rearrange("p a b -> p (a b)"), ps[:, :], [pb], [ABtb[tc]])
                cnt += 1
        for tc in range(16):
            step1(256 + tc * 128, AB, ABb, tc)
        for stile in range(4):
            cs, csb = csr.next()
            ss, ssb = ssr.next()
            P.dma("gpsimd", cs[:], rows(k.cs2048[:, stile * 512:(stile + 1) * 512]), writes=[csb], sembuf=csb)
            P.dma("gpsimd", ss[:], rows(k.ss2048[:, stile * 512:(stile + 1) * 512]), writes=[ssb], sembuf=ssb)
            for g in range(4):
                ps, pb = psum_next(k)
                pairs = []
                for sc in range(16):
                    pairs.append((AB[:, sc, g, 0:128], cs[:, sc, :]))
                    pairs.append((AB[:, sc, g, 128:256], ss[:, sc, :]))
                mm_group(P, ps[:, :], pairs, reads=ABb + [csb, ssb], writes=[pb])
                sg, sgb = stgr.next()
                evac(P, cnt, sg[:, :], ps[:, :], [pb], [sgb])
                cnt += 1
                P.dma("sync", k.oT[1536 + g * 128:1536 + (g + 1) * 128, 256 + stile * 512:256 + (stile + 1) * 512], sg[:, :],
                      reads=[sgb], sembuf=sgb)
        if l == 0:
            P.dma("gpsimd", c256[:], rows(k.c256_d), writes=[c2b], sembuf=c2b)
            P.dma("gpsimd", s256[:], rows(k.s256_d), writes=[s2b], sembuf=s2b)
            for tc in range(2):
                step1(tc * 128, ABc, ABcb, tc)
            for g in range(4):
                ps, pb = psum_next(k)
                pairs = []
                for sc in range(2):
                    pairs.append((ABc[:, sc, g, 0:128], c256[:, sc, :]))
                    pairs.append((ABc[:, sc, g, 128:256], s256[:, sc, :]))
                mm_group(P, ps[:, 0:256], pairs, reads=ABcb + [c2b, s2b], writes=[pb])
                sg, sgb = stgr.next()
                evac(P, cnt, sg[:, 0:256], ps[:, 0:256], [pb], [sgb])
                cnt += 1
                P.dma("sync", k.oT[1536 + g * 128:1536 + (g + 1) * 128, 0:256], sg[:, 0:256], reads=[sgb], sembuf=sgb)
        P.barrier()
        P.emit()
        P.end_phase()


def ph_wout(k, l):
    nc, P = k.nc, k.P
    tiles = mk_tiles(ALL_TILES)
    if l == 1:
        tiles = tiles[1:]
    with ExitStack() as st:
        o = sbt(st, nc, "o", [128, 16, NTOK], BF16)
        wor = Ring(st, nc, "wo", 3, [128, 16, 256], BF16)
        xor_ = Ring(st, nc, "xo", 3, [128, 512], F32)
        ob = [Buf() for _ in range(16)]
        P.begin_phase()
        for kc in range(16):
            P.dma("sync", o[:, kc, :], k.oT[kc * 128:(kc + 1) * 128, :], writes=[ob[kc]], sembuf=ob[kc])
        for dp in range(8):
            wo, wob = wor.next()
            P.dma("gpsimd", wo[:], rows(k.w_out[l, :, dp * 256:(dp + 1) * 256]), writes=[wob], sembuf=wob)
            for ds in range(2):
                dc = dp * 2 + ds
                for (t0, n, typ, c0) in tiles:
                    ps, pb = psum_next(k)
                    mm_group(P, ps[:, 0:n], [(wo[:, kc, ds * 128:(ds + 1) * 128], o[:, kc, t0:t0 + n]) for kc in range(16)],
                             reads=[wob] + ob, writes=[pb])
                    xo, xob = xor_.next()
                    P.dma("scalar", xo[:, 0:n], k.xT[dc * 128:(dc + 1) * 128, t0:t0 + n], writes=[xob], sembuf=xob)
                    gcol = k.coef[:, l, typ, 5, dc:dc + 1]
                    P.op("vector", lambda e, xo=xo, ps=ps, gcol=gcol, n=n: e.scalar_tensor_tensor(
                        out=xo[:, 0:n], in0=ps[:, 0:n], scalar=gcol, in1=xo[:, 0:n], op0=ALU.mult, op1=ALU.add),
                        reads=[pb, xob, k.coefb], writes=[xob])
                    P.dma("sync", k.xT[dc * 128:(dc + 1) * 128, t0:t0 + n], xo[:, 0:n], reads=[xob], sembuf=xob)
        P.barrier()
        P.emit()
        P.end_phase()


def ph_final(k):
    nc, P = k.nc, k.P
    with ExitStack() as st:
        yr = Ring(st, nc, "y", 2, [128, 16, 512], F32)
        osr = Ring(st, nc, "os", 2, [128, D], F32)
        xin = Ring(st, nc, "xin", 3, [128, 2, 512], F32)
        sqr = Ring(st, nc, "sq", 2, [128, 2, 512], BF16)
        tmpr = Ring(st, nc, "tmp", 2, [128, 512], F32)
        rsr = Ring(st, nc, "rs", 2, [128, 512], F32)
        P.begin_phase()
        cnt = 0
        for (t0, n) in LAT_TILES:
            y, yb = yr.next()
            norm_mod(k, (xin, sqr, tmpr, rsr), [(t0, n, 0, 0)], y, [yb], 0, 0, g_only=k.fnT)
            for tc in range(4):
                os_, osb = osr.next()
                for q in range(4):
                    ps, pb = psum_next(k)

                    def f(e, ps=ps, y=y, tc=tc, q=q):
                        for j in range(4):
                            kc = q * 4 + j
                            ins = mmc(e, ps[:, j * 128:(j + 1) * 128], lhsT=y[:, kc, tc * 128:(tc + 1) * 128], rhs=k.ident[:, :],
                                           start=True, stop=True)
                        return ins
                    P.op("tensor", f, reads=[yb, k.constb], writes=[pb])
                    evac(P, cnt, os_[:, q * 512:(q + 1) * 512], ps[:, :], [pb], [osb])
                    cnt += 1
                r0 = t0 - NCTX + tc * 128
                P.dma("sync", k.out[r0:r0 + 128, :], os_[:, :], reads=[osb], sembuf=osb)
        P.barrier()
        P.emit()
        P.end_phase()


SCRATCH = [("ckvnT", [256, NTOK]), ("cqnT", [512, NTOK]), ("knT", [1024, NTOK]), ("Vs", [NTOK, 1024]), ("krT", [128, NTOK]),
           ("dkT", [512, NTOK]), ("DVs", [NTOK, 512]), ("qnT", [1024, NTOK]), ("qrT", [512, NTOK]), ("dqT", [512, NTOK]),
           ("uT", [512, NTOK]), ("oT", [2048, NTOK])]


NB2 = 3
SW_WINDOW = 5
DBG_MODE = ""


def build(stop_after=None, debug=False, skip=()):
    nc = bass.Bass("TRN2", target_bir_lowering=False)
    k = K()
    k.nc = nc
    MM_COUNT[0] = 0

    def din(name, shape, dt=F32):
        return nc.dram_tensor(name, list(shape), dt, kind="ExternalInput").ap()

    def dscratch(name, shape, dt):
        if debug:
            return nc.dram_tensor(name, list(shape), dt, kind="ExternalOutput").ap()
        return nc.dram_tensor(name, list(shape), dt).ap()

    k.x = din("x", [NLAT, D])
    k.ctx = din("ctx", [NCTX, D])
    cT_d = din("cT", [128, 32])
    k.ada_w = din("ada_w", [2, D, 9 * D])
    adabT_d = din("adabT", [128, 2 * 144])
    gT_d = din("gT", [128, 2 * 3 * 16])
    k.ffn_wg = din("ffn_wg", [2, 2, D, DFF])
    k.ffn_wu = din("ffn_wu", [2, 2, D, DFF])
    k.ffn_wd = din("ffn_wd", [2, 2, DFF, D])
    ident_d = din("ident", [128, 128])
    k.w_in2 = din("w_in2", [2, D, 4096])
    k.w_ukv2 = din("w_ukv2", [2, 256, 2048])
    k.w_uq2 = din("w_uq2", [2, 512, 2048])
    k.w_out = din("w_out", [2, D, D])
    k.cosT_d = din("cosT", [128, NTOK])
    k.sinT_d = din("sinT", [128, NTOK])
    kvnT_d = din("kvnT", [128, 4])
    qnormT_d = din("qnormT", [128, 8])
    sublnT_d = din("sublnT", [128, 2])
    lamrep_d = din("lamrep", [128, 512])
    fnT_d = din("fnT", [128, 16])
    k.rperm_d = din("rperm", [128, 128])
    k.ccsc_d = din("ccsc", [128, 256])
    k.cs2048 = din("cs2048", [2048, 2048])
    k.ss2048 = din("ss2048", [2048, 2048])
    k.c256_d = din("c256", [256, 256])
    k.s256_d = din("s256", [256, 256])
    k.out = nc.dram_tensor("out", [NLAT, D], F32, kind="ExternalOutput").ap()
    k.xT = dscratch("xT", [D, NTOK], F32)
    for name, shape in SCRATCH:
        setattr(k, name, dscratch(name, shape, BF16))

    with ExitStack() as st:
        P = Prog(nc)
        P.open(st)
        k.P = P
        k.ps = [st.enter_context(nc.psum_tensor("ps%d" % i, [128, 512], F32)) for i in range(8)]
        k.psb = [Buf("ps%d" % i) for i in range(8)]
        k.ps_i = 0
        k.ps_n = 8
        sb = lambda name, shape, dt: st.enter_context(nc.sbuf_tensor("s_" + name, shape, dt))
        k.ident = sb("ident", [128, 128], F32)
        k.ones_bf = sb("ones_bf", [128, 128], BF16)
        k.epsc = sb("epsc", [128, 1], F32)
        k.cT = sb("cT", [128, 16, 2], F32)
        k.sT = sb("sT", [128, 16, 2], BF16)
        k.adabT = sb("adabT", [128, 2, 144], F32)
        k.gT = sb("gT", [128, 2, 3, 16], F32)
        k.mT = sb("mT", [128, 2, 2, 144], F32)
        k.coef = sb("coef", [128, 2, 2, 9, 16], F32)
        k.kvnT = sb("kvnT", [128, 2, 2], F32)
        k.qnormT = sb("qnormT", [128, 2, 4], F32)
        k.sublnT = sb("sublnT", [128, 2], F32)
        k.lamrep = sb("lamrep", [128, 2, 4, 64], F32)
        k.fnT = sb("fnT", [128, 16], F32)
        k.constb = Buf("const")
        k.sTb = Buf("sT")
        k.coefb = Buf("coef")

        P.begin_phase()
        cb = k.constb
        P.dma("sync", k.ident[:], ident_d, writes=[cb], sembuf=cb)
        P.dma("sync", k.cT[:].rearrange("p a b -> p (a b)"), cT_d, writes=[cb], sembuf=cb)
        P.dma("sync", k.adabT[:].rearrange("p a b -> p (a b)"), adabT_d, writes=[cb], sembuf=cb)
        P.dma("sync", k.gT[:].rearrange("p a b c -> p (a b c)"), gT_d, writes=[cb], sembuf=cb)
        P.dma("sync", k.kvnT[:].rearrange("p a b -> p (a b)"), kvnT_d, writes=[cb], sembuf=cb)
        P.dma("sync", k.qnormT[:].rearrange("p a b -> p (a b)"), qnormT_d, writes=[cb], sembuf=cb)
        P.dma("sync", k.sublnT[:], sublnT_d, writes=[cb], sembuf=cb)
        P.dma("sync", k.lamrep[:].rearrange("p a b c -> p (a b c)"), lamrep_d, writes=[cb], sembuf=cb)
        P.dma("sync", k.fnT[:], fnT_d, writes=[cb], sembuf=cb)
        P.op("vector", lambda e: e.memset(k.ones_bf[:], 1.0), writes=[cb])
        P.op("vector", lambda e: e.memset(k.epsc[:], EPS), writes=[cb])
        P.barrier()
        P.emit()
        P.end_phase()

        stages = [("tin", lambda: ph_transpose_in(k))]
        for l in range(2):
            if l == 0:
                stages.append(("ada%d" % l, lambda l=l: ph_ada(k, l, ntiles=12, subs=(0,))))
            stages.append(("ffn%d_0" % l, lambda l=l: ph_ffn(k, l, 0, FULL_BLOCKS, bg=(l == 0))))
            stages.append(("proj%d" % l, lambda l=l: ph_proj(k, l)))
            stages.append(("proj2_%d" % l, lambda l=l: ph_proj2(k, l)))
            stages.append(("mla%d" % l, lambda l=l: ph_mla(k, l, bg_layer=(1 if l == 0 else None))))
            stages.append(("diff%d" % l, lambda l=l: ph_diff(k, l)))
            stages.append(("four%d" % l, lambda l=l: ph_fourier(k, l)))
            stages.append(("wout%d" % l, lambda l=l: ph_wout(k, l)))
            stages.append(("ffn%d_1" % l, lambda l=l: ph_ffn(k, l, 1, (FULL_BLOCKS if l == 0 else LAT_BLOCKS)[:NB2])))
        stages.append(("final", lambda: ph_final(k)))
        k.stage_mm = []
        for name, fn in stages:
            if name not in skip:
                fn()
            k.stage_mm.append((name, MM_COUNT[0]))
            if stop_after == name:
                break
        P.begin_phase()
        P.wait("sync")
        P.emit()
        P.end_phase()
        k.n_inst = P.n_inst
    return nc, k


_CONST_CACHE = {}


def host_consts():
    if _CONST_CACHE:
        return _CONST_CACHE
    f = np.float32
    s = np.arange(NLAT)
    pos = [s // 64, s % 64]
    cosT = np.ones((64, NTOK), np.float64)
    sinT = np.zeros((64, NTOK), np.float64)
    for i in range(64):
        jj = i % 16
        inv = np.float32(10000.0) ** np.float32(-2.0 * jj / 32.0)
        ang = pos[i // 32].astype(np.float32) * np.float32(inv)
        cosT[i, NCTX:] = np.cos(ang.astype(np.float64))
        sn = np.sin(ang.astype(np.float64))
        sinT[i, NCTX:] = -sn if (i % 32) < 16 else sn
    _CONST_CACHE["cosT"] = np.ascontiguousarray(np.concatenate([cosT, cosT], 0).astype(f))
    _CONST_CACHE["sinT"] = np.ascontiguousarray(np.concatenate([sinT, sinT], 0).astype(f))
    c = np.arange(128)
    ang = 2 * np.pi * ((c[:, None] * c[None, :]) % 128) / 128.0
    _CONST_CACHE["ccsc"] = np.ascontiguousarray(np.concatenate([np.cos(ang), -np.sin(ang)], 1).astype(f))
    s2 = np.arange(2048, dtype=np.int64)
    ang = 2 * np.pi * ((s2[:, None] * s2[None, :]) % 2048) / 2048.0
    _CONST_CACHE["cs2048"] = np.ascontiguousarray((np.cos(ang) / 512.0).astype(f))
    _CONST_CACHE["ss2048"] = np.ascontiguousarray((np.sin(ang) / 512.0).astype(f))
    s3 = np.arange(256, dtype=np.int64)
    ang = 2 * np.pi * ((s3[:, None] * s3[None, :]) % 256) / 256.0
    nrm = math.sqrt(256.0 * 128.0)
    _CONST_CACHE["c256"] = np.ascontiguousarray((np.cos(ang) / nrm).astype(f))
    _CONST_CACHE["s256"] = np.ascontiguousarray((np.sin(ang) / nrm).astype(f))
    _CONST_CACHE["ident"] = np.eye(128, dtype=f)
    rp = np.zeros((128, 128), f)
    rp[np.arange(128) ^ 16, np.arange(128)] = 1.0
    _CONST_CACHE["rperm"] = rp
    return _CONST_CACHE


def swap64(a):
    m = a.shape[1] // 64
    idx = (np.arange(m)[:, None] * 64 + (np.arange(64) ^ 16)[None, :]).reshape(-1)
    return a[:, idx]


def host_shared(inp):
    f = np.float32
    w2 = []
    for l in range(2):
        wi = np.asarray(inp["w_in"][l], f)
        ckv, kr, dk, dv = wi[:, 0:256], wi[:, 256:320], wi[:, 320:832], wi[:, 832:1344]
        cq, dq, u = wi[:, 1344:1856], wi[:, 1856:2368], wi[:, 2368:2880]
        w2.append(np.concatenate([ckv, kr, kr, swap64(kr), swap64(kr), dk, swap64(dk), cq, dq, swap64(dq), u, dv], axis=1))
    ukv = []
    uq = []
    for l in range(2):
        r = np.asarray(inp["mla_w_ukv"][l], f).reshape(256, 8, 256)
        ukv.append(np.concatenate([r[:, :, :128].reshape(256, 1024), r[:, :, 128:].reshape(256, 1024)], 1))
        r = np.asarray(inp["mla_w_uq"][l], f).reshape(512, 8, 192)
        rope = r[:, :, 128:].reshape(512, 512)
        uq.append(np.concatenate([r[:, :, :128].reshape(512, 1024), rope, swap64(rope)], 1))
    g = np.asarray(inp["norm_g"], f)
    sh = {
        "ada_w": np.asarray(inp["ada_w"], f),
        "adabT": np.ascontiguousarray(np.stack([np.asarray(inp["ada_b"][l], f).reshape(144, 128).T for l in range(2)], axis=1).reshape(128, 288)),
        "gT": np.ascontiguousarray(g.reshape(2, 3, 16, 128).transpose(3, 0, 1, 2).reshape(128, 96)),
        "ffn_wg": np.asarray(inp["ffn_wg"], f),
        "ffn_wu": np.asarray(inp["ffn_wu"], f),
        "ffn_wd": np.asarray(inp["ffn_wd"], f),
        "w_in2": np.ascontiguousarray(np.stack(w2, 0)),
        "w_ukv2": np.ascontiguousarray(np.stack(ukv, 0)),
        "w_uq2": np.ascontiguousarray(np.stack(uq, 0)),
        "w_out": np.asarray(inp["w_out"], f),
        "kvnT": np.ascontiguousarray(np.asarray(inp["mla_kv_norm"], f).reshape(2, 2, 128).transpose(2, 0, 1).reshape(128, 4)),
        "qnormT": np.ascontiguousarray(np.asarray(inp["mla_q_norm"], f).reshape(2, 4, 128).transpose(2, 0, 1).reshape(128, 8)),
        "sublnT": np.ascontiguousarray(np.asarray(inp["diff_subln"], f).T),
        "lamrep": np.ascontiguousarray(np.broadcast_to(np.asarray(inp["diff_lambda"], f).reshape(1, 512), (128, 512))),
        "fnT": np.ascontiguousarray(np.asarray(inp["final_norm"], f).reshape(16, 128).T),
    }
    sh.update(host_consts())
    return sh


def host_inputs(inp, b, shared=None):
    f = np.float32
    if shared is None:
        shared = host_shared(inp)
    c = np.asarray(inp["c"][b], f)
    cc = np.asarray(inp["c_ctx"], f)
    cT = np.stack([c.reshape(16, 128).T, cc.reshape(16, 128).T], axis=-1).reshape(128, 32)
    d = dict(shared)
    d["x"] = np.ascontiguousarray(inp["x"][b], dtype=f)
    d["ctx"] = np.ascontiguousarray(inp["ctx"][b], dtype=f)
    d["cT"] = np.ascontiguousarray(cT)
    return d


def kernel(**inputs):
    nc, k = build()
    shared = host_shared(inputs)
    in_maps = [host_inputs(inputs, b, shared) for b in range(8)]
    res = run_bass_kernel_spmd(nc, in_maps, core_ids=list(range(8)))
    return np.stack([np.asarray(r["out"], dtype=np.float32) for r in res.results], axis=0)
```

```python
import math
from contextlib import ExitStack

import numpy as np
import concourse.bass as bass
import concourse.mybir as mybir
from concourse.bass_utils import run_bass_kernel_spmd

F32 = mybir.dt.float32
BF16 = mybir.dt.bfloat16
AF = mybir.ActivationFunctionType
ALU = mybir.AluOpType

D = 2048
NTOK = 2304
NCTX = 256
NLAT = 2048
DFF = 5632
EPS = 1e-6
ENGS = ("tensor", "vector", "scalar", "gpsimd", "sync")


class Buf:
    __slots__ = ("name", "last_w", "readers", "sem")

    def __init__(self, name=""):
        self.name = name
        self.last_w = None
        self.readers = []
        self.sem = None


class Op:
    __slots__ = ("eng", "fn", "deps", "is_dma", "sem", "val", "signal")

    def __init__(self, eng, fn, is_dma):
        self.eng = eng
        self.fn = fn
        self.deps = []
        self.is_dma = is_dma
        self.sem = None
        self.val = None
        self.signal = is_dma


class Prog:
    def __init__(self, nc, n_dma_sems=64):
        self.nc = nc
        self.eng_sem = {}
        self.eng_cnt = {e: 0 for e in ENGS}
        self.dma_sems = []
        self.dma_cnt = []
        self.n_dma_sems = n_dma_sems
        self.ops = []
        self.barrier_deps = {}
        self.free_dma = {}
        self.dma_last = {}
        self.n_inst = 0
        self._waited = {e: {} for e in ENGS}
        self._phase_sembufs = []
        self.per_eng_inst = {}

    def open(self, stack):
        for e in ENGS:
            self.eng_sem[e] = stack.enter_context(self.nc.semaphore("es_" + e))
        for i in range(self.n_dma_sems):
            self.dma_sems.append(stack.enter_context(self.nc.semaphore("ds%d" % i)))
            self.dma_cnt.append(0)
        half = self.n_dma_sems // 2
        self.free_dma = {True: list(range(half)), False: list(range(half, self.n_dma_sems))}

    def _track(self, op, reads, writes):
        deps = op.deps
        for b in reads:
            if b.last_w is not None:
                deps.append(b.last_w)
        for b in writes:
            if b.last_w is not None:
                deps.append(b.last_w)
            deps.extend(b.readers)
        for b in writes:
            b.last_w = op
            b.readers = []
        for b in reads:
            b.readers.append(op)
        bd = self.barrier_deps.pop(op.eng, None)
        if bd:
            deps.extend(bd)
        self.ops.append(op)

    def op(self, eng, fn, reads=(), writes=()):
        o = Op(eng, fn, False)
        self._track(o, reads, writes)
        return o

    def dma(self, eng, out, in_, reads=(), writes=(), sembuf=None):
        sw = (eng == "gpsimd")
        if sembuf.sem is None:
            sembuf.sem = self.free_dma[sw].pop()
            self._phase_sembufs.append((sembuf, sw))
        s = sembuf.sem

        def fn(e, out=out, in_=in_):
            return e.dma_start(out=out, in_=in_)

        o = Op(eng, fn, True)
        o.sem = s
        self.dma_cnt[s] += 16
        o.val = self.dma_cnt[s]
        prev = self.dma_last.get(s)
        if prev is not None:
            o.deps.append(prev)
        self.dma_last[s] = o
        if sw and SW_WINDOW:
            if len(self.sw_hist) >= SW_WINDOW:
                o.deps.append(self.sw_hist[-SW_WINDOW])
            self.sw_hist.append(o)
        self._track(o, reads, writes)
        return o

    def wait(self, eng, reads=(), writes=()):
        o = Op(eng, None, False)
        self._track(o, reads, writes)
        return o

    def begin_phase(self):
        self.sw_hist = []
        self.ops = []
        self._phase_sembufs = []
        self.dma_last = {}

    def barrier(self):
        last = {}
        dmas = {}
        for o in self.ops:
            if o.fn is None:
                continue
            if o.is_dma:
                dmas[o.sem] = o
            last[o.eng] = o
        front = list(last.values()) + list(dmas.values())
        for o in front:
            o.signal = True
        for e in ENGS:
            self.barrier_deps.setdefault(e, []).extend(front)

    def end_phase(self, bufs=()):
        for b, sw in self._phase_sembufs:
            self.free_dma[sw].append(b.sem)
            b.sem = None
        for b in bufs:
            b.last_w = None
            b.readers = []

    def emit(self):
        nc = self.nc
        ops = self.ops
        for o in ops:
            for d in o.deps:
                d.signal = True
        for o in ops:
            if not o.is_dma and o.signal and o.val is None and o.fn is not None:
                self.eng_cnt[o.eng] += 1
                o.sem = ("E", o.eng)
                o.val = self.eng_cnt[o.eng]
        per = {e: [] for e in ENGS}
        for o in ops:
            per[o.eng].append(o)

        def semh(s):
            return self.eng_sem[s[1]] if isinstance(s, tuple) else self.dma_sems[s]

        def run(eng_name, e):
            n0 = nc.n_instructions()
            try:
                run_(eng_name, e)
            finally:
                self.per_eng_inst[eng_name] = self.per_eng_inst.get(eng_name, 0) + nc.n_instructions() - n0

        def run_(eng_name, e):
            w = self._waited[eng_name]
            for o in per[eng_name]:
                need = {}
                for d in o.deps:
                    if need.get(d.sem, 0) < d.val:
                        need[d.sem] = d.val
                for s, v in need.items():
                    if w.get(s, 0) < v:
                        e.wait_ge(semh(s), v)
                        self.n_inst += 1
                        w[s] = v
                if o.fn is None:
                    continue
                ins = o.fn(e)
                self.n_inst += 1
                if o.signal:
                    ins.then_inc(semh(o.sem), 16 if o.is_dma else 1)

        with nc.Block() as block:
            @block.tensor
            def _(e):
                run("tensor", e)

            @block.vector
            def _(e):
                run("vector", e)

            @block.scalar
            def _(e):
                run("scalar", e)

            @block.gpsimd
            def _(e):
                run("gpsimd", e)

            @block.sync
            def _(e):
                run("sync", e)


_uid = [0]


def uname(name):
    _uid[0] += 1
    return "%s_u%d" % (name, _uid[0])


def sbt(st, nc, name, shape, dtype):
    return st.enter_context(nc.sbuf_tensor(uname(name), shape, dtype))


class Ring:
    def __init__(self, st, nc, name, n, shape, dtype):
        self.t = [sbt(st, nc, "%s%d" % (name, i), shape, dtype) for i in range(n)]
        self.b = [Buf("%s%d" % (name, i)) for i in range(n)]
        self.i = 0

    def next(self):
        k = self.i % len(self.t)
        self.i += 1
        return self.t[k], self.b[k]


class K:
    pass


MM_COUNT = [0]


def mmc(e, *a, **kw):
    MM_COUNT[0] += 1
    return e.matmul(*a, **kw)


def rows(ap, p=128):
    return ap.rearrange("(kc p) n -> p kc n", p=p)


def psum_next(k):
    i = k.ps_i % k.ps_n
    k.ps_i += 1
    return k.ps[i], k.psb[i]


def ph_transpose_in(k):
    nc, P = k.nc, k.P
    with ExitStack() as st:
        xin = Ring(st, nc, "tx", 2, [128, 4, D], F32)
        stg = Ring(st, nc, "ts", 2, [128, 16, 512], F32)
        P.begin_phase()
        groups = [(k.ctx, 0, 0, 256)] + [(k.x, i * 512, 256 + i * 512, 512) for i in range(4)]
        cnt = 0
        for src, r0, t0, n in groups:
            nj = n // 128
            xt, xb = xin.next()
            P.dma("sync", xt[:, 0:nj, :], src[r0:r0 + n, :].rearrange("(j p) d -> p j d", p=128),
                  writes=[xb], sembuf=xb)
            sg, sgb = stg.next()
            for kc in range(16):
                ps, pb = psum_next(k)

                def f(e, ps=ps, xt=xt, kc=kc, nj=nj):
                    for j in range(nj):
                        ins = mmc(e, ps[:, j * 128:(j + 1) * 128], lhsT=xt[:, j, kc * 128:(kc + 1) * 128],
                                       rhs=k.ident[:, :], start=True, stop=True)
                    return ins
                P.op("tensor", f, reads=[xb, k.constb], writes=[pb])
                if cnt % 2 == 0:
                    P.op("vector", lambda e, sg=sg, ps=ps, kc=kc, n=n: e.tensor_copy(out=sg[:, kc, 0:n], in_=ps[:, 0:n]),
                         reads=[pb], writes=[sgb])
                else:
                    P.op("scalar", lambda e, sg=sg, ps=ps, kc=kc, n=n: e.copy(out=sg[:, kc, 0:n], in_=ps[:, 0:n]),
                         reads=[pb], writes=[sgb])
                cnt += 1
            P.dma("sync", rows(k.xT[:, t0:t0 + n]), sg[:, :, 0:n], reads=[sgb], sembuf=sgb)
        P.barrier()
        P.emit()
        P.end_phase()


def ph_ada(k, l, ntiles=36, subs=(0, 1, 2)):
    nc, P = k.nc, k.P
    with ExitStack() as st:
        slab = Ring(st, nc, "aw", 3, [128, 16, 512], BF16)
        mrow = sbt(st, nc, "mrow", [2, 18432], F32)
        mrowb = [Buf() for _ in range(ntiles)]
        P.begin_phase()
        if l == 0:
            P.op("scalar", lambda e: e.activation(out=k.sT[:], in_=k.cT[:], func=AF.Silu), reads=[k.constb], writes=[k.sTb])
        for nt in range(ntiles):
            sl, sb = slab.next()
            P.dma("gpsimd", sl[:], rows(k.ada_w[l, :, nt * 512:(nt + 1) * 512]), writes=[sb], sembuf=sb)
            ps, pb = psum_next(k)

            def f(e, ps=ps, sl=sl):
                for kc in range(16):
                    ins = mmc(e, ps[0:2, 0:512], lhsT=k.sT[:, kc, :], rhs=sl[:, kc, :], start=(kc == 0), stop=(kc == 15))
                return ins
            P.op("tensor", f, reads=[sb, k.sTb], writes=[pb])
            if nt % 2 == 0:
                P.op("vector", lambda e, ps=ps, nt=nt: e.tensor_copy(out=mrow[0:2, nt * 512:(nt + 1) * 512], in_=ps[0:2, 0:512]),
                     reads=[pb], writes=[mrowb[nt]])
            else:
                P.op("scalar", lambda e, ps=ps, nt=nt: e.copy(out=mrow[0:2, nt * 512:(nt + 1) * 512], in_=ps[0:2, 0:512]),
                     reads=[pb], writes=[mrowb[nt]])
        ps, pb = psum_next(k)

        def f(e, ps=ps):
            for j in range(4 * ntiles):
                ins = mmc(e, ps[:, 2 * j:2 * j + 2], lhsT=mrow[0:2, j * 128:(j + 1) * 128], rhs=k.ident[0:2, 0:2],
                               start=True, stop=True)
            return ins
        P.op("tensor", f, reads=mrowb + [k.constb], writes=[pb])
        mb = k.coefb
        psv = ps[:, 0:8 * ntiles].rearrange("p (j t) -> p j t", t=2)
        for t in range(2):
            P.op("vector", lambda e, t=t: e.tensor_tensor(out=k.mT[:, l, t, 0:4 * ntiles], in0=psv[:, :, t], in1=k.adabT[:, l, 0:4 * ntiles], op=ALU.add),
                 reads=[pb, k.constb], writes=[mb])
        ada_coefs(k, l, subs)
        P.barrier()
        P.emit()
        P.end_phase()


def ada_coefs(k, l, subs=(0, 1, 2)):
        P = k.P
        mb = k.coefb
        for t in range(2):
            for s in subs:
                P.op("vector", lambda e, t=t, s=s: e.scalar_tensor_tensor(
                    out=k.coef[:, l, t, 3 * s + 0, :], in0=k.mT[:, l, t, (3 * s + 1) * 16:(3 * s + 2) * 16], scalar=1.0,
                    in1=k.gT[:, l, s, :], op0=ALU.add, op1=ALU.mult), reads=[mb, k.constb], writes=[mb])
                P.op("vector", lambda e, t=t, s=s: e.tensor_copy(
                    out=k.coef[:, l, t, 3 * s + 1, :], in_=k.mT[:, l, t, (3 * s) * 16:(3 * s + 1) * 16]), reads=[mb], writes=[mb])
                P.op("vector", lambda e, t=t, s=s: e.tensor_scalar(
                    out=k.coef[:, l, t, 3 * s + 2, :], in0=k.mT[:, l, t, (3 * s + 2) * 16:(3 * s + 3) * 16],
                    scalar1=(1.0 if s == 1 else 0.5), scalar2=None, op0=ALU.mult), reads=[mb], writes=[mb])


def ada_bg_steps(k, l, st, bank, bankb, W=512, nslab=3, col0=0, ncols=9 * D, subs=(0, 1, 2), bankT=None, bankTb=None):
    nc, P = k.nc, k.P
    slab = Ring(st, nc, "awb", nslab, [128, 16, W], BF16)
    rowr = Ring(st, nc, "arow", 2, [2, W], F32)
    mb = k.coefb
    NJ = W // 128
    pend = []
    if bankT is None:
        bankT, bankTb = bank, bankb

    def step(nt):
        sl, sb = slab.next()
        P.dma("gpsimd", sl[:], rows(k.ada_w[l, :, col0 + nt * W:col0 + (nt + 1) * W]), writes=[sb], sembuf=sb)

        def f(e):
            for kc in range(16):
                ins = mmc(e, bank[0:2, 0:W], lhsT=k.sT[:, kc, :], rhs=sl[:, kc, :], start=(kc == 0), stop=(kc == 15))
            return ins
        P.op("tensor", f, reads=[sb, k.sTb], writes=[bankb])
        rw, rwb = rowr.next()
        P.op("vector", lambda e: e.tensor_copy(out=rw[0:2, 0:W], in_=bank[0:2, 0:W]), reads=[bankb], writes=[rwb])
        if pend:
            pend.pop(0)()

        def tail():
            def f2(e):
                for j in range(NJ):
                    ins = mmc(e, bankT[:, 2 * j:2 * j + 2], lhsT=rw[0:2, j * 128:(j + 1) * 128], rhs=k.ident[0:2, 0:2], start=True, stop=True)
                return ins
            P.op("tensor", f2, reads=[rwb, k.constb], writes=[bankTb])
            psv = bankT[:, 0:2 * NJ].rearrange("p (j t) -> p j t", t=2)
            c0 = col0 // 128 + nt * NJ
            for t in range(2):
                P.op("vector", lambda e, t=t: e.tensor_tensor(out=k.mT[:, l, t, c0:c0 + NJ], in0=psv[:, :, t],
                                                              in1=k.adabT[:, l, c0:c0 + NJ], op=ALU.add),
                     reads=[bankTb, k.constb], writes=[mb])
        pend.append(tail)

    def finish():
        while pend:
            pend.pop(0)()
        ada_coefs(k, l, subs)
    steps = [(lambda nt=nt: step(nt)) for nt in range(ncols // W)]
    return steps, finish


def norm_mod(k, st_rings, tiles, hT, hTb, l, s, g_only=None, dq="sync", split=False):
    P = k.P
    xin, sqr, tmpr, rsr = st_rings
    G = xin.t[0].shape[1]
    NQ = 16 // G

    def loads(t0, n):
        pend = []
        depth = len(xin.t)

        def issue(q):
            xt, xb = xin.next()
            qn_ = dq if isinstance(dq, str) else dq[q % len(dq)]
            P.dma(qn_, xt[:, :, 0:n], rows(k.xT[q * G * 128:(q + 1) * G * 128, t0:t0 + n]), writes=[xb], sembuf=xb)
            pend.append((xt, xb))
        for q in range(min(depth, NQ)):
            issue(q)
        for q in range(NQ):
            yield q, pend[q]
            if q + depth < NQ:
                issue(q + depth)

    def pass1(t0, n):
        pss, pssb = psum_next(k)
        for q, (xt, xb) in loads(t0, n):
            sq, sqb = sqr.next()
            P.op("scalar", lambda e, sq=sq, xt=xt: e.activation(out=sq[:, :, 0:n], in_=xt[:, :, 0:n], func=AF.Square),
                 reads=[xb], writes=[sqb])

            def f(e, sq=sq, q=q):
                for j in range(G):
                    ins = mmc(e, pss[:, 0:n], lhsT=k.ones_bf[:, :], rhs=sq[:, j, 0:n], start=(q == 0 and j == 0),
                              stop=(q == NQ - 1 and j == G - 1))
                return ins
            P.op("tensor", f, reads=[sqb, k.constb], writes=[pssb])
        rs, rsb = rsr.next()
        P.op("scalar", lambda e: e.activation(out=rs[:, 0:n], in_=pss[:, 0:n], func=AF.Sqrt, bias=k.epsc[:, 0:1], scale=1.0 / D),
             reads=[pssb, k.constb], writes=[rsb])
        P.op("vector", lambda e: e.reciprocal(out=rs[:, 0:n], in_=rs[:, 0:n]), reads=[rsb], writes=[rsb])
        return rs, rsb

    def pass2(ti, t0, n, typ, c0, rs, rsb):
        hTb1 = hTb[ti]
        for q, (xt, xb) in loads(t0, n):
            for j in range(G):
                kc = G * q + j
                tm, tmb = tmpr.next()
                P.op("vector", lambda e, tm=tm, xt=xt, j=j: e.tensor_tensor(
                    out=tm[:, 0:n], in0=xt[:, j, 0:n], in1=rs[:, 0:n], op=ALU.mult), reads=[xb, rsb], writes=[tmb])
                if g_only is None:
                    sc = k.coef[:, l, typ, 3 * s + 0, kc:kc + 1]
                    bi = k.coef[:, l, typ, 3 * s + 1, kc:kc + 1]
                    P.op("scalar", lambda e, tm=tm, kc=kc, sc=sc, bi=bi: e.activation(
                        out=hT[:, kc, c0:c0 + n], in_=tm[:, 0:n], func=AF.Identity, bias=bi, scale=sc),
                        reads=[tmb, k.coefb], writes=[hTb1])
                else:
                    sc = g_only[:, kc:kc + 1]
                    P.op("scalar", lambda e, tm=tm, kc=kc, sc=sc: e.activation(
                        out=hT[:, kc, c0:c0 + n], in_=tm[:, 0:n], func=AF.Identity, scale=sc),
                        reads=[tmb, k.constb], writes=[hTb1])

    if split:
        rss = [pass1(t0, n) for (t0, n, typ, c0) in tiles]
        for ti, (t0, n, typ, c0) in enumerate(tiles):
            pass2(ti, t0, n, typ, c0, *rss[ti])
    else:
        for ti, (t0, n, typ, c0) in enumerate(tiles):
            rs, rsb = pass1(t0, n)
            pass2(ti, t0, n, typ, c0, rs, rsb)


def mk_tiles(block):
    out = []
    c0 = 0
    for t0, n in block:
        out.append((t0, n, 1 if t0 < NCTX else 0, c0))
        c0 += n
    return out


FULL_BLOCKS = [[(0, 256), (256, 512)], [(768, 512), (1280, 256)], [(1536, 512), (2048, 256)]]
LAT_BLOCKS = [[(256, 512), (768, 256)], [(1024, 512), (1536, 256)], [(1792, 512)]]
XT_REGIONS = [(0, 256), (256, 512), (768, 512), (768, 256), (1024, 512), (1280, 256), (1536, 512), (1536, 256),
              (1792, 512), (2048, 256), (1280, 512)]


def ph_ffn(k, l, j, blocks, bg=False):
    nc, P = k.nc, k.P
    s = 0 if j == 0 else 2
    wg_d, wu_d, wd_d = k.ffn_wg[l, j], k.ffn_wu[l, j], k.ffn_wd[l, j]
    BT = 768
    with ExitStack() as st:
        hT = sbt(st, nc, "hT", [128, 16, BT], BF16)
        aT = sbt(st, nc, "aT", [128, 44, BT], BF16)
        wgu = Ring(st, nc, "wgu", 4, [128, 16, 256], BF16)
        wdr = Ring(st, nc, "wd", 2 if bg else 3, [128, 44, 128], BF16)
        xin = Ring(st, nc, "xin", 3, [128, 2, 512], F32)
        sqr = Ring(st, nc, "sq", 2, [128, 2, 512], BF16)
        tmpr = Ring(st, nc, "tmp", 2, [128, 512], F32)
        rsr = Ring(st, nc, "rs", 2, [128, 512], F32)
        sgr = Ring(st, nc, "sg", 2, [128, 512], F32)
        xor_ = Ring(st, nc, "xo", 3, [128, 512], F32)
        P.begin_phase()
        bg_steps, bg_finish = [], None
        if bg:
            k.ps_n = 6
            bg_steps, bg_finish = ada_bg_steps(k, l, st, k.ps[6], k.psb[6], W=256, nslab=2, col0=3 * D, ncols=6 * D, subs=(1, 2),
                                               bankT=k.ps[7], bankTb=k.psb[7])
        hTb = [Buf("hT%d" % i) for i in range(4)]
        aTbs = [Buf() for _ in range(44)]
        tl = [mk_tiles(b) for b in blocks]
        norm_mod(k, (xin, sqr, tmpr, rsr), tl[0], hT, hTb, l, s, dq=("sync", "scalar"))
        for bi, tiles in enumerate(tl):
            for fp in range(22):
                if bg_steps:
                    bg_steps.pop(0)()
                wg, wgb = wgu.next()
                P.dma("gpsimd", wg[:], rows(wg_d[:, fp * 256:(fp + 1) * 256]), writes=[wgb], sembuf=wgb)
                wu, wub = wgu.next()
                P.dma("gpsimd", wu[:], rows(wu_d[:, fp * 256:(fp + 1) * 256]), writes=[wub], sembuf=wub)
                for fs in range(2):
                    fc = fp * 2 + fs
                    for ti, (t0, n, typ, c0) in enumerate(tiles):
                        pg, pgb = psum_next(k)
                        pu, pub = psum_next(k)

                        def f(e, ps=pg, w=wg, fs=fs, c0=c0, n=n):
                            for kc in range(16):
                                ins = mmc(e, ps[:, 0:n], lhsT=w[:, kc, fs * 128:(fs + 1) * 128], rhs=hT[:, kc, c0:c0 + n],
                                               start=(kc == 0), stop=(kc == 15))
                            return ins
                        P.op("tensor", f, reads=[wgb, hTb[ti]], writes=[pgb])

                        def f2(e, ps=pu, w=wu, fs=fs, c0=c0, n=n):
                            for kc in range(16):
                                ins = mmc(e, ps[:, 0:n], lhsT=w[:, kc, fs * 128:(fs + 1) * 128], rhs=hT[:, kc, c0:c0 + n],
                                               start=(kc == 0), stop=(kc == 15))
                            return ins
                        P.op("tensor", f2, reads=[wub, hTb[ti]], writes=[pub])
                        sg, sgb = sgr.next()
                        P.op("scalar", lambda e, sg=sg, pg=pg, n=n: e.activation(out=sg[:, 0:n], in_=pg[:, 0:n], func=AF.Silu),
                             reads=[pgb], writes=[sgb])
                        P.op("vector", lambda e, sg=sg, pu=pu, fc=fc, c0=c0, n=n: e.tensor_tensor(
                            out=aT[:, fc, c0:c0 + n], in0=sg[:, 0:n], in1=pu[:, 0:n], op=ALU.mult),
                            reads=[sgb, pub], writes=[aTbs[fc]])
            for dc in range(16):
                if dc == 3 and bi + 1 < len(tl):
                    norm_mod(k, (xin, sqr, tmpr, rsr), tl[bi + 1], hT, hTb, l, s, dq="scalar", split=True)
                wd, wdb = wdr.next()
                P.dma("gpsimd", wd[:, 0:22, :], rows(wd_d[0:2816, dc * 128:(dc + 1) * 128]), writes=[wdb], sembuf=wdb)
                P.dma("gpsimd", wd[:, 22:44, :], rows(wd_d[2816:5632, dc * 128:(dc + 1) * 128]), writes=[wdb], sembuf=wdb)
                for (t0, n, typ, c0) in tiles:
                    ps, pb = psum_next(k)

                    def f(e, ps=ps, wd=wd, c0=c0, n=n):
                        for kc in range(44):
                            ins = mmc(e, ps[:, 0:n], lhsT=wd[:, kc, :], rhs=aT[:, kc, c0:c0 + n], start=(kc == 0), stop=(kc == 43))
                        return ins
                    P.op("tensor", f, reads=[wdb] + aTbs, writes=[pb])
                    xo, xob = xor_.next()
                    P.dma("sync", xo[:, 0:n], k.xT[dc * 128:(dc + 1) * 128, t0:t0 + n], writes=[xob], sembuf=xob)
                    gcol = k.coef[:, l, typ, 3 * s + 2, dc:dc + 1]
                    P.op("vector", lambda e, xo=xo, ps=ps, gcol=gcol, n=n: e.scalar_tensor_tensor(
                        out=xo[:, 0:n], in0=ps[:, 0:n], scalar=gcol, in1=xo[:, 0:n], op0=ALU.mult, op1=ALU.add),
                        reads=[pb, xob, k.coefb], writes=[xob])
                    P.dma("sync", k.xT[dc * 128:(dc + 1) * 128, t0:t0 + n], xo[:, 0:n], reads=[xob], sembuf=xob)
        while bg_steps:
            bg_steps.pop(0)()
        if bg_finish is not None:
            bg_finish()
        k.ps_n = 8
        P.barrier()
        P.emit()
        P.end_phase()


ALL_TILES = [(0, 256), (256, 512), (768, 512), (1280, 512), (1792, 512)]
LAT_TILES = ALL_TILES[1:]
MLA_SCALE = 192.0 ** -0.5
DIFF_SCALE = 0.125


def evac(P, idx, out, in_, reads, writes):
    if idx % 2 == 0:
        P.op("vector", lambda e: e.tensor_copy(out=out, in_=in_), reads=reads, writes=writes)
    else:
        P.op("scalar", lambda e: e.copy(out=out, in_=in_), reads=reads, writes=writes)


def mm_group(P, ps_ap, pairs, reads, writes):
    def f(e):
        n = len(pairs)
        for i, (a, b) in enumerate(pairs):
            ins = mmc(e, ps_ap, lhsT=a, rhs=b, start=(i == 0), stop=(i == n - 1))
        return ins
    return P.op("tensor", f, reads=reads, writes=writes)


def rope_combine(k, P, rings, psA, pAb, psB, pBb, t0, n, dst_ap, stg, stgb):
    t1r, t2r = rings
    t1, t1b = t1r.next()
    t2, t2b = t2r.next()
    P.op("vector", lambda e: e.tensor_tensor(out=t1[:, 0:n], in0=psA[:, 0:n], in1=k.cosT[:, t0:t0 + n], op=ALU.mult),
         reads=[pAb, k.ropeb], writes=[t1b])
    P.op("vector", lambda e: e.tensor_tensor(out=t2[:, 0:n], in0=psB[:, 0:n], in1=k.sinT[:, t0:t0 + n], op=ALU.mult),
         reads=[pBb, k.ropeb], writes=[t2b])
    P.op("vector", lambda e: e.tensor_tensor(out=stg[:, 0:n], in0=t1[:, 0:n], in1=t2[:, 0:n], op=ALU.add),
         reads=[t1b, t2b], writes=[stgb])
    P.dma("sync", dst_ap, stg[:, 0:n], reads=[stgb], sembuf=stgb)


def ph_proj(k, l):
    nc, P = k.nc, k.P
    last = (l == 1)
    w = k.w_in2[l]
    with ExitStack() as st:
        hT = sbt(st, nc, "hTp", [128, 16, NTOK], BF16)
        slabr = Ring(st, nc, "wsl", 4, [128, 16, 256], BF16)
        dvs = sbt(st, nc, "dvs", [128, 16, 512], BF16)
        dvsb = Buf()
        xin = Ring(st, nc, "xin", 3, [128, 2, 512], F32)
        sqr = Ring(st, nc, "sq", 2, [128, 2, 512], BF16)
        xar = Ring(st, nc, "xa", 2, [128, 512], BF16)
        rperm = sbt(st, nc, "rperm", [128, 128], BF16)
        rpb = Buf()
        tmpr = Ring(st, nc, "tmp", 3, [128, 512], F32)
        rsr = Ring(st, nc, "rs", 5, [128, 512], F32)
        rs2r = Ring(st, nc, "rs2", 1, [128, 512], F32)
        rawr = Ring(st, nc, "raw", 5, [128, 512], F32)
        sq2r = Ring(st, nc, "sq2", 4, [128, 512], BF16)
        t1r = Ring(st, nc, "t1", 2, [128, 512], F32)
        t2r = Ring(st, nc, "t2", 2, [128, 512], F32)
        stgr = Ring(st, nc, "stg", 2, [128, 512], BF16)
        k.cosT = sbt(st, nc, "cosT", [128, NTOK], F32)
        k.sinT = sbt(st, nc, "sinT", [128, NTOK], F32)
        k.ropeb = Buf()
        P.begin_phase()
        P.dma("sync", k.cosT[:], k.cosT_d, writes=[k.ropeb], sembuf=k.ropeb)
        P.dma("sync", k.sinT[:], k.sinT_d, writes=[k.ropeb], sembuf=k.ropeb)
        P.dma("gpsimd", rperm[:], k.rperm_d, writes=[rpb], sembuf=rpb)
        hTb = [Buf() for _ in range(5)]
        tiles = mk_tiles(ALL_TILES)
        norm_mod(k, (xin, sqr, tmpr, rsr), tiles, hT, hTb, l, 1, split=True, dq=("sync", "scalar"))
        qtiles = [(ti, t) for ti, t in enumerate(tiles) if not (last and ti == 0)]
        atiles = list(enumerate(tiles))
        slabs = {}

        def get_slab(si):
            sl, sb_ = slabr.next()
            P.dma("gpsimd", sl[:], rows(w[:, si * 256:(si + 1) * 256]), writes=[sb_], sembuf=sb_)
            return sl, sb_

        def proj_mm(sl, sb_, sub, ti, c0, n):
            ps, pb = psum_next(k)
            mm_group(P, ps[:, 0:n], [(sl[:, kc, sub * 128:(sub + 1) * 128], hT[:, kc, c0:c0 + n]) for kc in range(16)],
                     reads=[sb_, hTb[ti]], writes=[pb])
            return ps, pb

        def normed(slab_ids, nch, gcol, dst, tl):
            sls = [get_slab(si) for si in slab_ids]
            for ti, (t0, n, typ, c0) in tl:
                raws = []
                sqs = []
                for ch in range(nch):
                    sl, sb_ = sls[ch // 2]
                    ps, pb = proj_mm(sl, sb_, ch % 2, ti, c0, n)
                    rw, rwb = rawr.next()
                    P.op("vector", lambda e, rw=rw, ps=ps, n=n: e.tensor_copy(out=rw[:, 0:n], in_=ps[:, 0:n]), reads=[pb], writes=[rwb])
                    sq, sqb = sq2r.next()
                    P.op("scalar", lambda e, sq=sq, rw=rw, n=n: e.activation(out=sq[:, 0:n], in_=rw[:, 0:n], func=AF.Square),
                         reads=[rwb], writes=[sqb])
                    raws.append((rw, rwb))
                    sqs.append((sq, sqb))
                pss, pssb = psum_next(k)
                for ch in range(nch):
                    sq, sqb = sqs[ch]
                    P.op("tensor", lambda e, pss=pss, sq=sq, n=n, ch=ch: mmc(e, pss[:, 0:n], lhsT=k.ones_bf[:, :], rhs=sq[:, 0:n],
                                                                                 start=(ch == 0), stop=(ch == nch - 1)),
                         reads=[sqb, k.constb], writes=[pssb])
                rs, rsb = rs2r.next()
                P.op("scalar", lambda e, rs=rs, pss=pss, n=n: e.activation(out=rs[:, 0:n], in_=pss[:, 0:n], func=AF.Sqrt,
                                                                          bias=k.epsc[:, 0:1], scale=1.0 / (128 * nch)),
                     reads=[pssb, k.constb], writes=[rsb])
                P.op("vector", lambda e, rs=rs, n=n: e.reciprocal(out=rs[:, 0:n], in_=rs[:, 0:n]), reads=[rsb], writes=[rsb])
                for ch in range(nch):
                    rw, rwb = raws[ch]
                    P.op("vector", lambda e, rw=rw, rs=rs, n=n: e.tensor_tensor(out=rw[:, 0:n], in0=rw[:, 0:n], in1=rs[:, 0:n], op=ALU.mult),
                         reads=[rwb, rsb], writes=[rwb])
                    sg, sgb = stgr.next()
                    P.op("scalar", lambda e, sg=sg, rw=rw, ch=ch, n=n: e.activation(out=sg[:, 0:n], in_=rw[:, 0:n], func=AF.Identity,
                                                                                   scale=gcol[:, ch:ch + 1]),
                         reads=[rwb, k.constb], writes=[sgb])
                    P.dma("sync", dst[ch * 128:(ch + 1) * 128, t0:t0 + n], sg[:, 0:n], reads=[sgb], sembuf=sgb)

        pend_rope = []

        def roped(main_slab, main_sub, dst_rows, tl):
            for ti, (t0, n, typ, c0) in tl:
                psA, pAb = proj_mm(main_slab[0], main_slab[1], main_sub, ti, c0, n)
                xa, xab = xar.next()
                P.op("scalar", lambda e, xa=xa, psA=psA, n=n: e.copy(out=xa[:, 0:n], in_=psA[:, 0:n]), reads=[pAb], writes=[xab])
                if pend_rope:
                    pend_rope.pop(0)()

                def tail(psA=psA, pAb=pAb, xa=xa, xab=xab, t0=t0, n=n):
                    psB, pBb = psum_next(k)
                    P.op("tensor", lambda e: mmc(e, psB[:, 0:n], lhsT=rperm[:, :], rhs=xa[:, 0:n], start=True, stop=True),
                         reads=[xab, rpb], writes=[pBb])
                    sg, sgb = stgr.next()
                    rope_combine(k, P, (t1r, t2r), psA, pAb, psB, pBb, t0, n, dst_rows[:, t0:t0 + n], sg, sgb)
                pend_rope.append(tail)

        normed([0], 2, k.kvnT[:, l, :], k.ckvnT, atiles)
        s1 = get_slab(1)
        roped(s1, 0, k.krT, atiles)
        for hp in range(2):
            sm = get_slab(2 + hp)
            for sub in range(2):
                h = hp * 2 + sub
                roped(sm, sub, k.dkT[h * 128:(h + 1) * 128, :], atiles)
        while pend_rope:
            pend_rope.pop(0)()
        normed([6, 7], 4, k.qnormT[:, l, :], k.cqnT, qtiles)
        for hp in range(2):
            sm = get_slab(8 + hp)
            for sub in range(2):
                h = hp * 2 + sub
                roped(sm, sub, k.dqT[h * 128:(h + 1) * 128, :], qtiles)
        while pend_rope:
            pend_rope.pop(0)()
        cnt = 0
        for hp in range(2):
            sm = get_slab(12 + hp)
            for sub in range(2):
                g = hp * 2 + sub
                for ti, (t0, n, typ, c0) in qtiles:
                    ps, pb = proj_mm(sm[0], sm[1], sub, ti, c0, n)
                    sg, sgb = stgr.next()
                    evac(P, cnt, sg[:, 0:n], ps[:, 0:n], [pb], [sgb])
                    cnt += 1
                    P.dma("sync", k.uT[g * 128:(g + 1) * 128, t0:t0 + n], sg[:, 0:n], reads=[sgb], sembuf=sgb)
        P.dma("gpsimd", dvs[:], rows(w[:, 28 * 128:32 * 128]), writes=[dvsb], sembuf=dvsb)
        for tc in range(18):
            ti = 0 if tc < 2 else 1 + (tc - 2) // 4
            ps, pb = psum_next(k)
            mm_group(P, ps[:, :], [(hT[:, kc, tc * 128:(tc + 1) * 128], dvs[:, kc, :]) for kc in range(16)],
                     reads=[dvsb, hTb[ti]], writes=[pb])
            sg, sgb = stgr.next()
            evac(P, tc, sg[:, :], ps[:, :], [pb], [sgb])
            P.dma("sync", k.DVs[tc * 128:(tc + 1) * 128, :], sg[:, :], reads=[sgb], sembuf=sgb)
        P.barrier()
        P.emit()
        P.end_phase()


def ph_proj2(k, l):
    nc, P = k.nc, k.P
    last = (l == 1)
    with ExitStack() as st:
        ckv = sbt(st, nc, "ckv", [128, 2, NTOK], BF16)
        cq = sbt(st, nc, "cq", [128, 4, NTOK], BF16)
        wukv = sbt(st, nc, "wukv", [128, 2, 2048], BF16)
        wuq = sbt(st, nc, "wuq", [128, 4, 2048], BF16)
        t1r = Ring(st, nc, "t1", 2, [128, 512], F32)
        t2r = Ring(st, nc, "t2", 2, [128, 512], F32)
        stgr = Ring(st, nc, "stg", 4, [128, 512], BF16)
        k.cosT = sbt(st, nc, "cosT", [128, NTOK], F32)
        k.sinT = sbt(st, nc, "sinT", [128, NTOK], F32)
        k.ropeb = Buf()
        ckvb, cqb, wkb, wqb = Buf(), Buf(), Buf(), Buf()
        P.begin_phase()
        P.dma("sync", ckv[:], rows(k.ckvnT[:, :]), writes=[ckvb], sembuf=ckvb)
        P.dma("sync", cq[:], rows(k.cqnT[:, :]), writes=[cqb], sembuf=cqb)
        P.dma("sync", k.cosT[:], k.cosT_d, writes=[k.ropeb], sembuf=k.ropeb)
        P.dma("sync", k.sinT[:], k.sinT_d, writes=[k.ropeb], sembuf=k.ropeb)
        P.dma("gpsimd", wukv[:], rows(k.w_ukv2[l]), writes=[wkb], sembuf=wkb)
        P.dma("gpsimd", wuq[:], rows(k.w_uq2[l]), writes=[wqb], sembuf=wqb)
        tiles = mk_tiles(ALL_TILES)
        qtiles = tiles[1:] if last else tiles
        cnt = 0
        for h in range(8):
            for (t0, n, typ, c0) in tiles:
                ps, pb = psum_next(k)
                mm_group(P, ps[:, 0:n], [(wukv[:, kc, h * 128:(h + 1) * 128], ckv[:, kc, t0:t0 + n]) for kc in range(2)],
                         reads=[wkb, ckvb], writes=[pb])
                sg, sgb = stgr.next()
                evac(P, cnt, sg[:, 0:n], ps[:, 0:n], [pb], [sgb])
                cnt += 1
                P.dma("sync", k.knT[h * 128:(h + 1) * 128, t0:t0 + n], sg[:, 0:n], reads=[sgb], sembuf=sgb)
        for tc in range(18):
            for half in range(2):
                ps, pb = psum_next(k)
                mm_group(P, ps[:, :], [(ckv[:, kc, tc * 128:(tc + 1) * 128], wukv[:, kc, 1024 + half * 512:1024 + (half + 1) * 512])
                                      for kc in range(2)], reads=[wkb, ckvb], writes=[pb])
                sg, sgb = stgr.next()
                evac(P, cnt, sg[:, :], ps[:, :], [pb], [sgb])
                cnt += 1
                P.dma("sync", k.Vs[tc * 128:(tc + 1) * 128, half * 512:(half + 1) * 512], sg[:, :], reads=[sgb], sembuf=sgb)
        for h in range(8):
            for (t0, n, typ, c0) in qtiles:
                ps, pb = psum_next(k)
                mm_group(P, ps[:, 0:n], [(wuq[:, kc, h * 128:(h + 1) * 128], cq[:, kc, t0:t0 + n]) for kc in range(4)],
                         reads=[wqb, cqb], writes=[pb])
                sg, sgb = stgr.next()
                evac(P, cnt, sg[:, 0:n], ps[:, 0:n], [pb], [sgb])
                cnt += 1
                P.dma("sync", k.qnT[h * 128:(h + 1) * 128, t0:t0 + n], sg[:, 0:n], reads=[sgb], sembuf=sgb)
        for r in range(4):
            for (t0, n, typ, c0) in qtiles:
                psA, pAb = psum_next(k)
                mm_group(P, psA[:, 0:n], [(wuq[:, kc, 1024 + r * 128:1024 + (r + 1) * 128], cq[:, kc, t0:t0 + n]) for kc in range(4)],
                         reads=[wqb, cqb], writes=[pAb])
                psB, pBb = psum_next(k)
                mm_group(P, psB[:, 0:n], [(wuq[:, kc, 1536 + r * 128:1536 + (r + 1) * 128], cq[:, kc, t0:t0 + n]) for kc in range(4)],
                         reads=[wqb, cqb], writes=[pBb])
                sg, sgb = stgr.next()
                rope_combine(k, P, (t1r, t2r), psA, pAb, psB, pBb, t0, n, k.qrT[r * 128:(r + 1) * 128, t0:t0 + n], sg, sgb)
        P.barrier()
        P.emit()
        P.end_phase()


def ph_mla(k, l, bg_layer=None):
    nc, P = k.nc, k.P
    with ExitStack() as st:
        kn = sbt(st, nc, "kn", [128, 8, NTOK], BF16)
        V = sbt(st, nc, "V", [128, 18, 1024], BF16)
        kr = sbt(st, nc, "kr", [128, NTOK], BF16)
        qnr = Ring(st, nc, "qn", 2, [128, 8, 512], BF16)
        qrr = Ring(st, nc, "qr", 2, [128, 8, 512], BF16)
        Er = Ring(st, nc, "E", 4, [128, 512], BF16)
        rzr = Ring(st, nc, "rz", 2, [128, 512], F32)
        stgr = Ring(st, nc, "stg", 3, [128, 512], BF16)
        knb, Vb, krb = Buf(), Buf(), Buf()
        P.begin_phase()
        bg_steps, bg_finish = [], None
        if bg_layer is not None:
            bg_steps, bg_finish = ada_bg_steps(k, bg_layer, st, k.ps[3], k.psb[3])
        NS = 3 if bg_layer is not None else 4
        for t_, b_ in zip(qrr.t, qrr.b):
            P.op("vector", lambda e, t_=t_: e.memset(t_[:], 0.0), writes=[b_])
        P.dma("sync", kn[:], rows(k.knT[:, :]), writes=[knb], sembuf=knb)
        P.dma("sync", kr[:], k.krT[:, :], writes=[krb], sembuf=krb)
        qt = [((256 + i * 512), 512, list(range(18))) for i in range(4)]
        if l == 0:
            qt = [(0, 256, [0, 1])] + qt
        si = 0
        hh = 0
        def load_q(t0, n):
            qn, qnb = qnr.next()
            qr, qrb = qrr.next()
            P.dma("sync", qn[:, :, 0:n], rows(k.qnT[:, t0:t0 + n]), writes=[qnb], sembuf=qnb)
            for h_ in range(8):
                hp_ = 64 * (h_ % 2)
                r0_ = (h_ // 2) * 128 + hp_
                P.dma("sync", qr[hp_:hp_ + 64, h_, 0:n], k.qrT[r0_:r0_ + 64, t0:t0 + n], writes=[qrb], sembuf=qrb)
            return qn, qnb, qr, qrb
        qloads = [load_q(qt[0][0], qt[0][1])]
        P.dma("sync", V[:], rows(k.Vs[:, :]), writes=[Vb], sembuf=Vb)
        for qi, (t0, n, keys) in enumerate(qt):
            if qi + 1 < len(qt):
                qloads.append(load_q(qt[qi + 1][0], qt[qi + 1][1]))
            qn, qnb, qr, qrb = qloads[qi]

            def do_head(h, t0, n, keys, qn, qnb, qr, qrb, O, Ob, Z, Zb):
                nonlocal si
                hp = 64 * (h % 2)

                def s_mm(j):
                    nonlocal si
                    S, Sb = k.ps[si % NS], k.psb[si % NS]
                    si += 1
                    mm_group(P, S[:, 0:n], [(kn[:, h, j * 128:(j + 1) * 128], qn[:, h, 0:n]),
                                            (kr[:, j * 128:(j + 1) * 128], qr[:, h, 0:n])],
                             reads=[knb, krb, qnb, qrb], writes=[Sb])
                    return S, Sb
                cur = s_mm(keys[0])
                for ji, j in enumerate(keys):
                    S, Sb = cur
                    if ji + 1 < len(keys):
                        cur = s_mm(keys[ji + 1])
                    E, Eb = Er.next()
                    P.op("scalar", lambda e, E=E, S=S: e.activation(out=E[:, 0:n], in_=S[:, 0:n], func=AF.Exp, scale=MLA_SCALE),
                         reads=[Sb], writes=[Eb])

                    def pv(e, E=E, j=j, ji=ji, O=O, Z=Z):
                        mmc(e, O[:, 0:n], lhsT=V[:, j, h * 128:(h + 1) * 128], rhs=E[:, 0:n], start=(ji == 0), stop=(ji == len(keys) - 1))
                        return mmc(e, Z[:, 0:n], lhsT=k.ones_bf[:, :], rhs=E[:, 0:n], start=(ji == 0), stop=(ji == len(keys) - 1))
                    P.op("tensor", pv, reads=[Eb, Vb, k.constb], writes=[Ob, Zb])
                rz, rzb = rzr.next()
                P.op("vector", lambda e, rz=rz, Z=Z: e.reciprocal(out=rz[:, 0:n], in_=Z[:, 0:n]), reads=[Zb], writes=[rzb])
                sg, sgb = stgr.next()
                P.op("vector", lambda e, sg=sg, O=O, rz=rz: e.tensor_tensor(out=sg[:, 0:n], in0=O[:, 0:n], in1=rz[:, 0:n], op=ALU.mult),
                     reads=[Ob, rzb], writes=[sgb])
                P.dma("sync", k.oT[h * 128:(h + 1) * 128, t0:t0 + n], sg[:, 0:n], reads=[sgb], sembuf=sgb)
            for h in range(8):
                if bg_steps:
                    bg_steps.pop(0)()
                do_head(h, t0, n, keys, qn, qnb, qr, qrb, k.ps[4 + hh % 2], k.psb[4 + hh % 2], k.ps[6 + hh % 2], k.psb[6 + hh % 2])
                hh += 1
        while bg_steps:
            bg_steps.pop(0)()
        if bg_finish is not None:
            bg_finish()
        P.barrier()
        P.emit()
        P.end_phase()


def ph_diff(k, l):
    nc, P = k.nc, k.P
    lam_init = 0.8 - 0.6 * math.exp(-0.3 * l)
    cfac = 1.0 - lam_init
    with ExitStack() as st:
        dk = sbt(st, nc, "dk", [128, 4, NTOK], BF16)
        DV = sbt(st, nc, "DV", [128, 18, 512], BF16)
        dqr = Ring(st, nc, "dq", 2, [128, 4, 512], BF16)
        Er = Ring(st, nc, "E", 6, [128, 512], BF16)
        fr = Ring(st, nc, "f", 12, [128, 512], F32)
        sqr = Ring(st, nc, "sq", 2, [128, 512], BF16)
        stgr = Ring(st, nc, "stg", 3, [128, 512], BF16)
        pending = []
        lt = sbt(st, nc, "lt", [128, 2, 64], F32)
        lr = sbt(st, nc, "lr", [128, 4], F32)
        dkb, DVb, lb = Buf(), Buf(), Buf()
        P.begin_phase()
        P.dma("sync", dk[:], rows(k.dkT[:, :]), writes=[dkb], sembuf=dkb)
        lam = k.lamrep
        P.op("vector", lambda e: e.tensor_tensor(out=lt[:, 0, :], in0=lam[:, l, 0, :], in1=lam[:, l, 1, :], op=ALU.mult), reads=[k.constb], writes=[lb])
        P.op("vector", lambda e: e.tensor_tensor(out=lt[:, 1, :], in0=lam[:, l, 2, :], in1=lam[:, l, 3, :], op=ALU.mult), reads=[k.constb, lb], writes=[lb])
        P.op("vector", lambda e: e.reduce_sum(out=lr[:, 0:2], in_=lt[:, :, :], axis=mybir.AxisListType.X), reads=[lb], writes=[lb])
        P.op("scalar", lambda e: e.activation(out=lr[:, 0:2], in_=lr[:, 0:2], func=AF.Exp), reads=[lb], writes=[lb])
        P.op("vector", lambda e: e.tensor_tensor(out=lr[:, 2:3], in0=lr[:, 1:2], in1=lr[:, 0:1], op=ALU.subtract), reads=[lb], writes=[lb])
        P.op("vector", lambda e: e.tensor_scalar(out=lr[:, 2:3], in0=lr[:, 2:3], scalar1=-lam_init, scalar2=None, op0=ALU.add), reads=[lb], writes=[lb])
        P.op("vector", lambda e: e.memset(lr[:, 3:4], EPS / (cfac * cfac)), reads=[lb], writes=[lb])
        neglam = lr[:, 2:3]
        epsc2 = lr[:, 3:4]
        qt = [((256 + i * 512), 512, list(range(18))) for i in range(4)]
        if l == 0:
            qt = [(0, 256, [0, 1])] + qt
        si = 0
        def load_q(t0, n):
            dq, dqb = dqr.next()
            P.dma("sync", dq[:, :, 0:n], rows(k.dqT[:, t0:t0 + n]), writes=[dqb], sembuf=dqb)
            return dq, dqb
        qloads = [load_q(qt[0][0], qt[0][1])]
        P.dma("sync", DV[:], rows(k.DVs[:, :]), writes=[DVb], sembuf=DVb)
        for qi, (t0, n, keys) in enumerate(qt):
            if qi + 1 < len(qt):
                qloads.append(load_q(qt[qi + 1][0], qt[qi + 1][1]))
            dq, dqb = qloads[qi]

            def do_head(h, t0, n, keys, dq, dqb):
                nonlocal si
                acc = [(k.ps[4 + i], k.psb[4 + i]) for i in range(4)]

                def s_mm(j):
                    nonlocal si
                    S1, S1b = k.ps[(2 * si) % 4], k.psb[(2 * si) % 4]
                    S2, S2b = k.ps[(2 * si) % 4 + 1], k.psb[(2 * si) % 4 + 1]
                    si += 1
                    P.op("tensor", lambda e: mmc(e, S1[:, 0:n], lhsT=dk[0:64, h, j * 128:(j + 1) * 128], rhs=dq[0:64, h, 0:n], start=True, stop=True),
                         reads=[dkb, dqb], writes=[S1b])
                    P.op("tensor", lambda e: mmc(e, S2[:, 0:n], lhsT=dk[64:128, h, j * 128:(j + 1) * 128], rhs=dq[64:128, h, 0:n], start=True, stop=True),
                         reads=[dkb, dqb], writes=[S2b])
                    return (S1, S1b, S2, S2b)
                cur = s_mm(keys[0])
                for ji, j in enumerate(keys):
                    S1, S1b, S2, S2b = cur
                    if ji + 1 < len(keys):
                        cur = s_mm(keys[ji + 1])
                    first, lastj = (ji == 0), (ji == len(keys) - 1)
                    for (S, Sb, (O, Ob), (Z, Zb)) in ((S1, S1b, acc[0], acc[1]), (S2, S2b, acc[2], acc[3])):
                        E, Eb = Er.next()
                        P.op("scalar", lambda e, E=E, S=S: e.activation(out=E[:, 0:n], in_=S[:, 0:n], func=AF.Exp, scale=DIFF_SCALE),
                             reads=[Sb], writes=[Eb])

                        def pv(e, E=E, O=O, Z=Z, j=j, first=first, lastj=lastj):
                            mmc(e, O[:, 0:n], lhsT=DV[:, j, h * 128:(h + 1) * 128], rhs=E[:, 0:n], start=first, stop=lastj)
                            return mmc(e, Z[:, 0:n], lhsT=k.ones_bf[:, :], rhs=E[:, 0:n], start=first, stop=lastj)
                        P.op("tensor", pv, reads=[Eb, DVb, k.constb], writes=[Ob, Zb])
                    if pending and (ji == 8 or lastj):
                        pending.pop(0)()
                cps = []
                for (A, Ab) in acc:
                    c, cb = fr.next()
                    P.op("vector", lambda e, c=c, A=A: e.tensor_copy(out=c[:, 0:n], in_=A[:, 0:n]), reads=[Ab], writes=[cb])
                    cps.append((c, cb))
                (o1, o1b), (z1, z1b), (o2, o2b), (z2, z2b) = cps
                P.op("vector", lambda e: e.reciprocal(out=z1[:, 0:n], in_=z1[:, 0:n]), reads=[z1b], writes=[z1b])
                P.op("vector", lambda e: e.reciprocal(out=z2[:, 0:n], in_=z2[:, 0:n]), reads=[z2b], writes=[z2b])
                P.op("vector", lambda e: e.tensor_tensor(out=o1[:, 0:n], in0=o1[:, 0:n], in1=z1[:, 0:n], op=ALU.mult), reads=[o1b, z1b], writes=[o1b])
                P.op("vector", lambda e: e.tensor_tensor(out=o2[:, 0:n], in0=o2[:, 0:n], in1=z2[:, 0:n], op=ALU.mult), reads=[o2b, z2b], writes=[o2b])
                o, ob = fr.next()
                P.op("vector", lambda e: e.scalar_tensor_tensor(out=o[:, 0:n], in0=o2[:, 0:n], scalar=neglam, in1=o1[:, 0:n],
                                                                op0=ALU.mult, op1=ALU.add), reads=[o1b, o2b, lb], writes=[ob])
                sq, sqb = sqr.next()
                P.op("vector", lambda e: e.tensor_tensor(out=sq[:, 0:n], in0=o[:, 0:n], in1=o[:, 0:n], op=ALU.mult), reads=[ob], writes=[sqb])

                def finish():
                    nonlocal si
                    SS, SSb = k.ps[(2 * si) % 4], k.psb[(2 * si) % 4]
                    P.op("tensor", lambda e: mmc(e, SS[:, 0:n], lhsT=k.ones_bf[:, :], rhs=sq[:, 0:n], start=True, stop=True),
                         reads=[sqb, k.constb], writes=[SSb])
                    rs, rsb = fr.next()
                    P.op("scalar", lambda e: e.activation(out=rs[:, 0:n], in_=SS[:, 0:n], func=AF.Ln, bias=epsc2,
                                                          scale=1.0 / (128.0 * cfac * cfac)), reads=[SSb, lb], writes=[rsb])
                    P.op("scalar", lambda e: e.activation(out=rs[:, 0:n], in_=rs[:, 0:n], func=AF.Exp, scale=-0.5), reads=[rsb], writes=[rsb])
                    P.op("vector", lambda e: e.tensor_tensor(out=o[:, 0:n], in0=o[:, 0:n], in1=rs[:, 0:n], op=ALU.mult),
                         reads=[ob, rsb], writes=[ob])
                    sg, sgb = stgr.next()
                    P.op("scalar", lambda e: e.activation(out=sg[:, 0:n], in_=o[:, 0:n], func=AF.Identity, scale=k.sublnT[:, l:l + 1]),
                         reads=[ob, k.constb], writes=[sgb])
                    P.dma("sync", k.oT[1024 + h * 128:1024 + (h + 1) * 128, t0:t0 + n], sg[:, 0:n], reads=[sgb], sembuf=sgb)
                pending.append(finish)
            for h in range(4):
                do_head(h, t0, n, keys, dq, dqb)
        while pending:
            pending.pop(0)()
        P.barrier()
        P.emit()
        P.end_phase()


def ph_fourier(k, l):
    nc, P = k.nc, k.P
    with ExitStack() as st:
        u = sbt(st, nc, "u", [128, 4, NTOK], BF16)
        ccsc = sbt(st, nc, "ccsc", [128, 256], BF16)
        AB = sbt(st, nc, "AB", [128, 16, 4, 256], BF16)
        ABc = sbt(st, nc, "ABc", [128, 2, 4, 256], BF16)
        csr = Ring(st, nc, "cs", 2, [128, 16, 512], BF16)
        ssr = Ring(st, nc, "ss", 2, [128, 16, 512], BF16)
        c256 = sbt(st, nc, "c256", [128, 2, 256], BF16)
        s256 = sbt(st, nc, "s256", [128, 2, 256], BF16)
        stgr = Ring(st, nc, "stg", 3, [128, 512], BF16)
        ub, cb_, c2b, s2b = Buf(), Buf(), Buf(), Buf()
        ABb = [Buf() for _ in range(16)]
        ABcb = [Buf() for _ in range(2)]
        P.begin_phase()
        P.dma("sync", u[:], rows(k.uT[:, :]), writes=[ub], sembuf=ub)
        P.dma("gpsimd", ccsc[:], k.ccsc_d, writes=[cb_], sembuf=cb_)
        cnt = 0

        def step1(tok0, ABt, ABtb, tc):
            nonlocal cnt
            for gp in range(2):
                ps, pb = psum_next(k)

                def f(e, ps=ps, gp=gp):
                    for gs in range(2):
                        ins = mmc(e, ps[:, gs * 256:(gs + 1) * 256], lhsT=u[:, gp * 2 + gs, tok0:tok0 + 128], rhs=ccsc[:, :], start=True, stop=True)
                    return ins
                P.op("tensor", f, reads=[ub, cb_], writes=[pb])
                evac(P, cnt, ABt[:, tc, gp * 2:gp * 2 + 2, :].rearrange("p a b -> p (a b)"), ps[:, :], [pb], [ABtb[tc]])
                cnt += 1
        for tc in range(16):
            step1(256 + tc * 128, AB, ABb, tc)
        for stile in range(4):
            cs, csb = csr.next()
            ss, ssb = ssr.next()
            P.dma("gpsimd", cs[:], rows(k.cs2048[:, stile * 512:(stile + 1) * 512]), writes=[csb], sembuf=csb)
            P.dma("gpsimd", ss[:], rows(k.ss2048[:, stile * 512:(stile + 1) * 512]), writes=[ssb], sembuf=ssb)
            for g in range(4):
                ps, pb = psum_next(k)
                pairs = []
                for sc in range(16):
                    pairs.append((AB[:, sc, g, 0:128], cs[:, sc, :]))
                    pairs.append((AB[:, sc, g, 128:256], ss[:, sc, :]))
                mm_group(P, ps[:, :], pairs, reads=ABb + [csb, ssb], writes=[pb])
                sg, sgb = stgr.next()
                evac(P, cnt, sg[:, :], ps[:, :], [pb], [sgb])
                cnt += 1
                P.dma("sync", k.oT[1536 + g * 128:1536 + (g + 1) * 128, 256 + stile * 512:256 + (stile + 1) * 512], sg[:, :],
                      reads=[sgb], sembuf=sgb)
        if l == 0:
            P.dma("gpsimd", c256[:], rows(k.c256_d), writes=[c2b], sembuf=c2b)
            P.dma("gpsimd", s256[:], rows(k.s256_d), writes=[s2b], sembuf=s2b)
            for tc in range(2):
                step1(tc * 128, ABc, ABcb, tc)
            for g in range(4):
                ps, pb = psum_next(k)
                pairs = []
                for sc in range(2):
                    pairs.append((ABc[:, sc, g, 0:128], c256[:, sc, :]))
                    pairs.append((ABc[:, sc, g, 128:256], s256[:, sc, :]))
                mm_group(P, ps[:, 0:256], pairs, reads=ABcb + [c2b, s2b], writes=[pb])
                sg, sgb = stgr.next()
                evac(P, cnt, sg[:, 0:256], ps[:, 0:256], [pb], [sgb])
                cnt += 1
                P.dma("sync", k.oT[1536 + g * 128:1536 + (g + 1) * 128, 0:256], sg[:, 0:256], reads=[sgb], sembuf=sgb)
        P.barrier()
        P.emit()
        P.end_phase()


def ph_wout(k, l):
    nc, P = k.nc, k.P
    tiles = mk_tiles(ALL_TILES)
    if l == 1:
        tiles = tiles[1:]
    with ExitStack() as st:
        o = sbt(st, nc, "o", [128, 16, NTOK], BF16)
        wor = Ring(st, nc, "wo", 3, [128, 16, 256], BF16)
        xor_ = Ring(st, nc, "xo", 3, [128, 512], F32)
        ob = [Buf() for _ in range(16)]
        P.begin_phase()
        for kc in range(16):
            P.dma("sync", o[:, kc, :], k.oT[kc * 128:(kc + 1) * 128, :], writes=[ob[kc]], sembuf=ob[kc])
        for dp in range(8):
            wo, wob = wor.next()
            P.dma("gpsimd", wo[:], rows(k.w_out[l, :, dp * 256:(dp + 1) * 256]), writes=[wob], sembuf=wob)
            for ds in range(2):
                dc = dp * 2 + ds
                for (t0, n, typ, c0) in tiles:
                    ps, pb = psum_next(k)
                    mm_group(P, ps[:, 0:n], [(wo[:, kc, ds * 128:(ds + 1) * 128], o[:, kc, t0:t0 + n]) for kc in range(16)],
                             reads=[wob] + ob, writes=[pb])
                    xo, xob = xor_.next()
                    P.dma("scalar", xo[:, 0:n], k.xT[dc * 128:(dc + 1) * 128, t0:t0 + n], writes=[xob], sembuf=xob)
                    gcol = k.coef[:, l, typ, 5, dc:dc + 1]
                    P.op("vector", lambda e, xo=xo, ps=ps, gcol=gcol, n=n: e.scalar_tensor_tensor(
                        out=xo[:, 0:n], in0=ps[:, 0:n], scalar=gcol, in1=xo[:, 0:n], op0=ALU.mult, op1=ALU.add),
                        reads=[pb, xob, k.coefb], writes=[xob])
                    P.dma("sync", k.xT[dc * 128:(dc + 1) * 128, t0:t0 + n], xo[:, 0:n], reads=[xob], sembuf=xob)
        P.barrier()
        P.emit()
        P.end_phase()


def ph_final(k):
    nc, P = k.nc, k.P
    with ExitStack() as st:
        yr = Ring(st, nc, "y", 2, [128, 16, 512], F32)
        osr = Ring(st, nc, "os", 2, [128, D], F32)
        xin = Ring(st, nc, "xin", 3, [128, 2, 512], F32)
        sqr = Ring(st, nc, "sq", 2, [128, 2, 512], BF16)
        tmpr = Ring(st, nc, "tmp", 2, [128, 512], F32)
        rsr = Ring(st, nc, "rs", 2, [128, 512], F32)
        P.begin_phase()
        cnt = 0
        for (t0, n) in LAT_TILES:
            y, yb = yr.next()
            norm_mod(k, (xin, sqr, tmpr, rsr), [(t0, n, 0, 0)], y, [yb], 0, 0, g_only=k.fnT)
            for tc in range(4):
                os_, osb = osr.next()
                for q in range(4):
                    ps, pb = psum_next(k)

                    def f(e, ps=ps, y=y, tc=tc, q=q):
                        for j in range(4):
                            kc = q * 4 + j
                            ins = mmc(e, ps[:, j * 128:(j + 1) * 128], lhsT=y[:, kc, tc * 128:(tc + 1) * 128], rhs=k.ident[:, :],
                                           start=True, stop=True)
                        return ins
                    P.op("tensor", f, reads=[yb, k.constb], writes=[pb])
                    evac(P, cnt, os_[:, q * 512:(q + 1) * 512], ps[:, :], [pb], [osb])
                    cnt += 1
                r0 = t0 - NCTX + tc * 128
                P.dma("sync", k.out[r0:r0 + 128, :], os_[:, :], reads=[osb], sembuf=osb)
        P.barrier()
        P.emit()
        P.end_phase()


SCRATCH = [("ckvnT", [256, NTOK]), ("cqnT", [512, NTOK]), ("knT", [1024, NTOK]), ("Vs", [NTOK, 1024]), ("krT", [128, NTOK]),
           ("dkT", [512, NTOK]), ("DVs", [NTOK, 512]), ("qnT", [1024, NTOK]), ("qrT", [512, NTOK]), ("dqT", [512, NTOK]),
           ("uT", [512, NTOK]), ("oT", [2048, NTOK])]


NB2 = 3
SW_WINDOW = 5
DBG_MODE = ""


def build(stop_after=None, debug=False, skip=()):
    nc = bass.Bass("TRN2", target_bir_lowering=False)
    k = K()
    k.nc = nc
    MM_COUNT[0] = 0

    def din(name, shape, dt=F32):
        return nc.dram_tensor(name, list(shape), dt, kind="ExternalInput").ap()

    def dscratch(name, shape, dt):
        if debug:
            return nc.dram_tensor(name, list(shape), dt, kind="ExternalOutput").ap()
        return nc.dram_tensor(name, list(shape), dt).ap()

    k.x = din("x", [NLAT, D])
    k.ctx = din("ctx", [NCTX, D])
    cT_d = din("cT", [128, 32])
    k.ada_w = din("ada_w", [2, D, 9 * D])
    adabT_d = din("adabT", [128, 2 * 144])
    gT_d = din("gT", [128, 2 * 3 * 16])
    k.ffn_wg = din("ffn_wg", [2, 2, D, DFF])
    k.ffn_wu = din("ffn_wu", [2, 2, D, DFF])
    k.ffn_wd = din("ffn_wd", [2, 2, DFF, D])
    ident_d = din("ident", [128, 128])
    k.w_in2 = din("w_in2", [2, D, 4096])
    k.w_ukv2 = din("w_ukv2", [2, 256, 2048])
    k.w_uq2 = din("w_uq2", [2, 512, 2048])
    k.w_out = din("w_out", [2, D, D])
    k.cosT_d = din("cosT", [128, NTOK])
    k.sinT_d = din("sinT", [128, NTOK])
    kvnT_d = din("kvnT", [128, 4])
    qnormT_d = din("qnormT", [128, 8])
    sublnT_d = din("sublnT", [128, 2])
    lamrep_d = din("lamrep", [128, 512])
    fnT_d = din("fnT", [128, 16])
    k.rperm_d = din("rperm", [128, 128])
    k.ccsc_d = din("ccsc", [128, 256])
    k.cs2048 = din("cs2048", [2048, 2048])
    k.ss2048 = din("ss2048", [2048, 2048])
    k.c256_d = din("c256", [256, 256])
    k.s256_d = din("s256", [256, 256])
    k.out = nc.dram_tensor("out", [NLAT, D], F32, kind="ExternalOutput").ap()
    k.xT = dscratch("xT", [D, NTOK], F32)
    for name, shape in SCRATCH:
        setattr(k, name, dscratch(name, shape, BF16))

    with ExitStack() as st:
        P = Prog(nc)
        P.open(st)
        k.P = P
        k.ps = [st.enter_context(nc.psum_tensor("ps%d" % i, [128, 512], F32)) for i in range(8)]
        k.psb = [Buf("ps%d" % i) for i in range(8)]
        k.ps_i = 0
        k.ps_n = 8
        sb = lambda name, shape, dt: st.enter_context(nc.sbuf_tensor("s_" + name, shape, dt))
        k.ident = sb("ident", [128, 128], F32)
        k.ones_bf = sb("ones_bf", [128, 128], BF16)
        k.epsc = sb("epsc", [128, 1], F32)
        k.cT = sb("cT", [128, 16, 2], F32)
        k.sT = sb("sT", [128, 16, 2], BF16)
        k.adabT = sb("adabT", [128, 2, 144], F32)
        k.gT = sb("gT", [128, 2, 3, 16], F32)
        k.mT = sb("mT", [128, 2, 2, 144], F32)
        k.coef = sb("coef", [128, 2, 2, 9, 16], F32)
        k.kvnT = sb("kvnT", [128, 2, 2], F32)
        k.qnormT = sb("qnormT", [128, 2, 4], F32)
        k.sublnT = sb("sublnT", [128, 2], F32)
        k.lamrep = sb("lamrep", [128, 2, 4, 64], F32)
        k.fnT = sb("fnT", [128, 16], F32)
        k.constb = Buf("const")
        k.sTb = Buf("sT")
        k.coefb = Buf("coef")

        P.begin_phase()
        cb = k.constb
        P.dma("sync", k.ident[:], ident_d, writes=[cb], sembuf=cb)
        P.dma("sync", k.cT[:].rearrange("p a b -> p (a b)"), cT_d, writes=[cb], sembuf=cb)
        P.dma("sync", k.adabT[:].rearrange("p a b -> p (a b)"), adabT_d, writes=[cb], sembuf=cb)
        P.dma("sync", k.gT[:].rearrange("p a b c -> p (a b c)"), gT_d, writes=[cb], sembuf=cb)
        P.dma("sync", k.kvnT[:].rearrange("p a b -> p (a b)"), kvnT_d, writes=[cb], sembuf=cb)
        P.dma("sync", k.qnormT[:].rearrange("p a b -> p (a b)"), qnormT_d, writes=[cb], sembuf=cb)
        P.dma("sync", k.sublnT[:], sublnT_d, writes=[cb], sembuf=cb)
        P.dma("sync", k.lamrep[:].rearrange("p a b c -> p (a b c)"), lamrep_d, writes=[cb], sembuf=cb)
        P.dma("sync", k.fnT[:], fnT_d, writes=[cb], sembuf=cb)
        P.op("vector", lambda e: e.memset(k.ones_bf[:], 1.0), writes=[cb])
        P.op("vector", lambda e: e.memset(k.epsc[:], EPS), writes=[cb])
        P.barrier()
        P.emit()
        P.end_phase()

        stages = [("tin", lambda: ph_transpose_in(k))]
        for l in range(2):
            if l == 0:
                stages.append(("ada%d" % l, lambda l=l: ph_ada(k, l, ntiles=12, subs=(0,))))
            stages.append(("ffn%d_0" % l, lambda l=l: ph_ffn(k, l, 0, FULL_BLOCKS, bg=(l == 0))))
            stages.append(("proj%d" % l, lambda l=l: ph_proj(k, l)))
            stages.append(("proj2_%d" % l, lambda l=l: ph_proj2(k, l)))
            stages.append(("mla%d" % l, lambda l=l: ph_mla(k, l, bg_layer=(1 if l == 0 else None))))
            stages.append(("diff%d" % l, lambda l=l: ph_diff(k, l)))
            stages.append(("four%d" % l, lambda l=l: ph_fourier(k, l)))
            stages.append(("wout%d" % l, lambda l=l: ph_wout(k, l)))
            stages.append(("ffn%d_1" % l, lambda l=l: ph_ffn(k, l, 1, (FULL_BLOCKS if l == 0 else LAT_BLOCKS)[:NB2])))
        stages.append(("final", lambda: ph_final(k)))
        k.stage_mm = []
        for name, fn in stages:
            if name not in skip:
                fn()
            k.stage_mm.append((name, MM_COUNT[0]))
            if stop_after == name:
                break
        P.begin_phase()
        P.wait("sync")
        P.emit()
        P.end_phase()
        k.n_inst = P.n_inst
    return nc, k


_CONST_CACHE = {}


def host_consts():
    if _CONST_CACHE:
        return _CONST_CACHE
    f = np.float32
    s = np.arange(NLAT)
    pos = [s // 64, s % 64]
    cosT = np.ones((64, NTOK), np.float64)
    sinT = np.zeros((64, NTOK), np.float64)
    for i in range(64):
        jj = i % 16
        inv = np.float32(10000.0) ** np.float32(-2.0 * jj / 32.0)
        ang = pos[i // 32].astype(np.float32) * np.float32(inv)
        cosT[i, NCTX:] = np.cos(ang.astype(np.float64))
        sn = np.sin(ang.astype(np.float64))
        sinT[i, NCTX:] = -sn if (i % 32) < 16 else sn
    _CONST_CACHE["cosT"] = np.ascontiguousarray(np.concatenate([cosT, cosT], 0).astype(f))
    _CONST_CACHE["sinT"] = np.ascontiguousarray(np.concatenate([sinT, sinT], 0).astype(f))
    c = np.arange(128)
    ang = 2 * np.pi * ((c[:, None] * c[None, :]) % 128) / 128.0
    _CONST_CACHE["ccsc"] = np.ascontiguousarray(np.concatenate([np.cos(ang), -np.sin(ang)], 1).astype(f))
    s2 = np.arange(2048, dtype=np.int64)
    ang = 2 * np.pi * ((s2[:, None] * s2[None, :]) % 2048) / 2048.0
    _CONST_CACHE["cs2048"] = np.ascontiguousarray((np.cos(ang) / 512.0).astype(f))
    _CONST_CACHE["ss2048"] = np.ascontiguousarray((np.sin(ang) / 512.0).astype(f))
    s3 = np.arange(256, dtype=np.int64)
    ang = 2 * np.pi * ((s3[:, None] * s3[None, :]) % 256) / 256.0
    nrm = math.sqrt(256.0 * 128.0)
    _CONST_CACHE["c256"] = np.ascontiguousarray((np.cos(ang) / nrm).astype(f))
    _CONST_CACHE["s256"] = np.ascontiguousarray((np.sin(ang) / nrm).astype(f))
    _CONST_CACHE["ident"] = np.eye(128, dtype=f)
    rp = np.zeros((128, 128), f)
    rp[np.arange(128) ^ 16, np.arange(128)] = 1.0
    _CONST_CACHE["rperm"] = rp
    return _CONST_CACHE


def swap64(a):
    m = a.shape[1] // 64
    idx = (np.arange(m)[:, None] * 64 + (np.arange(64) ^ 16)[None, :]).reshape(-1)
    return a[:, idx]


def host_shared(inp):
    f = np.float32
    w2 = []
    for l in range(2):
        wi = np.asarray(inp["w_in"][l], f)
        ckv, kr, dk, dv = wi[:, 0:256], wi[:, 256:320], wi[:, 320:832], wi[:, 832:1344]
        cq, dq, u = wi[:, 1344:1856], wi[:, 1856:2368], wi[:, 2368:2880]
        w2.append(np.concatenate([ckv, kr, kr, swap64(kr), swap64(kr), dk, swap64(dk), cq, dq, swap64(dq), u, dv], axis=1))
    ukv = []
    uq = []
    for l in range(2):
        r = np.asarray(inp["mla_w_ukv"][l], f).reshape(256, 8, 256)
        ukv.append(np.concatenate([r[:, :, :128].reshape(256, 1024), r[:, :, 128:].reshape(256, 1024)], 1))
        r = np.asarray(inp["mla_w_uq"][l], f).reshape(512, 8, 192)
        rope = r[:, :, 128:].reshape(512, 512)
        uq.append(np.concatenate([r[:, :, :128].reshape(512, 1024), rope, swap64(rope)], 1))
    g = np.asarray(inp["norm_g"], f)
    sh = {
        "ada_w": np.asarray(inp["ada_w"], f),
        "adabT": np.ascontiguousarray(np.stack([np.asarray(inp["ada_b"][l], f).reshape(144, 128).T for l in range(2)], axis=1).reshape(128, 288)),
        "gT": np.ascontiguousarray(g.reshape(2, 3, 16, 128).transpose(3, 0, 1, 2).reshape(128, 96)),
        "ffn_wg": np.asarray(inp["ffn_wg"], f),
        "ffn_wu": np.asarray(inp["ffn_wu"], f),
        "ffn_wd": np.asarray(inp["ffn_wd"], f),
        "w_in2": np.ascontiguousarray(np.stack(w2, 0)),
        "w_ukv2": np.ascontiguousarray(np.stack(ukv, 0)),
        "w_uq2": np.ascontiguousarray(np.stack(uq, 0)),
        "w_out": np.asarray(inp["w_out"], f),
        "kvnT": np.ascontiguousarray(np.asarray(inp["mla_kv_norm"], f).reshape(2, 2, 128).transpose(2, 0, 1).reshape(128, 4)),
        "qnormT": np.ascontiguousarray(np.asarray(inp["mla_q_norm"], f).reshape(2, 4, 128).transpose(2, 0, 1).reshape(128, 8)),
        "sublnT": np.ascontiguousarray(np.asarray(inp["diff_subln"], f).T),
        "lamrep": np.ascontiguousarray(np.broadcast_to(np.asarray(inp["diff_lambda"], f).reshape(1, 512), (128, 512))),
        "fnT": np.ascontiguousarray(np.asarray(inp["final_norm"], f).reshape(16, 128).T),
    }
    sh.update(host_consts())
    return sh


def host_inputs(inp, b, shared=None):
    f = np.float32
    if shared is None:
        shared = host_shared(inp)
    c = np.asarray(inp["c"][b], f)
    cc = np.asarray(inp["c_ctx"], f)
    cT = np.stack([c.reshape(16, 128).T, cc.reshape(16, 128).T], axis=-1).reshape(128, 32)
    d = dict(shared)
    d["x"] = np.ascontiguousarray(inp["x"][b], dtype=f)
    d["ctx"] = np.ascontiguousarray(inp["ctx"][b], dtype=f)
    d["cT"] = np.ascontiguousarray(cT)
    return d


def kernel(**inputs):
    nc, k = build()
    shared = host_shared(inputs)
    in_maps = [host_inputs(inputs, b, shared) for b in range(8)]
    res = run_bass_kernel_spmd(nc, in_maps, core_ids=list(range(8)))
    return np.stack([np.asarray(r["out"], dtype=np.float32) for r in res.results], axis=0)
```

```python
import math
from contextlib import ExitStack

import numpy as np
import concourse.bass as bass
import concourse.mybir as mybir
from concourse.bass_utils import run_bass_kernel_spmd

F32 = mybir.dt.float32
BF16 = mybir.dt.bfloat16
AF = mybir.ActivationFunctionType
ALU = mybir.AluOpType

D = 2048
NTOK = 2304
NCTX = 256
NLAT = 2048
DFF = 5632
EPS = 1e-6
ENGS = ("tensor", "vector", "scalar", "gpsimd", "sync")


class Buf:
    __slots__ = ("name", "last_w", "readers", "sem")

    def __init__(self, name=""):
        self.name = name
        self.last_w = None
        self.readers = []
        self.sem = None


class Op:
    __slots__ = ("eng", "fn", "deps", "is_dma", "sem", "val", "signal")

    def __init__(self, eng, fn, is_dma):
        self.eng = eng
        self.fn = fn
        self.deps = []
        self.is_dma = is_dma
        self.sem = None
        self.val = None
        self.signal = is_dma


class Prog:
    def __init__(self, nc, n_dma_sems=64):
        self.nc = nc
        self.eng_sem = {}
        self.eng_cnt = {e: 0 for e in ENGS}
        self.dma_sems = []
        self.dma_cnt = []
        self.n_dma_sems = n_dma_sems
        self.ops = []
        self.barrier_deps = {}
        self.free_dma = {}
        self.dma_last = {}
        self.n_inst = 0
        self._waited = {e: {} for e in ENGS}
        self._phase_sembufs = []
        self.per_eng_inst = {}

    def open(self, stack):
        for e in ENGS:
            self.eng_sem[e] = stack.enter_context(self.nc.semaphore("es_" + e))
        for i in range(self.n_dma_sems):
            self.dma_sems.append(stack.enter_context(self.nc.semaphore("ds%d" % i)))
            self.dma_cnt.append(0)
        half = self.n_dma_sems // 2
        self.free_dma = {True: list(range(half)), False: list(range(half, self.n_dma_sems))}

    def _track(self, op, reads, writes):
        deps = op.deps
        for b in reads:
            if b.last_w is not None:
                deps.append(b.last_w)
        for b in writes:
            if b.last_w is not None:
                deps.append(b.last_w)
            deps.extend(b.readers)
        for b in writes:
            b.last_w = op
            b.readers = []
        for b in reads:
            b.readers.append(op)
        bd = self.barrier_deps.pop(op.eng, None)
        if bd:
            deps.extend(bd)
        self.ops.append(op)

    def op(self, eng, fn, reads=(), writes=()):
        o = Op(eng, fn, False)
        self._track(o, reads, writes)
        return o

    def dma(self, eng, out, in_, reads=(), writes=(), sembuf=None):
        sw = (eng == "gpsimd")
        if sembuf.sem is None:
            sembuf.sem = self.free_dma[sw].pop()
            self._phase_sembufs.append((sembuf, sw))
        s = sembuf.sem

        def fn(e, out=out, in_=in_):
            return e.dma_start(out=out, in_=in_)

        o = Op(eng, fn, True)
        o.sem = s
        self.dma_cnt[s] += 16
        o.val = self.dma_cnt[s]
        prev = self.dma_last.get(s)
        if prev is not None:
            o.deps.append(prev)
        self.dma_last[s] = o
        if sw and SW_WINDOW:
            if len(self.sw_hist) >= SW_WINDOW:
                o.deps.append(self.sw_hist[-SW_WINDOW])
            self.sw_hist.append(o)
        self._track(o, reads, writes)
        return o

    def wait(self, eng, reads=(), writes=()):
        o = Op(eng, None, False)
        self._track(o, reads, writes)
        return o

    def begin_phase(self):
        self.sw_hist = []
        self.ops = []
        self._phase_sembufs = []
        self.dma_last = {}

    def barrier(self):
        last = {}
        dmas = {}
        for o in self.ops:
            if o.fn is None:
                continue
            if o.is_dma:
                dmas[o.sem] = o
            last[o.eng] = o
        front = list(last.values()) + list(dmas.values())
        for o in front:
            o.signal = True
        for e in ENGS:
            self.barrier_deps.setdefault(e, []).extend(front)

    def end_phase(self, bufs=()):
        for b, sw in self._phase_sembufs:
            self.free_dma[sw].append(b.sem)
            b.sem = None
        for b in bufs:
            b.last_w = None
            b.readers = []

    def emit(self):
        nc = self.nc
        ops = self.ops
        for o in ops:
            for d in o.deps:
                d.signal = True
        for o in ops:
            if not o.is_dma and o.signal and o.val is None and o.fn is not None:
                self.eng_cnt[o.eng] += 1
                o.sem = ("E", o.eng)
                o.val = self.eng_cnt[o.eng]
        per = {e: [] for e in ENGS}
        for o in ops:
            per[o.eng].append(o)

        def semh(s):
            return self.eng_sem[s[1]] if isinstance(s, tuple) else self.dma_sems[s]

        def run(eng_name, e):
            n0 = nc.n_instructions()
            try:
                run_(eng_name, e)
            finally:
                self.per_eng_inst[eng_name] = self.per_eng_inst.get(eng_name, 0) + nc.n_instructions() - n0

        def run_(eng_name, e):
            w = self._waited[eng_name]
            for o in per[eng_name]:
                need = {}
                for d in o.deps:
                    if need.get(d.sem, 0) < d.val:
                        need[d.sem] = d.val
                for s, v in need.items():
                    if w.get(s, 0) < v:
                        e.wait_ge(semh(s), v)
                        self.n_inst += 1
                        w[s] = v
                if o.fn is None:
                    continue
                ins = o.fn(e)
                self.n_inst += 1
                if o.signal:
                    ins.then_inc(semh(o.sem), 16 if o.is_dma else 1)

        with nc.Block() as block:
            @block.tensor
            def _(e):
                run("tensor", e)

            @block.vector
            def _(e):
                run("vector", e)

            @block.scalar
            def _(e):
                run("scalar", e)

            @block.gpsimd
            def _(e):
                run("gpsimd", e)

            @block.sync
            def _(e):
                run("sync", e)


_uid = [0]


def uname(name):
    _uid[0] += 1
    return "%s_u%d" % (name, _uid[0])


def sbt(st, nc, name, shape, dtype):
    return st.enter_context(nc.sbuf_tensor(uname(name), shape, dtype))


class Ring:
    def __init__(self, st, nc, name, n, shape, dtype):
        self.t = [sbt(st, nc, "%s%d" % (name, i), shape, dtype) for i in range(n)]
        self.b = [Buf("%s%d" % (name, i)) for i in range(n)]
        self.i = 0

    def next(self):
        k = self.i % len(self.t)
        self.i += 1
        return self.t[k], self.b[k]


class K:
    pass


MM_COUNT = [0]


def mmc(e, *a, **kw):
    MM_COUNT[0] += 1
    return e.matmul(*a, **kw)


def rows(ap, p=128):
    return ap.rearrange("(kc p) n -> p kc n", p=p)


def psum_next(k):
    i = k.ps_i % k.ps_n
    k.ps_i += 1
    return k.ps[i], k.psb[i]


def ph_transpose_in(k):
    nc, P = k.nc, k.P
    with ExitStack() as st:
        xin = Ring(st, nc, "tx", 2, [128, 4, D], F32)
        stg = Ring(st, nc, "ts", 2, [128, 16, 512], F32)
        P.begin_phase()
        groups = [(k.ctx, 0, 0, 256)] + [(k.x, i * 512, 256 + i * 512, 512) for i in range(4)]
        cnt = 0
        for src, r0, t0, n in groups:
            nj = n // 128
            xt, xb = xin.next()
            P.dma("sync", xt[:, 0:nj, :], src[r0:r0 + n, :].rearrange("(j p) d -> p j d", p=128),
                  writes=[xb], sembuf=xb)
            sg, sgb = stg.next()
            for kc in range(16):
                ps, pb = psum_next(k)

                def f(e, ps=ps, xt=xt, kc=kc, nj=nj):
                    for j in range(nj):
                        ins = mmc(e, ps[:, j * 128:(j + 1) * 128], lhsT=xt[:, j, kc * 128:(kc + 1) * 128],
                                       rhs=k.ident[:, :], start=True, stop=True)
                    return ins
                P.op("tensor", f, reads=[xb, k.constb], writes=[pb])
                if cnt % 2 == 0:
                    P.op("vector", lambda e, sg=sg, ps=ps, kc=kc, n=n: e.tensor_copy(out=sg[:, kc, 0:n], in_=ps[:, 0:n]),
                         reads=[pb], writes=[sgb])
                else:
                    P.op("scalar", lambda e, sg=sg, ps=ps, kc=kc, n=n: e.copy(out=sg[:, kc, 0:n], in_=ps[:, 0:n]),
                         reads=[pb], writes=[sgb])
                cnt += 1
            P.dma("sync", rows(k.xT[:, t0:t0 + n]), sg[:, :, 0:n], reads=[sgb], sembuf=sgb)
        P.barrier()
        P.emit()
        P.end_phase()


def ph_ada(k, l, ntiles=36, subs=(0, 1, 2)):
    nc, P = k.nc, k.P
    with ExitStack() as st:
        slab = Ring(st, nc, "aw", 3, [128, 16, 512], BF16)
        mrow = sbt(st, nc, "mrow", [2, 18432], F32)
        mrowb = [Buf() for _ in range(ntiles)]
        P.begin_phase()
        if l == 0:
            P.op("scalar", lambda e: e.activation(out=k.sT[:], in_=k.cT[:], func=AF.Silu), reads=[k.constb], writes=[k.sTb])
        for nt in range(ntiles):
            sl, sb = slab.next()
            P.dma("gpsimd", sl[:], rows(k.ada_w[l, :, nt * 512:(nt + 1) * 512]), writes=[sb], sembuf=sb)
            ps, pb = psum_next(k)

            def f(e, ps=ps, sl=sl):
                for kc in range(16):
                    ins = mmc(e, ps[0:2, 0:512], lhsT=k.sT[:, kc, :], rhs=sl[:, kc, :], start=(kc == 0), stop=(kc == 15))
                return ins
            P.op("tensor", f, reads=[sb, k.sTb], writes=[pb])
            if nt % 2 == 0:
                P.op("vector", lambda e, ps=ps, nt=nt: e.tensor_copy(out=mrow[0:2, nt * 512:(nt + 1) * 512], in_=ps[0:2, 0:512]),
                     reads=[pb], writes=[mrowb[nt]])
            else:
                P.op("scalar", lambda e, ps=ps, nt=nt: e.copy(out=mrow[0:2, nt * 512:(nt + 1) * 512], in_=ps[0:2, 0:512]),
                     reads=[pb], writes=[mrowb[nt]])
        ps, pb = psum_next(k)

        def f(e, ps=ps):
            for j in range(4 * ntiles):
                ins = mmc(e, ps[:, 2 * j:2 * j + 2], lhsT=mrow[0:2, j * 128:(j + 1) * 128], rhs=k.ident[0:2, 0:2],
                               start=True, stop=True)
            return ins
        P.op("tensor", f, reads=mrowb + [k.constb], writes=[pb])
        mb = k.coefb
        psv = ps[:, 0:8 * ntiles].rearrange("p (j t) -> p j t", t=2)
        for t in range(2):
            P.op("vector", lambda e, t=t: e.tensor_tensor(out=k.mT[:, l, t, 0:4 * ntiles], in0=psv[:, :, t], in1=k.adabT[:, l, 0:4 * ntiles], op=ALU.add),
                 reads=[pb, k.constb], writes=[mb])
        ada_coefs(k, l, subs)
        P.barrier()
        P.emit()
        P.end_phase()


def ada_coefs(k, l, subs=(0, 1, 2)):
        P = k.P
        mb = k.coefb
        for t in range(2):
            for s in subs:
                P.op("vector", lambda e, t=t, s=s: e.scalar_tensor_tensor(
                    out=k.coef[:, l, t, 3 * s + 0, :], in0=k.mT[:, l, t, (3 * s + 1) * 16:(3 * s + 2) * 16], scalar=1.0,
                    in1=k.gT[:, l, s, :], op0=ALU.add, op1=ALU.mult), reads=[mb, k.constb], writes=[mb])
                P.op("vector", lambda e, t=t, s=s: e.tensor_copy(
                    out=k.coef[:, l, t, 3 * s + 1, :], in_=k.mT[:, l, t, (3 * s) * 16:(3 * s + 1) * 16]), reads=[mb], writes=[mb])
                P.op("vector", lambda e, t=t, s=s: e.tensor_scalar(
                    out=k.coef[:, l, t, 3 * s + 2, :], in0=k.mT[:, l, t, (3 * s + 2) * 16:(3 * s + 3) * 16],
                    scalar1=(1.0 if s == 1 else 0.5), scalar2=None, op0=ALU.mult), reads=[mb], writes=[mb])


def ada_bg_steps(k, l, st, bank, bankb, W=512, nslab=3, col0=0, ncols=9 * D, subs=(0, 1, 2), bankT=None, bankTb=None, auto_tail=True):
    nc, P = k.nc, k.P
    slab = Ring(st, nc, "awb", nslab, [128, 16, W], BF16)
    rowr = Ring(st, nc, "arow", 2, [2, W], F32)
    mb = k.coefb
    NJ = W // 128
    pend = []
    if bankT is None:
        bankT, bankTb = bank, bankb

    def step(nt):
        sl, sb = slab.next()
        P.dma("gpsimd", sl[:], rows(k.ada_w[l, :, col0 + nt * W:col0 + (nt + 1) * W]), writes=[sb], sembuf=sb)

        def f(e):
            for kc in range(16):
                ins = mmc(e, bank[0:2, 0:W], lhsT=k.sT[:, kc, :], rhs=sl[:, kc, :], start=(kc == 0), stop=(kc == 15))
            return ins
        P.op("tensor", f, reads=[sb, k.sTb], writes=[bankb])
        rw, rwb = rowr.next()
        P.op("vector", lambda e: e.tensor_copy(out=rw[0:2, 0:W], in_=bank[0:2, 0:W]), reads=[bankb], writes=[rwb])
        if pend and auto_tail:
            pend.pop(0)()

        def tail():
            def f2(e):
                for j in range(NJ):
                    ins = mmc(e, bankT[:, 2 * j:2 * j + 2], lhsT=rw[0:2, j * 128:(j + 1) * 128], rhs=k.ident[0:2, 0:2], start=True, stop=True)
                return ins
            P.op("tensor", f2, reads=[rwb, k.constb], writes=[bankTb])
            psv = bankT[:, 0:2 * NJ].rearrange("p (j t) -> p j t", t=2)
            c0 = col0 // 128 + nt * NJ
            for t in range(2):
                P.op("vector", lambda e, t=t: e.tensor_tensor(out=k.mT[:, l, t, c0:c0 + NJ], in0=psv[:, :, t],
                                                              in1=k.adabT[:, l, c0:c0 + NJ], op=ALU.add),
                     reads=[bankTb, k.constb], writes=[mb])
        pend.append(tail)

    def finish():
        while pend:
            pend.pop(0)()
        ada_coefs(k, l, subs)
    steps = [(lambda nt=nt: step(nt)) for nt in range(ncols // W)]
    if not auto_tail:
        return steps, finish, pend
    return steps, finish


def norm_mod(k, st_rings, tiles, hT, hTb, l, s, g_only=None, dq="sync", split=False):
    P = k.P
    xin, sqr, tmpr, rsr = st_rings
    G = xin.t[0].shape[1]
    NQ = 16 // G

    def loads(t0, n):
        pend = []
        depth = len(xin.t)

        def issue(q):
            xt, xb = xin.next()
            qn_ = dq if isinstance(dq, str) else dq[q % len(dq)]
            P.dma(qn_, xt[:, :, 0:n], rows(k.xT[q * G * 128:(q + 1) * G * 128, t0:t0 + n]), writes=[xb], sembuf=xb)
            pend.append((xt, xb))
        for q in range(min(depth, NQ)):
            issue(q)
        for q in range(NQ):
            yield q, pend[q]
            if q + depth < NQ:
                issue(q + depth)

    def pass1(t0, n):
        pss, pssb = psum_next(k)
        for q, (xt, xb) in loads(t0, n):
            sq, sqb = sqr.next()
            P.op("scalar", lambda e, sq=sq, xt=xt: e.activation(out=sq[:, :, 0:n], in_=xt[:, :, 0:n], func=AF.Square),
                 reads=[xb], writes=[sqb])

            def f(e, sq=sq, q=q):
                for j in range(G):
                    ins = mmc(e, pss[:, 0:n], lhsT=k.ones_bf[:, :], rhs=sq[:, j, 0:n], start=(q == 0 and j == 0),
                              stop=(q == NQ - 1 and j == G - 1))
                return ins
            P.op("tensor", f, reads=[sqb, k.constb], writes=[pssb])
        rs, rsb = rsr.next()
        P.op("scalar", lambda e: e.activation(out=rs[:, 0:n], in_=pss[:, 0:n], func=AF.Sqrt, bias=k.epsc[:, 0:1], scale=1.0 / D),
             reads=[pssb, k.constb], writes=[rsb])
        P.op("vector", lambda e: e.reciprocal(out=rs[:, 0:n], in_=rs[:, 0:n]), reads=[rsb], writes=[rsb])
        return rs, rsb

    def pass2(ti, t0, n, typ, c0, rs, rsb):
        hTb1 = hTb[ti]
        for q, (xt, xb) in loads(t0, n):
            for j in range(G):
                kc = G * q + j
                tm, tmb = tmpr.next()
                P.op("vector", lambda e, tm=tm, xt=xt, j=j: e.tensor_tensor(
                    out=tm[:, 0:n], in0=xt[:, j, 0:n], in1=rs[:, 0:n], op=ALU.mult), reads=[xb, rsb], writes=[tmb])
                if g_only is None:
                    sc = k.coef[:, l, typ, 3 * s + 0, kc:kc + 1]
                    bi = k.coef[:, l, typ, 3 * s + 1, kc:kc + 1]
                    P.op("scalar", lambda e, tm=tm, kc=kc, sc=sc, bi=bi: e.activation(
                        out=hT[:, kc, c0:c0 + n], in_=tm[:, 0:n], func=AF.Identity, bias=bi, scale=sc),
                        reads=[tmb, k.coefb], writes=[hTb1])
                else:
                    sc = g_only[:, kc:kc + 1]
                    P.op("scalar", lambda e, tm=tm, kc=kc, sc=sc: e.activation(
                        out=hT[:, kc, c0:c0 + n], in_=tm[:, 0:n], func=AF.Identity, scale=sc),
                        reads=[tmb, k.constb], writes=[hTb1])

    if split:
        rss = [pass1(t0, n) for (t0, n, typ, c0) in tiles]
        for ti, (t0, n, typ, c0) in enumerate(tiles):
            pass2(ti, t0, n, typ, c0, *rss[ti])
    else:
        for ti, (t0, n, typ, c0) in enumerate(tiles):
            rs, rsb = pass1(t0, n)
            pass2(ti, t0, n, typ, c0, rs, rsb)


def mk_tiles(block):
    out = []
    c0 = 0
    for t0, n in block:
        out.append((t0, n, 1 if t0 < NCTX else 0, c0))
        c0 += n
    return out


FULL_BLOCKS = [[(0, 256), (256, 512)], [(768, 512), (1280, 256)], [(1536, 512), (2048, 256)]]
LAT_BLOCKS = [[(256, 512), (768, 256)], [(1024, 512), (1536, 256)], [(1792, 512)]]
XT_REGIONS = [(0, 256), (256, 512), (768, 512), (768, 256), (1024, 512), (1280, 256), (1536, 512), (1536, 256),
              (1792, 512), (2048, 256), (1280, 512)]


def ph_ffn(k, l, j, blocks, bg=False):
    nc, P = k.nc, k.P
    s = 0 if j == 0 else 2
    wg_d, wu_d, wd_d = k.ffn_wg[l, j], k.ffn_wu[l, j], k.ffn_wd[l, j]
    BT = 768
    with ExitStack() as st:
        hT = sbt(st, nc, "hT", [128, 16, BT], BF16)
        aT = sbt(st, nc, "aT", [128, 44, BT], BF16)
        wgu = Ring(st, nc, "wgu", 4, [128, 16, 256], BF16)
        wdr = Ring(st, nc, "wd", 2 if bg else 3, [128, 44, 128], BF16)
        xin = Ring(st, nc, "xin", 3, [128, 2, 512], F32)
        sqr = Ring(st, nc, "sq", 2, [128, 2, 512], BF16)
        tmpr = Ring(st, nc, "tmp", 2, [128, 512], F32)
        rsr = Ring(st, nc, "rs", 2, [128, 512], F32)
        sgr = Ring(st, nc, "sg", 2, [128, 512], F32)
        xor_ = Ring(st, nc, "xo", 3, [128, 512], F32)
        P.begin_phase()
        bg_steps, bg_finish = [], None
        if bg:
            k.ps_n = 6
            bg_steps, bg_finish = ada_bg_steps(k, l, st, k.ps[6], k.psb[6], W=256, nslab=2, col0=3 * D, ncols=6 * D, subs=(1, 2),
                                               bankT=k.ps[7], bankTb=k.psb[7])
        hTb = [Buf("hT%d" % i) for i in range(4)]
        aTbs = [Buf() for _ in range(44)]
        tl = [mk_tiles(b) for b in blocks]
        norm_mod(k, (xin, sqr, tmpr, rsr), tl[0], hT, hTb, l, s, dq=("sync", "scalar"))
        for bi, tiles in enumerate(tl):
            for fp in range(22):
                if bg_steps:
                    bg_steps.pop(0)()
                wg, wgb = wgu.next()
                P.dma("gpsimd", wg[:], rows(wg_d[:, fp * 256:(fp + 1) * 256]), writes=[wgb], sembuf=wgb)
                wu, wub = wgu.next()
                P.dma("gpsimd", wu[:], rows(wu_d[:, fp * 256:(fp + 1) * 256]), writes=[wub], sembuf=wub)
                for fs in range(2):
                    fc = fp * 2 + fs
                    for ti, (t0, n, typ, c0) in enumerate(tiles):
                        pg, pgb = psum_next(k)
                        pu, pub = psum_next(k)

                        def f(e, ps=pg, w=wg, fs=fs, c0=c0, n=n):
                            for kc in range(16):
                                ins = mmc(e, ps[:, 0:n], lhsT=w[:, kc, fs * 128:(fs + 1) * 128], rhs=hT[:, kc, c0:c0 + n],
                                               start=(kc == 0), stop=(kc == 15))
                            return ins
                        P.op("tensor", f, reads=[wgb, hTb[ti]], writes=[pgb])

                        def f2(e, ps=pu, w=wu, fs=fs, c0=c0, n=n):
                            for kc in range(16):
                                ins = mmc(e, ps[:, 0:n], lhsT=w[:, kc, fs * 128:(fs + 1) * 128], rhs=hT[:, kc, c0:c0 + n],
                                               start=(kc == 0), stop=(kc == 15))
                            return ins
                        P.op("tensor", f2, reads=[wub, hTb[ti]], writes=[pub])
                        sg, sgb = sgr.next()
                        P.op("scalar", lambda e, sg=sg, pg=pg, n=n: e.activation(out=sg[:, 0:n], in_=pg[:, 0:n], func=AF.Silu),
                             reads=[pgb], writes=[sgb])
                        P.op("vector", lambda e, sg=sg, pu=pu, fc=fc, c0=c0, n=n: e.tensor_tensor(
                            out=aT[:, fc, c0:c0 + n], in0=sg[:, 0:n], in1=pu[:, 0:n], op=ALU.mult),
                            reads=[sgb, pub], writes=[aTbs[fc]])
            for dc in range(16):
                if dc == 3 and bi + 1 < len(tl):
                    norm_mod(k, (xin, sqr, tmpr, rsr), tl[bi + 1], hT, hTb, l, s, dq="scalar", split=True)
                wd, wdb = wdr.next()
                P.dma("gpsimd", wd[:, 0:22, :], rows(wd_d[0:2816, dc * 128:(dc + 1) * 128]), writes=[wdb], sembuf=wdb)
                P.dma("gpsimd", wd[:, 22:44, :], rows(wd_d[2816:5632, dc * 128:(dc + 1) * 128]), writes=[wdb], sembuf=wdb)
                for (t0, n, typ, c0) in tiles:
                    ps, pb = psum_next(k)

                    def f(e, ps=ps, wd=wd, c0=c0, n=n):
                        for kc in range(44):
                            ins = mmc(e, ps[:, 0:n], lhsT=wd[:, kc, :], rhs=aT[:, kc, c0:c0 + n], start=(kc == 0), stop=(kc == 43))
                        return ins
                    P.op("tensor", f, reads=[wdb] + aTbs, writes=[pb])
                    xo, xob = xor_.next()
                    P.dma("sync", xo[:, 0:n], k.xT[dc * 128:(dc + 1) * 128, t0:t0 + n], writes=[xob], sembuf=xob)
                    gcol = k.coef[:, l, typ, 3 * s + 2, dc:dc + 1]
                    P.op("vector", lambda e, xo=xo, ps=ps, gcol=gcol, n=n: e.scalar_tensor_tensor(
                        out=xo[:, 0:n], in0=ps[:, 0:n], scalar=gcol, in1=xo[:, 0:n], op0=ALU.mult, op1=ALU.add),
                        reads=[pb, xob, k.coefb], writes=[xob])
                    P.dma("sync", k.xT[dc * 128:(dc + 1) * 128, t0:t0 + n], xo[:, 0:n], reads=[xob], sembuf=xob)
        while bg_steps:
            bg_steps.pop(0)()
        if bg_finish is not None:
            bg_finish()
        k.ps_n = 8
        P.barrier()
        P.emit()
        P.end_phase()


ALL_TILES = [(0, 256), (256, 512), (768, 512), (1280, 512), (1792, 512)]
LAT_TILES = ALL_TILES[1:]
MLA_SCALE = 192.0 ** -0.5
DIFF_SCALE = 0.125


def evac(P, idx, out, in_, reads, writes):
    if idx % 2 == 0:
        P.op("vector", lambda e: e.tensor_copy(out=out, in_=in_), reads=reads, writes=writes)
    else:
        P.op("scalar", lambda e: e.copy(out=out, in_=in_), reads=reads, writes=writes)


def mm_group(P, ps_ap, pairs, reads, writes):
    def f(e):
        n = len(pairs)
        for i, (a, b) in enumerate(pairs):
            ins = mmc(e, ps_ap, lhsT=a, rhs=b, start=(i == 0), stop=(i == n - 1))
        return ins
    return P.op("tensor", f, reads=reads, writes=writes)


def rope_combine(k, P, rings, psA, pAb, psB, pBb, t0, n, dst_ap, stg, stgb):
    t1r, t2r = rings
    t1, t1b = t1r.next()
    t2, t2b = t2r.next()
    P.op("vector", lambda e: e.tensor_tensor(out=t1[:, 0:n], in0=psA[:, 0:n], in1=k.cosT[:, t0:t0 + n], op=ALU.mult),
         reads=[pAb, k.ropeb], writes=[t1b])
    P.op("vector", lambda e: e.tensor_tensor(out=t2[:, 0:n], in0=psB[:, 0:n], in1=k.sinT[:, t0:t0 + n], op=ALU.mult),
         reads=[pBb, k.ropeb], writes=[t2b])
    P.op("vector", lambda e: e.tensor_tensor(out=stg[:, 0:n], in0=t1[:, 0:n], in1=t2[:, 0:n], op=ALU.add),
         reads=[t1b, t2b], writes=[stgb])
    P.dma("sync", dst_ap, stg[:, 0:n], reads=[stgb], sembuf=stgb)


def ph_proj(k, l):
    nc, P = k.nc, k.P
    last = (l == 1)
    w = k.w_in2[l]
    with ExitStack() as st:
        hT = sbt(st, nc, "hTp", [128, 16, NTOK], BF16)
        slabr = Ring(st, nc, "wsl", 4, [128, 16, 256], BF16)
        dvs = sbt(st, nc, "dvs", [128, 16, 512], BF16)
        dvsb = Buf()
        xin = Ring(st, nc, "xin", 3, [128, 2, 512], F32)
        sqr = Ring(st, nc, "sq", 2, [128, 2, 512], BF16)
        xar = Ring(st, nc, "xa", 2, [128, 512], BF16)
        rperm = sbt(st, nc, "rperm", [128, 128], BF16)
        rpb = Buf()
        tmpr = Ring(st, nc, "tmp", 3, [128, 512], F32)
        rsr = Ring(st, nc, "rs", 5, [128, 512], F32)
        rs2r = Ring(st, nc, "rs2", 1, [128, 512], F32)
        rawr = Ring(st, nc, "raw", 5, [128, 512], F32)
        sq2r = Ring(st, nc, "sq2", 4, [128, 512], BF16)
        t1r = Ring(st, nc, "t1", 2, [128, 512], F32)
        t2r = Ring(st, nc, "t2", 2, [128, 512], F32)
        stgr = Ring(st, nc, "stg", 2, [128, 512], BF16)
        k.cosT = sbt(st, nc, "cosT", [128, NTOK], F32)
        k.sinT = sbt(st, nc, "sinT", [128, NTOK], F32)
        k.ropeb = Buf()
        P.begin_phase()
        P.dma("sync", k.cosT[:], k.cosT_d, writes=[k.ropeb], sembuf=k.ropeb)
        P.dma("sync", k.sinT[:], k.sinT_d, writes=[k.ropeb], sembuf=k.ropeb)
        P.dma("gpsimd", rperm[:], k.rperm_d, writes=[rpb], sembuf=rpb)
        hTb = [Buf() for _ in range(5)]
        tiles = mk_tiles(ALL_TILES)
        norm_mod(k, (xin, sqr, tmpr, rsr), tiles, hT, hTb, l, 1, split=True, dq=("sync", "scalar"))
        qtiles = [(ti, t) for ti, t in enumerate(tiles) if not (last and ti == 0)]
        atiles = list(enumerate(tiles))
        slabs = {}

        def get_slab(si):
            sl, sb_ = slabr.next()
            P.dma("gpsimd", sl[:], rows(w[:, si * 256:(si + 1) * 256]), writes=[sb_], sembuf=sb_)
            return sl, sb_

        def proj_mm(sl, sb_, sub, ti, c0, n):
            ps, pb = psum_next(k)
            mm_group(P, ps[:, 0:n], [(sl[:, kc, sub * 128:(sub + 1) * 128], hT[:, kc, c0:c0 + n]) for kc in range(16)],
                     reads=[sb_, hTb[ti]], writes=[pb])
            return ps, pb

        def normed(slab_ids, nch, gcol, dst, tl):
            sls = [get_slab(si) for si in slab_ids]
            for ti, (t0, n, typ, c0) in tl:
                raws = []
                sqs = []
                for ch in range(nch):
                    sl, sb_ = sls[ch // 2]
                    ps, pb = proj_mm(sl, sb_, ch % 2, ti, c0, n)
                    rw, rwb = rawr.next()
                    P.op("vector", lambda e, rw=rw, ps=ps, n=n: e.tensor_copy(out=rw[:, 0:n], in_=ps[:, 0:n]), reads=[pb], writes=[rwb])
                    sq, sqb = sq2r.next()
                    P.op("scalar", lambda e, sq=sq, rw=rw, n=n: e.activation(out=sq[:, 0:n], in_=rw[:, 0:n], func=AF.Square),
                         reads=[rwb], writes=[sqb])
                    raws.append((rw, rwb))
                    sqs.append((sq, sqb))
                pss, pssb = psum_next(k)
                for ch in range(nch):
                    sq, sqb = sqs[ch]
                    P.op("tensor", lambda e, pss=pss, sq=sq, n=n, ch=ch: mmc(e, pss[:, 0:n], lhsT=k.ones_bf[:, :], rhs=sq[:, 0:n],
                                                                                 start=(ch == 0), stop=(ch == nch - 1)),
                         reads=[sqb, k.constb], writes=[pssb])
                rs, rsb = rs2r.next()
                P.op("scalar", lambda e, rs=rs, pss=pss, n=n: e.activation(out=rs[:, 0:n], in_=pss[:, 0:n], func=AF.Sqrt,
                                                                          bias=k.epsc[:, 0:1], scale=1.0 / (128 * nch)),
                     reads=[pssb, k.constb], writes=[rsb])
                P.op("vector", lambda e, rs=rs, n=n: e.reciprocal(out=rs[:, 0:n], in_=rs[:, 0:n]), reads=[rsb], writes=[rsb])
                for ch in range(nch):
                    rw, rwb = raws[ch]
                    P.op("vector", lambda e, rw=rw, rs=rs, n=n: e.tensor_tensor(out=rw[:, 0:n], in0=rw[:, 0:n], in1=rs[:, 0:n], op=ALU.mult),
                         reads=[rwb, rsb], writes=[rwb])
                    sg, sgb = stgr.next()
                    P.op("scalar", lambda e, sg=sg, rw=rw, ch=ch, n=n: e.activation(out=sg[:, 0:n], in_=rw[:, 0:n], func=AF.Identity,
                                                                                   scale=gcol[:, ch:ch + 1]),
                         reads=[rwb, k.constb], writes=[sgb])
                    P.dma("sync", dst[ch * 128:(ch + 1) * 128, t0:t0 + n], sg[:, 0:n], reads=[sgb], sembuf=sgb)

        pend_rope = []

        def roped(main_slab, main_sub, dst_rows, tl):
            for ti, (t0, n, typ, c0) in tl:
                psA, pAb = proj_mm(main_slab[0], main_slab[1], main_sub, ti, c0, n)
                xa, xab = xar.next()
                P.op("scalar", lambda e, xa=xa, psA=psA, n=n: e.copy(out=xa[:, 0:n], in_=psA[:, 0:n]), reads=[pAb], writes=[xab])
                if pend_rope:
                    pend_rope.pop(0)()

                def tail(psA=psA, pAb=pAb, xa=xa, xab=xab, t0=t0, n=n):
                    psB, pBb = psum_next(k)
                    P.op("tensor", lambda e: mmc(e, psB[:, 0:n], lhsT=rperm[:, :], rhs=xa[:, 0:n], start=True, stop=True),
                         reads=[xab, rpb], writes=[pBb])
                    sg, sgb = stgr.next()
                    rope_combine(k, P, (t1r, t2r), psA, pAb, psB, pBb, t0, n, dst_rows[:, t0:t0 + n], sg, sgb)
                pend_rope.append(tail)

        normed([0], 2, k.kvnT[:, l, :], k.ckvnT, atiles)
        s1 = get_slab(1)
        roped(s1, 0, k.krT, atiles)
        for hp in range(2):
            sm = get_slab(2 + hp)
            for sub in range(2):
                h = hp * 2 + sub
                roped(sm, sub, k.dkT[h * 128:(h + 1) * 128, :], atiles)
        while pend_rope:
            pend_rope.pop(0)()
        normed([6, 7], 4, k.qnormT[:, l, :], k.cqnT, qtiles)
        for hp in range(2):
            sm = get_slab(8 + hp)
            for sub in range(2):
                h = hp * 2 + sub
                roped(sm, sub, k.dqT[h * 128:(h + 1) * 128, :], qtiles)
        while pend_rope:
            pend_rope.pop(0)()
        cnt = 0
        for hp in range(2):
            sm = get_slab(12 + hp)
            for sub in range(2):
                g = hp * 2 + sub
                for ti, (t0, n, typ, c0) in qtiles:
                    ps, pb = proj_mm(sm[0], sm[1], sub, ti, c0, n)
                    sg, sgb = stgr.next()
                    evac(P, cnt, sg[:, 0:n], ps[:, 0:n], [pb], [sgb])
                    cnt += 1
                    P.dma("sync", k.uT[g * 128:(g + 1) * 128, t0:t0 + n], sg[:, 0:n], reads=[sgb], sembuf=sgb)
        P.dma("gpsimd", dvs[:], rows(w[:, 28 * 128:32 * 128]), writes=[dvsb], sembuf=dvsb)
        for tc in range(18):
            ti = 0 if tc < 2 else 1 + (tc - 2) // 4
            ps, pb = psum_next(k)
            mm_group(P, ps[:, :], [(hT[:, kc, tc * 128:(tc + 1) * 128], dvs[:, kc, :]) for kc in range(16)],
                     reads=[dvsb, hTb[ti]], writes=[pb])
            sg, sgb = stgr.next()
            evac(P, tc, sg[:, :], ps[:, :], [pb], [sgb])
            P.dma("sync", k.DVs[tc * 128:(tc + 1) * 128, :], sg[:, :], reads=[sgb], sembuf=sgb)
        P.barrier()
        P.emit()
        P.end_phase()


def ph_proj2(k, l):
    nc, P = k.nc, k.P
    last = (l == 1)
    with ExitStack() as st:
        ckv = sbt(st, nc, "ckv", [128, 2, NTOK], BF16)
        cq = sbt(st, nc, "cq", [128, 4, NTOK], BF16)
        wukv = sbt(st, nc, "wukv", [128, 2, 2048], BF16)
        wuq = sbt(st, nc, "wuq", [128, 4, 2048], BF16)
        t1r = Ring(st, nc, "t1", 2, [128, 512], F32)
        t2r = Ring(st, nc, "t2", 2, [128, 512], F32)
        stgr = Ring(st, nc, "stg", 4, [128, 512], BF16)
        k.cosT = sbt(st, nc, "cosT", [128, NTOK], F32)
        k.sinT = sbt(st, nc, "sinT", [128, NTOK], F32)
        k.ropeb = Buf()
        ckvb, cqb, wkb, wqb = Buf(), Buf(), Buf(), Buf()
        P.begin_phase()
        P.dma("sync", ckv[:], rows(k.ckvnT[:, :]), writes=[ckvb], sembuf=ckvb)
        P.dma("sync", cq[:], rows(k.cqnT[:, :]), writes=[cqb], sembuf=cqb)
        P.dma("sync", k.cosT[:], k.cosT_d, writes=[k.ropeb], sembuf=k.ropeb)
        P.dma("sync", k.sinT[:], k.sinT_d, writes=[k.ropeb], sembuf=k.ropeb)
        P.dma("gpsimd", wukv[:], rows(k.w_ukv2[l]), writes=[wkb], sembuf=wkb)
        P.dma("gpsimd", wuq[:], rows(k.w_uq2[l]), writes=[wqb], sembuf=wqb)
        tiles = mk_tiles(ALL_TILES)
        qtiles = tiles[1:] if last else tiles
        cnt = 0
        for h in range(8):
            for (t0, n, typ, c0) in tiles:
                ps, pb = psum_next(k)
                mm_group(P, ps[:, 0:n], [(wukv[:, kc, h * 128:(h + 1) * 128], ckv[:, kc, t0:t0 + n]) for kc in range(2)],
                         reads=[wkb, ckvb], writes=[pb])
                sg, sgb = stgr.next()
                evac(P, cnt, sg[:, 0:n], ps[:, 0:n], [pb], [sgb])
                cnt += 1
                P.dma("sync", k.knT[h * 128:(h + 1) * 128, t0:t0 + n], sg[:, 0:n], reads=[sgb], sembuf=sgb)
        for tc in range(18):
            for half in range(2):
                ps, pb = psum_next(k)
                mm_group(P, ps[:, :], [(ckv[:, kc, tc * 128:(tc + 1) * 128], wukv[:, kc, 1024 + half * 512:1024 + (half + 1) * 512])
                                      for kc in range(2)], reads=[wkb, ckvb], writes=[pb])
                sg, sgb = stgr.next()
                evac(P, cnt, sg[:, :], ps[:, :], [pb], [sgb])
                cnt += 1
                P.dma("sync", k.Vs[tc * 128:(tc + 1) * 128, half * 512:(half + 1) * 512], sg[:, :], reads=[sgb], sembuf=sgb)
        for h in range(8):
            for (t0, n, typ, c0) in qtiles:
                ps, pb = psum_next(k)
                mm_group(P, ps[:, 0:n], [(wuq[:, kc, h * 128:(h + 1) * 128], cq[:, kc, t0:t0 + n]) for kc in range(4)],
                         reads=[wqb, cqb], writes=[pb])
                sg, sgb = stgr.next()
                evac(P, cnt, sg[:, 0:n], ps[:, 0:n], [pb], [sgb])
                cnt += 1
                P.dma("sync", k.qnT[h * 128:(h + 1) * 128, t0:t0 + n], sg[:, 0:n], reads=[sgb], sembuf=sgb)
        for r in range(4):
            for (t0, n, typ, c0) in qtiles:
                psA, pAb = psum_next(k)
                mm_group(P, psA[:, 0:n], [(wuq[:, kc, 1024 + r * 128:1024 + (r + 1) * 128], cq[:, kc, t0:t0 + n]) for kc in range(4)],
                         reads=[wqb, cqb], writes=[pAb])
                psB, pBb = psum_next(k)
                mm_group(P, psB[:, 0:n], [(wuq[:, kc, 1536 + r * 128:1536 + (r + 1) * 128], cq[:, kc, t0:t0 + n]) for kc in range(4)],
                         reads=[wqb, cqb], writes=[pBb])
                sg, sgb = stgr.next()
                rope_combine(k, P, (t1r, t2r), psA, pAb, psB, pBb, t0, n, k.qrT[r * 128:(r + 1) * 128, t0:t0 + n], sg, sgb)
        P.barrier()
        P.emit()
        P.end_phase()


def ph_mla(k, l, bg_layer=None):
    nc, P = k.nc, k.P
    with ExitStack() as st:
        kn = sbt(st, nc, "kn", [128, 8, NTOK], BF16)
        V = sbt(st, nc, "V", [128, 18, 1024], BF16)
        kr = sbt(st, nc, "kr", [128, NTOK], BF16)
        qnr = Ring(st, nc, "qn", 2, [128, 8, 512], BF16)
        qrr = Ring(st, nc, "qr", 2, [128, 8, 512], BF16)
        Er = Ring(st, nc, "E", 4, [128, 512], BF16)
        rzr = Ring(st, nc, "rz", 2, [128, 512], F32)
        stgr = Ring(st, nc, "stg", 3, [128, 512], BF16)
        knb, Vb, krb = Buf(), Buf(), Buf()
        P.begin_phase()
        bg_steps, bg_finish, bg_pend = [], None, []
        if bg_layer is not None:
            bg_steps, bg_finish, bg_pend = ada_bg_steps(k, bg_layer, st, k.ps[3], k.psb[3], auto_tail=False)
        NS = 3 if bg_layer is not None else 4
        for t_, b_ in zip(qrr.t, qrr.b):
            P.op("vector", lambda e, t_=t_: e.memset(t_[:], 0.0), writes=[b_])
        P.dma("sync", kn[:], rows(k.knT[:, :]), writes=[knb], sembuf=knb)
        P.dma("sync", kr[:], k.krT[:, :], writes=[krb], sembuf=krb)
        qt = [((256 + i * 512), 512, list(range(18))) for i in range(4)]
        if l == 0:
            qt = [(0, 256, [0, 1])] + qt
        si = 0
        hh = 0
        def load_q(t0, n):
            qn, qnb = qnr.next()
            qr, qrb = qrr.next()
            P.dma("sync", qn[:, :, 0:n], rows(k.qnT[:, t0:t0 + n]), writes=[qnb], sembuf=qnb)
            for h_ in range(8):
                hp_ = 64 * (h_ % 2)
                r0_ = (h_ // 2) * 128 + hp_
                P.dma("sync", qr[hp_:hp_ + 64, h_, 0:n], k.qrT[r0_:r0_ + 64, t0:t0 + n], writes=[qrb], sembuf=qrb)
            return qn, qnb, qr, qrb
        qloads = [load_q(qt[0][0], qt[0][1])]
        P.dma("sync", V[:], rows(k.Vs[:, :]), writes=[Vb], sembuf=Vb)
        for qi, (t0, n, keys) in enumerate(qt):
            if qi + 1 < len(qt):
                qloads.append(load_q(qt[qi + 1][0], qt[qi + 1][1]))
            qn, qnb, qr, qrb = qloads[qi]

            def do_head(h, t0, n, keys, qn, qnb, qr, qrb, O, Ob, Z, Zb):
                nonlocal si
                hp = 64 * (h % 2)

                def s_mm(j):
                    nonlocal si
                    S, Sb = k.ps[si % NS], k.psb[si % NS]
                    si += 1
                    mm_group(P, S[:, 0:n], [(kn[:, h, j * 128:(j + 1) * 128], qn[:, h, 0:n]),
                                            (kr[:, j * 128:(j + 1) * 128], qr[:, h, 0:n])],
                             reads=[knb, krb, qnb, qrb], writes=[Sb])
                    return S, Sb
                cur = s_mm(keys[0])
                for ji, j in enumerate(keys):
                    S, Sb = cur
                    if ji + 1 < len(keys):
                        cur = s_mm(keys[ji + 1])
                    E, Eb = Er.next()
                    P.op("scalar", lambda e, E=E, S=S: e.activation(out=E[:, 0:n], in_=S[:, 0:n], func=AF.Exp, scale=MLA_SCALE),
                         reads=[Sb], writes=[Eb])

                    def pv(e, E=E, j=j, ji=ji, O=O, Z=Z):
                        mmc(e, O[:, 0:n], lhsT=V[:, j, h * 128:(h + 1) * 128], rhs=E[:, 0:n], start=(ji == 0), stop=(ji == len(keys) - 1))
                        return mmc(e, Z[:, 0:n], lhsT=k.ones_bf[:, :], rhs=E[:, 0:n], start=(ji == 0), stop=(ji == len(keys) - 1))
                    P.op("tensor", pv, reads=[Eb, Vb, k.constb], writes=[Ob, Zb])
                    if bg_pend and (ji == 9 or ji == len(keys) - 1):
                        bg_pend.pop(0)()
                rz, rzb = rzr.next()
                P.op("vector", lambda e, rz=rz, Z=Z: e.reciprocal(out=rz[:, 0:n], in_=Z[:, 0:n]), reads=[Zb], writes=[rzb])
                sg, sgb = stgr.next()
                P.op("vector", lambda e, sg=sg, O=O, rz=rz: e.tensor_tensor(out=sg[:, 0:n], in0=O[:, 0:n], in1=rz[:, 0:n], op=ALU.mult),
                     reads=[Ob, rzb], writes=[sgb])
                P.dma("sync", k.oT[h * 128:(h + 1) * 128, t0:t0 + n], sg[:, 0:n], reads=[sgb], sembuf=sgb)
            for h in range(8):
                if bg_steps:
                    bg_steps.pop(0)()
                do_head(h, t0, n, keys, qn, qnb, qr, qrb, k.ps[4 + hh % 2], k.psb[4 + hh % 2], k.ps[6 + hh % 2], k.psb[6 + hh % 2])
                hh += 1
        while bg_steps:
            bg_steps.pop(0)()
            while bg_pend:
                bg_pend.pop(0)()
        if bg_finish is not None:
            bg_finish()
        P.barrier()
        P.emit()
        P.end_phase()


def ph_diff(k, l):
    nc, P = k.nc, k.P
    lam_init = 0.8 - 0.6 * math.exp(-0.3 * l)
    cfac = 1.0 - lam_init
    with ExitStack() as st:
        dk = sbt(st, nc, "dk", [128, 4, NTOK], BF16)
        DV = sbt(st, nc, "DV", [128, 18, 512], BF16)
        dqr = Ring(st, nc, "dq", 2, [128, 4, 512], BF16)
        Er = Ring(st, nc, "E", 6, [128, 512], BF16)
        fr = Ring(st, nc, "f", 12, [128, 512], F32)
        sqr = Ring(st, nc, "sq", 2, [128, 512], BF16)
        stgr = Ring(st, nc, "stg", 3, [128, 512], BF16)
        pending = []
        lt = sbt(st, nc, "lt", [128, 2, 64], F32)
        lr = sbt(st, nc, "lr", [128, 4], F32)
        dkb, DVb, lb = Buf(), Buf(), Buf()
        P.begin_phase()
        P.dma("sync", dk[:], rows(k.dkT[:, :]), writes=[dkb], sembuf=dkb)
        lam = k.lamrep
        P.op("vector", lambda e: e.tensor_tensor(out=lt[:, 0, :], in0=lam[:, l, 0, :], in1=lam[:, l, 1, :], op=ALU.mult), reads=[k.constb], writes=[lb])
        P.op("vector", lambda e: e.tensor_tensor(out=lt[:, 1, :], in0=lam[:, l, 2, :], in1=lam[:, l, 3, :], op=ALU.mult), reads=[k.constb, lb], writes=[lb])
        P.op("vector", lambda e: e.reduce_sum(out=lr[:, 0:2], in_=lt[:, :, :], axis=mybir.AxisListType.X), reads=[lb], writes=[lb])
        P.op("scalar", lambda e: e.activation(out=lr[:, 0:2], in_=lr[:, 0:2], func=AF.Exp), reads=[lb], writes=[lb])
        P.op("vector", lambda e: e.tensor_tensor(out=lr[:, 2:3], in0=lr[:, 1:2], in1=lr[:, 0:1], op=ALU.subtract), reads=[lb], writes=[lb])
        P.op("vector", lambda e: e.tensor_scalar(out=lr[:, 2:3], in0=lr[:, 2:3], scalar1=-lam_init, scalar2=None, op0=ALU.add), reads=[lb], writes=[lb])
        P.op("vector", lambda e: e.memset(lr[:, 3:4], EPS / (cfac * cfac)), reads=[lb], writes=[lb])
        neglam = lr[:, 2:3]
        epsc2 = lr[:, 3:4]
        qt = [((256 + i * 512), 512, list(range(18))) for i in range(4)]
        if l == 0:
            qt = [(0, 256, [0, 1])] + qt
        si = 0
        def load_q(t0, n):
            dq, dqb = dqr.next()
            P.dma("sync", dq[:, :, 0:n], rows(k.dqT[:, t0:t0 + n]), writes=[dqb], sembuf=dqb)
            return dq, dqb
        qloads = [load_q(qt[0][0], qt[0][1])]
        P.dma("sync", DV[:], rows(k.DVs[:, :]), writes=[DVb], sembuf=DVb)
        for qi, (t0, n, keys) in enumerate(qt):
            if qi + 1 < len(qt):
                qloads.append(load_q(qt[qi + 1][0], qt[qi + 1][1]))
            dq, dqb = qloads[qi]

            def do_head(h, t0, n, keys, dq, dqb):
                nonlocal si
                acc = [(k.ps[4 + i], k.psb[4 + i]) for i in range(4)]

                def s_mm(j):
                    nonlocal si
                    S1, S1b = k.ps[(2 * si) % 4], k.psb[(2 * si) % 4]
                    S2, S2b = k.ps[(2 * si) % 4 + 1], k.psb[(2 * si) % 4 + 1]
                    si += 1
                    P.op("tensor", lambda e: mmc(e, S1[:, 0:n], lhsT=dk[0:64, h, j * 128:(j + 1) * 128], rhs=dq[0:64, h, 0:n], start=True, stop=True),
                         reads=[dkb, dqb], writes=[S1b])
                    P.op("tensor", lambda e: mmc(e, S2[:, 0:n], lhsT=dk[64:128, h, j * 128:(j + 1) * 128], rhs=dq[64:128, h, 0:n], start=True, stop=True),
                         reads=[dkb, dqb], writes=[S2b])
                    return (S1, S1b, S2, S2b)
                cur = s_mm(keys[0])
                for ji, j in enumerate(keys):
                    S1, S1b, S2, S2b = cur
                    if ji + 1 < len(keys):
                        cur = s_mm(keys[ji + 1])
                    first, lastj = (ji == 0), (ji == len(keys) - 1)
                    for (S, Sb, (O, Ob), (Z, Zb)) in ((S1, S1b, acc[0], acc[1]), (S2, S2b, acc[2], acc[3])):
                        E, Eb = Er.next()
                        P.op("scalar", lambda e, E=E, S=S: e.activation(out=E[:, 0:n], in_=S[:, 0:n], func=AF.Exp, scale=DIFF_SCALE),
                             reads=[Sb], writes=[Eb])

                        def pv(e, E=E, O=O, Z=Z, j=j, first=first, lastj=lastj):
                            mmc(e, O[:, 0:n], lhsT=DV[:, j, h * 128:(h + 1) * 128], rhs=E[:, 0:n], start=first, stop=lastj)
                            return mmc(e, Z[:, 0:n], lhsT=k.ones_bf[:, :], rhs=E[:, 0:n], start=first, stop=lastj)
                        P.op("tensor", pv, reads=[Eb, DVb, k.constb], writes=[Ob, Zb])
                    if pending and (ji == 8 or lastj):
                        pending.pop(0)()
                cps = []
                for (A, Ab) in acc:
                    c, cb = fr.next()
                    P.op("vector", lambda e, c=c, A=A: e.tensor_copy(out=c[:, 0:n], in_=A[:, 0:n]), reads=[Ab], writes=[cb])
                    cps.append((c, cb))
                (o1, o1b), (z1, z1b), (o2, o2b), (z2, z2b) = cps
                P.op("vector", lambda e: e.reciprocal(out=z1[:, 0:n], in_=z1[:, 0:n]), reads=[z1b], writes=[z1b])
                P.op("vector", lambda e: e.reciprocal(out=z2[:, 0:n], in_=z2[:, 0:n]), reads=[z2b], writes=[z2b])
                P.op("vector", lambda e: e.tensor_tensor(out=o1[:, 0:n], in0=o1[:, 0:n], in1=z1[:, 0:n], op=ALU.mult), reads=[o1b, z1b], writes=[o1b])
                P.op("vector", lambda e: e.tensor_tensor(out=o2[:, 0:n], in0=o2[:, 0:n], in1=z2[:, 0:n], op=ALU.mult), reads=[o2b, z2b], writes=[o2b])
                o, ob = fr.next()
                P.op("vector", lambda e: e.scalar_tensor_tensor(out=o[:, 0:n], in0=o2[:, 0:n], scalar=neglam, in1=o1[:, 0:n],
                                                                op0=ALU.mult, op1=ALU.add), reads=[o1b, o2b, lb], writes=[ob])
                sq, sqb = sqr.next()
                P.op("vector", lambda e: e.tensor_tensor(out=sq[:, 0:n], in0=o[:, 0:n], in1=o[:, 0:n], op=ALU.mult), reads=[ob], writes=[sqb])

                def finish():
                    nonlocal si
                    SS, SSb = k.ps[(2 * si) % 4], k.psb[(2 * si) % 4]
                    P.op("tensor", lambda e: mmc(e, SS[:, 0:n], lhsT=k.ones_bf[:, :], rhs=sq[:, 0:n], start=True, stop=True),
                         reads=[sqb, k.constb], writes=[SSb])
                    rs, rsb = fr.next()
                    P.op("scalar", lambda e: e.activation(out=rs[:, 0:n], in_=SS[:, 0:n], func=AF.Ln, bias=epsc2,
                                                          scale=1.0 / (128.0 * cfac * cfac)), reads=[SSb, lb], writes=[rsb])
                    P.op("scalar", lambda e: e.activation(out=rs[:, 0:n], in_=rs[:, 0:n], func=AF.Exp, scale=-0.5), reads=[rsb], writes=[rsb])
                    P.op("vector", lambda e: e.tensor_tensor(out=o[:, 0:n], in0=o[:, 0:n], in1=rs[:, 0:n], op=ALU.mult),
                         reads=[ob, rsb], writes=[ob])
                    sg, sgb = stgr.next()
                    P.op("scalar", lambda e: e.activation(out=sg[:, 0:n], in_=o[:, 0:n], func=AF.Identity, scale=k.sublnT[:, l:l + 1]),
                         reads=[ob, k.constb], writes=[sgb])
                    P.dma("sync", k.oT[1024 + h * 128:1024 + (h + 1) * 128, t0:t0 + n], sg[:, 0:n], reads=[sgb], sembuf=sgb)
                pending.append(finish)
            for h in range(4):
                do_head(h, t0, n, keys, dq, dqb)
        while pending:
            pending.pop(0)()
        P.barrier()
        P.emit()
        P.end_phase()


def ph_fourier(k, l):
    nc, P = k.nc, k.P
    with ExitStack() as st:
        u = sbt(st, nc, "u", [128, 4, NTOK], BF16)
        ccsc = sbt(st, nc, "ccsc", [128, 256], BF16)
        AB = sbt(st, nc, "AB", [128, 16, 4, 256], BF16)
        ABc = sbt(st, nc, "ABc", [128, 2, 4, 256], BF16)
        csr = Ring(st, nc, "cs", 2, [128, 16, 512], BF16)
        ssr = Ring(st, nc, "ss", 2, [128, 16, 512], BF16)
        c256 = sbt(st, nc, "c256", [128, 2, 256], BF16)
        s256 = sbt(st, nc, "s256", [128, 2, 256], BF16)
        stgr = Ring(st, nc, "stg", 3, [128, 512], BF16)
        ub, cb_, c2b, s2b = Buf(), Buf(), Buf(), Buf()
        ABb = [Buf() for _ in range(16)]
        ABcb = [Buf() for _ in range(2)]
        P.begin_phase()
        P.dma("sync", u[:], rows(k.uT[:, :]), writes=[ub], sembuf=ub)
        P.dma("gpsimd", ccsc[:], k.ccsc_d, writes=[cb_], sembuf=cb_)
        cnt = 0

        def step1(tok0, ABt, ABtb, tc):
            nonlocal cnt
            for gp in range(2):
                ps, pb = psum_next(k)

                def f(e, ps=ps, gp=gp):
                    for gs in range(2):
                        ins = mmc(e, ps[:, gs * 256:(gs + 1) * 256], lhsT=u[:, gp * 2 + gs, tok0:tok0 + 128], rhs=ccsc[:, :], start=True, stop=True)
                    return ins
                P.op("tensor", f, reads=[ub, cb_], writes=[pb])
                evac(P, cnt, ABt[:, tc, gp * 2:gp * 2 + 2, :].rearrange("p a b -> p (a b)"), ps[:, :], [pb], [ABtb[tc]])
                cnt += 1
        for tc in range(16):
            step1(256 + tc * 128, AB, ABb, tc)
        for stile in range(4):
            cs, csb = csr.next()
            ss, ssb = ssr.next()
            P.dma("gpsimd", cs[:], rows(k.cs2048[:, stile * 512:(stile + 1) * 512]), writes=[csb], sembuf=csb)
            P.dma("gpsimd", ss[:], rows(k.ss2048[:, stile * 512:(stile + 1) * 512]), writes=[ssb], sembuf=ssb)
            for g in range(4):
                ps, pb = psum_next(k)
                pairs = []
                for sc in range(16):
                    pairs.append((AB[:, sc, g, 0:128], cs[:, sc, :]))
                    pairs.append((AB[:, sc, g, 128:256], ss[:, sc, :]))
                mm_group(P, ps[:, :], pairs, reads=ABb + [csb, ssb], writes=[pb])
                sg, sgb = stgr.next()
                evac(P, cnt, sg[:, :], ps[:, :], [pb], [sgb])
                cnt += 1
                P.dma("sync", k.oT[1536 + g * 128:1536 + (g + 1) * 128, 256 + stile * 512:256 + (stile + 1) * 512], sg[:, :],
                      reads=[sgb], sembuf=sgb)
        if l == 0:
            P.dma("gpsimd", c256[:], rows(k.c256_d), writes=[c2b], sembuf=c2b)
            P.dma("gpsimd", s256[:], rows(k.s256_d), writes=[s2b], sembuf=s2b)
            for tc in range(2):
                step1(tc * 128, ABc, ABcb, tc)
            for g in range(4):
                ps, pb = psum_next(k)
                pairs = []
                for sc in range(2):
                    pairs.append((ABc[:, sc, g, 0:128], c256[:, sc, :]))
                    pairs.append((ABc[:, sc, g, 128:256], s256[:, sc, :]))
                mm_group(P, ps[:, 0:256], pairs, reads=ABcb + [c2b, s2b], writes=[pb])
                sg, sgb = stgr.next()
                evac(P, cnt, sg[:, 0:256], ps[:, 0:256], [pb], [sgb])
                cnt += 1
                P.dma("sync", k.oT[1536 + g * 128:1536 + (g + 1) * 128, 0:256], sg[:, 0:256], reads=[sgb], sembuf=sgb)
        P.barrier()
        P.emit()
        P.end_phase()


def ph_wout(k, l):
    nc, P = k.nc, k.P
    tiles = mk_tiles(ALL_TILES)
    if l == 1:
        tiles = tiles[1:]
    with ExitStack() as st:
        o = sbt(st, nc, "o", [128, 16, NTOK], BF16)
        wor = Ring(st, nc, "wo", 3, [128, 16, 256], BF16)
        xor_ = Ring(st, nc, "xo", 3, [128, 512], F32)
        ob = [Buf() for _ in range(16)]
        P.begin_phase()
        for kc in range(16):
            P.dma("sync", o[:, kc, :], k.oT[kc * 128:(kc + 1) * 128, :], writes=[ob[kc]], sembuf=ob[kc])
        for dp in range(8):
            wo, wob = wor.next()
            P.dma("gpsimd", wo[:], rows(k.w_out[l, :, dp * 256:(dp + 1) * 256]), writes=[wob], sembuf=wob)
            for ds in range(2):
                dc = dp * 2 + ds
                for (t0, n, typ, c0) in tiles:
                    ps, pb = psum_next(k)
                    mm_group(P, ps[:, 0:n], [(wo[:, kc, ds * 128:(ds + 1) * 128], o[:, kc, t0:t0 + n]) for kc in range(16)],
                             reads=[wob] + ob, writes=[pb])
                    xo, xob = xor_.next()
                    P.dma("scalar", xo[:, 0:n], k.xT[dc * 128:(dc + 1) * 128, t0:t0 + n], writes=[xob], sembuf=xob)
                    gcol = k.coef[:, l, typ, 5, dc:dc + 1]
                    P.op("vector", lambda e, xo=xo, ps=ps, gcol=gcol, n=n: e.scalar_tensor_tensor(
                        out=xo[:, 0:n], in0=ps[:, 0:n], scalar=gcol, in1=xo[:, 0:n], op0=ALU.mult, op1=ALU.add),
                        reads=[pb, xob, k.coefb], writes=[xob])
                    P.dma("sync", k.xT[dc * 128:(dc + 1) * 128, t0:t0 + n], xo[:, 0:n], reads=[xob], sembuf=xob)
        P.barrier()
        P.emit()
        P.end_phase()


def ph_final(k):
    nc, P = k.nc, k.P
    with ExitStack() as st:
        yr = Ring(st, nc, "y", 2, [128, 16, 512], F32)
        osr = Ring(st, nc, "os", 2, [128, D], F32)
        xin = Ring(st, nc, "xin", 3, [128, 2, 512], F32)
        sqr = Ring(st, nc, "sq", 2, [128, 2, 512], BF16)
        tmpr = Ring(st, nc, "tmp", 2, [128, 512], F32)
        rsr = Ring(st, nc, "rs", 2, [128, 512], F32)
        P.begin_phase()
        cnt = 0
        for (t0, n) in LAT_TILES:
            y, yb = yr.next()
            norm_mod(k, (xin, sqr, tmpr, rsr), [(t0, n, 0, 0)], y, [yb], 0, 0, g_only=k.fnT)
            for tc in range(4):
                os_, osb = osr.next()
                for q in range(4):
                    ps, pb = psum_next(k)

                    def f(e, ps=ps, y=y, tc=tc, q=q):
                        for j in range(4):
                            kc = q * 4 + j
                            ins = mmc(e, ps[:, j * 128:(j + 1) * 128], lhsT=y[:, kc, tc * 128:(tc + 1) * 128], rhs=k.ident[:, :],
                                           start=True, stop=True)
                        return ins
                    P.op("tensor", f, reads=[yb, k.constb], writes=[pb])
                    evac(P, cnt, os_[:, q * 512:(q + 1) * 512], ps[:, :], [pb], [osb])
                    cnt += 1
                r0 = t0 - NCTX + tc * 128
                P.dma("sync", k.out[r0:r0 + 128, :], os_[:, :], reads=[osb], sembuf=osb)
        P.barrier()
        P.emit()
        P.end_phase()


SCRATCH = [("ckvnT", [256, NTOK]), ("cqnT", [512, NTOK]), ("knT", [1024, NTOK]), ("Vs", [NTOK, 1024]), ("krT", [128, NTOK]),
           ("dkT", [512, NTOK]), ("DVs", [NTOK, 512]), ("qnT", [1024, NTOK]), ("qrT", [512, NTOK]), ("dqT", [512, NTOK]),
           ("uT", [512, NTOK]), ("oT", [2048, NTOK])]


NB2 = 3
SW_WINDOW = 5
DBG_MODE = ""


def build(stop_after=None, debug=False, skip=()):
    nc = bass.Bass("TRN2", target_bir_lowering=False)
    k = K()
    k.nc = nc
    MM_COUNT[0] = 0

    def din(name, shape, dt=F32):
        return nc.dram_tensor(name, list(shape), dt, kind="ExternalInput").ap()

    def dscratch(name, shape, dt):
        if debug:
            return nc.dram_tensor(name, list(shape), dt, kind="ExternalOutput").ap()
        return nc.dram_tensor(name, list(shape), dt).ap()

    k.x = din("x", [NLAT, D])
    k.ctx = din("ctx", [NCTX, D])
    cT_d = din("cT", [128, 32])
    k.ada_w = din("ada_w", [2, D, 9 * D])
    adabT_d = din("adabT", [128, 2 * 144])
    gT_d = din("gT", [128, 2 * 3 * 16])
    k.ffn_wg = din("ffn_wg", [2, 2, D, DFF])
    k.ffn_wu = din("ffn_wu", [2, 2, D, DFF])
    k.ffn_wd = din("ffn_wd", [2, 2, DFF, D])
    ident_d = din("ident", [128, 128])
    k.w_in2 = din("w_in2", [2, D, 4096])
    k.w_ukv2 = din("w_ukv2", [2, 256, 2048])
    k.w_uq2 = din("w_uq2", [2, 512, 2048])
    k.w_out = din("w_out", [2, D, D])
    k.cosT_d = din("cosT", [128, NTOK])
    k.sinT_d = din("sinT", [128, NTOK])
    kvnT_d = din("kvnT", [128, 4])
    qnormT_d = din("qnormT", [128, 8])
    sublnT_d = din("sublnT", [128, 2])
    lamrep_d = din("lamrep", [128, 512])
    fnT_d = din("fnT", [128, 16])
    k.rperm_d = din("rperm", [128, 128])
    k.ccsc_d = din("ccsc", [128, 256])
    k.cs2048 = din("cs2048", [2048, 2048])
    k.ss2048 = din("ss2048", [2048, 2048])
    k.c256_d = din("c256", [256, 256])
    k.s256_d = din("s256", [256, 256])
    k.out = nc.dram_tensor("out", [NLAT, D], F32, kind="ExternalOutput").ap()
    k.xT = dscratch("xT", [D, NTOK], F32)
    for name, shape in SCRATCH:
        setattr(k, name, dscratch(name, shape, BF16))

    with ExitStack() as st:
        P = Prog(nc)
        P.open(st)
        k.P = P
        k.ps = [st.enter_context(nc.psum_tensor("ps%d" % i, [128, 512], F32)) for i in range(8)]
        k.psb = [Buf("ps%d" % i) for i in range(8)]
        k.ps_i = 0
        k.ps_n = 8
        sb = lambda name, shape, dt: st.enter_context(nc.sbuf_tensor("s_" + name, shape, dt))
        k.ident = sb("ident", [128, 128], F32)
        k.ones_bf = sb("ones_bf", [128, 128], BF16)
        k.epsc = sb("epsc", [128, 1], F32)
        k.cT = sb("cT", [128, 16, 2], F32)
        k.sT = sb("sT", [128, 16, 2], BF16)
        k.adabT = sb("adabT", [128, 2, 144], F32)
        k.gT = sb("gT", [128, 2, 3, 16], F32)
        k.mT = sb("mT", [128, 2, 2, 144], F32)
        k.coef = sb("coef", [128, 2, 2, 9, 16], F32)
        k.kvnT = sb("kvnT", [128, 2, 2], F32)
        k.qnormT = sb("qnormT", [128, 2, 4], F32)
        k.sublnT = sb("sublnT", [128, 2], F32)
        k.lamrep = sb("lamrep", [128, 2, 4, 64], F32)
        k.fnT = sb("fnT", [128, 16], F32)
        k.constb = Buf("const")
        k.sTb = Buf("sT")
        k.coefb = Buf("coef")

        P.begin_phase()
        cb = k.constb
        P.dma("sync", k.ident[:], ident_d, writes=[cb], sembuf=cb)
        P.dma("sync", k.cT[:].rearrange("p a b -> p (a b)"), cT_d, writes=[cb], sembuf=cb)
        P.dma("sync", k.adabT[:].rearrange("p a b -> p (a b)"), adabT_d, writes=[cb], sembuf=cb)
        P.dma("sync", k.gT[:].rearrange("p a b c -> p (a b c)"), gT_d, writes=[cb], sembuf=cb)
        P.dma("sync", k.kvnT[:].rearrange("p a b -> p (a b)"), kvnT_d, writes=[cb], sembuf=cb)
        P.dma("sync", k.qnormT[:].rearrange("p a b -> p (a b)"), qnormT_d, writes=[cb], sembuf=cb)
        P.dma("sync", k.sublnT[:], sublnT_d, writes=[cb], sembuf=cb)
        P.dma("sync", k.lamrep[:].rearrange("p a b c -> p (a b c)"), lamrep_d, writes=[cb], sembuf=cb)
        P.dma("sync", k.fnT[:], fnT_d, writes=[cb], sembuf=cb)
        P.op("vector", lambda e: e.memset(k.ones_bf[:], 1.0), writes=[cb])
        P.op("vector", lambda e: e.memset(k.epsc[:], EPS), writes=[cb])
        P.barrier()
        P.emit()
        P.end_phase()

        stages = [("tin", lambda: ph_transpose_in(k))]
        for l in range(2):
            if l == 0:
                stages.append(("ada%d" % l, lambda l=l: ph_ada(k, l, ntiles=12, subs=(0,))))
            stages.append(("ffn%d_0" % l, lambda l=l: ph_ffn(k, l, 0, FULL_BLOCKS, bg=(l == 0))))
            stages.append(("proj%d" % l, lambda l=l: ph_proj(k, l)))
            stages.append(("proj2_%d" % l, lambda l=l: ph_proj2(k, l)))
            stages.append(("mla%d" % l, lambda l=l: ph_mla(k, l, bg_layer=(1 if l == 0 else None))))
            stages.append(("diff%d" % l, lambda l=l: ph_diff(k, l)))
            stages.append(("four%d" % l, lambda l=l: ph_fourier(k, l)))
            stages.append(("wout%d" % l, lambda l=l: ph_wout(k, l)))
            stages.append(("ffn%d_1" % l, lambda l=l: ph_ffn(k, l, 1, (FULL_BLOCKS if l == 0 else LAT_BLOCKS)[:NB2])))
        stages.append(("final", lambda: ph_final(k)))
        k.stage_mm = []
        for name, fn in stages:
            if name not in skip:
                fn()
            k.stage_mm.append((name, MM_COUNT[0]))
            if stop_after == name:
                break
        P.begin_phase()
        P.wait("sync")
        P.emit()
        P.end_phase()
        k.n_inst = P.n_inst
    return nc, k


_CONST_CACHE = {}


def host_consts():
    if _CONST_CACHE:
        return _CONST_CACHE
    f = np.float32
    s = np.arange(NLAT)
    pos = [s // 64, s % 64]
    cosT = np.ones((64, NTOK), np.float64)
    sinT = np.zeros((64, NTOK), np.float64)
    for i in range(64):
        jj = i % 16
        inv = np.float32(10000.0) ** np.float32(-2.0 * jj / 32.0)
        ang = pos[i // 32].astype(np.float32) * np.float32(inv)
        cosT[i, NCTX:] = np.cos(ang.astype(np.float64))
        sn = np.sin(ang.astype(np.float64))
        sinT[i, NCTX:] = -sn if (i % 32) < 16 else sn
    _CONST_CACHE["cosT"] = np.ascontiguousarray(np.concatenate([cosT, cosT], 0).astype(f))
    _CONST_CACHE["sinT"] = np.ascontiguousarray(np.concatenate([sinT, sinT], 0).astype(f))
    c = np.arange(128)
    ang = 2 * np.pi * ((c[:, None] * c[None, :]) % 128) / 128.0
    _CONST_CACHE["ccsc"] = np.ascontiguousarray(np.concatenate([np.cos(ang), -np.sin(ang)], 1).astype(f))
    s2 = np.arange(2048, dtype=np.int64)
    ang = 2 * np.pi * ((s2[:, None] * s2[None, :]) % 2048) / 2048.0
    _CONST_CACHE["cs2048"] = np.ascontiguousarray((np.cos(ang) / 512.0).astype(f))
    _CONST_CACHE["ss2048"] = np.ascontiguousarray((np.sin(ang) / 512.0).astype(f))
    s3 = np.arange(256, dtype=np.int64)
    ang = 2 * np.pi * ((s3[:, None] * s3[None, :]) % 256) / 256.0
    nrm = math.sqrt(256.0 * 128.0)
    _CONST_CACHE["c256"] = np.ascontiguousarray((np.cos(ang) / nrm).astype(f))
    _CONST_CACHE["s256"] = np.ascontiguousarray((np.sin(ang) / nrm).astype(f))
    _CONST_CACHE["ident"] = np.eye(128, dtype=f)
    rp = np.zeros((128, 128), f)
    rp[np.arange(128) ^ 16, np.arange(128)] = 1.0
    _CONST_CACHE["rperm"] = rp
    return _CONST_CACHE


def swap64(a):
    m = a.shape[1] // 64
    idx = (np.arange(m)[:, None] * 64 + (np.arange(64) ^ 16)[None, :]).reshape(-1)
    return a[:, idx]


def host_shared(inp):
    f = np.float32
    w2 = []
    for l in range(2):
        wi = np.asarray(inp["w_in"][l], f)
        ckv, kr, dk, dv = wi[:, 0:256], wi[:, 256:320], wi[:, 320:832], wi[:, 832:1344]
        cq, dq, u = wi[:, 1344:1856], wi[:, 1856:2368], wi[:, 2368:2880]
        w2.append(np.concatenate([ckv, kr, kr, swap64(kr), swap64(kr), dk, swap64(dk), cq, dq, swap64(dq), u, dv], axis=1))
    ukv = []
    uq = []
    for l in range(2):
        r = np.asarray(inp["mla_w_ukv"][l], f).reshape(256, 8, 256)
        ukv.append(np.concatenate([r[:, :, :128].reshape(256, 1024), r[:, :, 128:].reshape(256, 1024)], 1))
        r = np.asarray(inp["mla_w_uq"][l], f).reshape(512, 8, 192)
        rope = r[:, :, 128:].reshape(512, 512)
        uq.append(np.concatenate([r[:, :, :128].reshape(512, 1024), rope, swap64(rope)], 1))
    g = np.asarray(inp["norm_g"], f)
    sh = {
        "ada_w": np.asarray(inp["ada_w"], f),
        "adabT": np.ascontiguousarray(np.stack([np.asarray(inp["ada_b"][l], f).reshape(144, 128).T for l in range(2)], axis=1).reshape(128, 288)),
        "gT": np.ascontiguousarray(g.reshape(2, 3, 16, 128).transpose(3, 0, 1, 2).reshape(128, 96)),
        "ffn_wg": np.asarray(inp["ffn_wg"], f),
        "ffn_wu": np.asarray(inp["ffn_wu"], f),
        "ffn_wd": np.asarray(inp["ffn_wd"], f),
        "w_in2": np.ascontiguousarray(np.stack(w2, 0)),
        "w_ukv2": np.ascontiguousarray(np.stack(ukv, 0)),
        "w_uq2": np.ascontiguousarray(np.stack(uq, 0)),
        "w_out": np.asarray(inp["w_out"], f),
        "kvnT": np.ascontiguousarray(np.asarray(inp["mla_kv_norm"], f).reshape(2, 2, 128).transpose(2, 0, 1).reshape(128, 4)),
        "qnormT": np.ascontiguousarray(np.asarray(inp["mla_q_norm"], f).reshape(2, 4, 128).transpose(2, 0, 1).reshape(128, 8)),
        "sublnT": np.ascontiguousarray(np.asarray(inp["diff_subln"], f).T),
        "lamrep": np.ascontiguousarray(np.broadcast_to(np.asarray(inp["diff_lambda"], f).reshape(1, 512), (128, 512))),
        "fnT": np.ascontiguousarray(np.asarray(inp["final_norm"], f).reshape(16, 128).T),
    }
    sh.update(host_consts())
    return sh


def host_inputs(inp, b, shared=None):
    f = np.float32
    if shared is None:
        shared = host_shared(inp)
    c = np.asarray(inp["c"][b], f)
    cc = np.asarray(inp["c_ctx"], f)
    cT = np.stack([c.reshape(16, 128).T, cc.reshape(16, 128).T], axis=-1).reshape(128, 32)
    d = dict(shared)
    d["x"] = np.ascontiguousarray(inp["x"][b], dtype=f)
    d["ctx"] = np.ascontiguousarray(inp["ctx"][b], dtype=f)
    d["cT"] = np.ascontiguousarray(cT)
    return d


def kernel(**inputs):
    nc, k = build()
    shared = host_shared(inputs)
    in_maps = [host_inputs(inputs, b, shared) for b in range(8)]
    res = run_bass_kernel_spmd(nc, in_maps, core_ids=list(range(8)))
    return np.stack([np.asarray(r["out"], dtype=np.float32) for r in res.results], axis=0)
```

```python
import math
from contextlib import ExitStack

import numpy as np
import concourse.bass as bass
import concourse.mybir as mybir
from concourse.bass_utils import run_bass_kernel_spmd

F32 = mybir.dt.float32
BF16 = mybir.dt.bfloat16
AF = mybir.ActivationFunctionType
ALU = mybir.AluOpType

D = 2048
NTOK = 2304
NCTX = 256
NLAT = 2048
DFF = 5632
EPS = 1e-6
ENGS = ("tensor", "vector", "scalar", "gpsimd", "sync")


class Buf:
    __slots__ = ("name", "last_w", "readers", "sem")

    def __init__(self, name=""):
        self.name = name
        self.last_w = None
        self.readers = []
        self.sem = None


class Op:
    __slots__ = ("eng", "fn", "deps", "is_dma", "sem", "val", "signal")

    def __init__(self, eng, fn, is_dma):
        self.eng = eng
        self.fn = fn
        self.deps = []
        self.is_dma = is_dma
        self.sem = None
        self.val = None
        self.signal = is_dma


class Prog:
    def __init__(self, nc, n_dma_sems=64):
        self.nc = nc
        self.eng_sem = {}
        self.eng_cnt = {e: 0 for e in ENGS}
        self.dma_sems = []
        self.dma_cnt = []
        self.n_dma_sems = n_dma_sems
        self.ops = []
        self.barrier_deps = {}
        self.free_dma = {}
        self.dma_last = {}
        self.n_inst = 0
        self._waited = {e: {} for e in ENGS}
        self._phase_sembufs = []
        self.per_eng_inst = {}

    def open(self, stack):
        for e in ENGS:
            self.eng_sem[e] = stack.enter_context(self.nc.semaphore("es_" + e))
        for i in range(self.n_dma_sems):
            self.dma_sems.append(stack.enter_context(self.nc.semaphore("ds%d" % i)))
            self.dma_cnt.append(0)
        half = self.n_dma_sems // 2
        self.free_dma = {True: list(range(half)), False: list(range(half, self.n_dma_sems))}

    def _track(self, op, reads, writes):
        deps = op.deps
        for b in reads:
            if b.last_w is not None:
                deps.append(b.last_w)
        for b in writes:
            if b.last_w is not None:
                deps.append(b.last_w)
            deps.extend(b.readers)
        for b in writes:
            b.last_w = op
            b.readers = []
        for b in reads:
            b.readers.append(op)
        bd = self.barrier_deps.pop(op.eng, None)
        if bd:
            deps.extend(bd)
        self.ops.append(op)

    def op(self, eng, fn, reads=(), writes=()):
        o = Op(eng, fn, False)
        self._track(o, reads, writes)
        return o

    def dma(self, eng, out, in_, reads=(), writes=(), sembuf=None):
        sw = (eng == "gpsimd")
        if sembuf.sem is None:
            sembuf.sem = self.free_dma[sw].pop()
            self._phase_sembufs.append((sembuf, sw))
        s = sembuf.sem

        def fn(e, out=out, in_=in_):
            return e.dma_start(out=out, in_=in_)

        o = Op(eng, fn, True)
        o.sem = s
        self.dma_cnt[s] += 16
        o.val = self.dma_cnt[s]
        prev = self.dma_last.get(s)
        if prev is not None:
            o.deps.append(prev)
        self.dma_last[s] = o
        if sw and SW_WINDOW:
            if len(self.sw_hist) >= SW_WINDOW:
                o.deps.append(self.sw_hist[-SW_WINDOW])
            self.sw_hist.append(o)
        self._track(o, reads, writes)
        return o

    def wait(self, eng, reads=(), writes=()):
        o = Op(eng, None, False)
        self._track(o, reads, writes)
        return o

    def begin_phase(self):
        self.sw_hist = []
        self.ops = []
        self._phase_sembufs = []
        self.dma_last = {}

    def barrier(self):
        last = {}
        dmas = {}
        for o in self.ops:
            if o.fn is None:
                continue
            if o.is_dma:
                dmas[o.sem] = o
            last[o.eng] = o
        front = list(last.values()) + list(dmas.values())
        for o in front:
            o.signal = True
        for e in ENGS:
            self.barrier_deps.setdefault(e, []).extend(front)

    def end_phase(self, bufs=()):
        for b, sw in self._phase_sembufs:
            self.free_dma[sw].append(b.sem)
            b.sem = None
        for b in bufs:
            b.last_w = None
            b.readers = []

    def emit(self):
        nc = self.nc
        ops = self.ops
        for o in ops:
            for d in o.deps:
                d.signal = True
        for o in ops:
            if not o.is_dma and o.signal and o.val is None and o.fn is not None:
                self.eng_cnt[o.eng] += 1
                o.sem = ("E", o.eng)
                o.val = self.eng_cnt[o.eng]
        per = {e: [] for e in ENGS}
        for o in ops:
            per[o.eng].append(o)

        def semh(s):
            return self.eng_sem[s[1]] if isinstance(s, tuple) else self.dma_sems[s]

        def run(eng_name, e):
            n0 = nc.n_instructions()
            try:
                run_(eng_name, e)
            finally:
                self.per_eng_inst[eng_name] = self.per_eng_inst.get(eng_name, 0) + nc.n_instructions() - n0

        def run_(eng_name, e):
            w = self._waited[eng_name]
            for o in per[eng_name]:
                need = {}
                for d in o.deps:
                    if need.get(d.sem, 0) < d.val:
                        need[d.sem] = d.val
                for s, v in need.items():
                    if w.get(s, 0) < v:
                        e.wait_ge(semh(s), v)
                        self.n_inst += 1
                        w[s] = v
                if o.fn is None:
                    continue
                ins = o.fn(e)
                self.n_inst += 1
                if o.signal:
                    ins.then_inc(semh(o.sem), 16 if o.is_dma else 1)

        with nc.Block() as block:
            @block.tensor
            def _(e):
                run("tensor", e)

            @block.vector
            def _(e):
                run("vector", e)

            @block.scalar
            def _(e):
                run("scalar", e)

            @block.gpsimd
            def _(e):
                run("gpsimd", e)

            @block.sync
            def _(e):
                run("sync", e)


_uid = [0]


def uname(name):
    _uid[0] += 1
    return "%s_u%d" % (name, _uid[0])


def sbt(st, nc, name, shape, dtype):
    return st.enter_context(nc.sbuf_tensor(uname(name), shape, dtype))


class Ring:
    def __init__(self, st, nc, name, n, shape, dtype):
        self.t = [sbt(st, nc, "%s%d" % (name, i), shape, dtype) for i in range(n)]
        self.b = [Buf("%s%d" % (name, i)) for i in range(n)]
        self.i = 0

    def next(self):
        k = self.i % len(self.t)
        self.i += 1
        return self.t[k], self.b[k]


class K:
    pass


MM_COUNT = [0]


def mmc(e, *a, **kw):
    MM_COUNT[0] += 1
    return e.matmul(*a, **kw)


def rows(ap, p=128):
    return ap.rearrange("(kc p) n -> p kc n", p=p)


def psum_next(k):
    i = k.ps_i % k.ps_n
    k.ps_i += 1
    return k.ps[i], k.psb[i]


def ph_transpose_in(k):
    nc, P = k.nc, k.P
    with ExitStack() as st:
        xin = Ring(st, nc, "tx", 2, [128, 4, D], F32)
        stg = Ring(st, nc, "ts", 2, [128, 16, 512], F32)
        P.begin_phase()
        groups = [(k.ctx, 0, 0, 256)] + [(k.x, i * 512, 256 + i * 512, 512) for i in range(4)]
        cnt = 0
        for src, r0, t0, n in groups:
            nj = n // 128
            xt, xb = xin.next()
            P.dma("sync", xt[:, 0:nj, :], src[r0:r0 + n, :].rearrange("(j p) d -> p j d", p=128),
                  writes=[xb], sembuf=xb)
            sg, sgb = stg.next()
            for kc in range(16):
                ps, pb = psum_next(k)

                def f(e, ps=ps, xt=xt, kc=kc, nj=nj):
                    for j in range(nj):
                        ins = mmc(e, ps[:, j * 128:(j + 1) * 128], lhsT=xt[:, j, kc * 128:(kc + 1) * 128],
                                       rhs=k.ident[:, :], start=True, stop=True)
                    return ins
                P.op("tensor", f, reads=[xb, k.constb], writes=[pb])
                if cnt % 2 == 0:
                    P.op("vector", lambda e, sg=sg, ps=ps, kc=kc, n=n: e.tensor_copy(out=sg[:, kc, 0:n], in_=ps[:, 0:n]),
                         reads=[pb], writes=[sgb])
                else:
                    P.op("scalar", lambda e, sg=sg, ps=ps, kc=kc, n=n: e.copy(out=sg[:, kc, 0:n], in_=ps[:, 0:n]),
                         reads=[pb], writes=[sgb])
                cnt += 1
            P.dma("sync", rows(k.xT[:, t0:t0 + n]), sg[:, :, 0:n], reads=[sgb], sembuf=sgb)
        P.barrier()
        P.emit()
        P.end_phase()


def ph_ada(k, l, ntiles=36, subs=(0, 1, 2)):
    nc, P = k.nc, k.P
    with ExitStack() as st:
        slab = Ring(st, nc, "aw", 3, [128, 16, 512], BF16)
        mrow = sbt(st, nc, "mrow", [2, 18432], F32)
        mrowb = [Buf() for _ in range(ntiles)]
        P.begin_phase()
        if l == 0:
            P.op("scalar", lambda e: e.activation(out=k.sT[:], in_=k.cT[:], func=AF.Silu), reads=[k.constb], writes=[k.sTb])
        for nt in range(ntiles):
            sl, sb = slab.next()
            P.dma("gpsimd", sl[:], rows(k.ada_w[l, :, nt * 512:(nt + 1) * 512]), writes=[sb], sembuf=sb)
            ps, pb = psum_next(k)

            def f(e, ps=ps, sl=sl):
                for kc in range(16):
                    ins = mmc(e, ps[0:2, 0:512], lhsT=k.sT[:, kc, :], rhs=sl[:, kc, :], start=(kc == 0), stop=(kc == 15))
                return ins
            P.op("tensor", f, reads=[sb, k.sTb], writes=[pb])
            if nt % 2 == 0:
                P.op("vector", lambda e, ps=ps, nt=nt: e.tensor_copy(out=mrow[0:2, nt * 512:(nt + 1) * 512], in_=ps[0:2, 0:512]),
                     reads=[pb], writes=[mrowb[nt]])
            else:
                P.op("scalar", lambda e, ps=ps, nt=nt: e.copy(out=mrow[0:2, nt * 512:(nt + 1) * 512], in_=ps[0:2, 0:512]),
                     reads=[pb], writes=[mrowb[nt]])
        ps, pb = psum_next(k)

        def f(e, ps=ps):
            for j in range(4 * ntiles):
                ins = mmc(e, ps[:, 2 * j:2 * j + 2], lhsT=mrow[0:2, j * 128:(j + 1) * 128], rhs=k.ident[0:2, 0:2],
                               start=True, stop=True)
            return ins
        P.op("tensor", f, reads=mrowb + [k.constb], writes=[pb])
        mb = k.coefb
        psv = ps[:, 0:8 * ntiles].rearrange("p (j t) -> p j t", t=2)
        for t in range(2):
            P.op("vector", lambda e, t=t: e.tensor_tensor(out=k.mT[:, l, t, 0:4 * ntiles], in0=psv[:, :, t], in1=k.adabT[:, l, 0:4 * ntiles], op=ALU.add),
                 reads=[pb, k.constb], writes=[mb])
        ada_coefs(k, l, subs)
        P.barrier()
        P.emit()
        P.end_phase()


def ada_coefs(k, l, subs=(0, 1, 2)):
        P = k.P
        mb = k.coefb
        for t in range(2):
            for s in subs:
                P.op("vector", lambda e, t=t, s=s: e.scalar_tensor_tensor(
                    out=k.coef[:, l, t, 3 * s + 0, :], in0=k.mT[:, l, t, (3 * s + 1) * 16:(3 * s + 2) * 16], scalar=1.0,
                    in1=k.gT[:, l, s, :], op0=ALU.add, op1=ALU.mult), reads=[mb, k.constb], writes=[mb])
                P.op("vector", lambda e, t=t, s=s: e.tensor_copy(
                    out=k.coef[:, l, t, 3 * s + 1, :], in_=k.mT[:, l, t, (3 * s) * 16:(3 * s + 1) * 16]), reads=[mb], writes=[mb])
                P.op("vector", lambda e, t=t, s=s: e.tensor_scalar(
                    out=k.coef[:, l, t, 3 * s + 2, :], in0=k.mT[:, l, t, (3 * s + 2) * 16:(3 * s + 3) * 16],
                    scalar1=(1.0 if s == 1 else 0.5), scalar2=None, op0=ALU.mult), reads=[mb], writes=[mb])


def ada_bg_steps(k, l, st, bank, bankb, W=512, nslab=3, col0=0, ncols=9 * D, subs=(0, 1, 2), bankT=None, bankTb=None, auto_tail=True):
    nc, P = k.nc, k.P
    slab = Ring(st, nc, "awb", nslab, [128, 16, W], BF16)
    rowr = Ring(st, nc, "arow", 2, [2, W], F32)
    mb = k.coefb
    NJ = W // 128
    pend = []
    if bankT is None:
        bankT, bankTb = bank, bankb

    def step(nt):
        sl, sb = slab.next()
        P.dma("gpsimd", sl[:], rows(k.ada_w[l, :, col0 + nt * W:col0 + (nt + 1) * W]), writes=[sb], sembuf=sb)

        def f(e):
            for kc in range(16):
                ins = mmc(e, bank[0:2, 0:W], lhsT=k.sT[:, kc, :], rhs=sl[:, kc, :], start=(kc == 0), stop=(kc == 15))
            return ins
        P.op("tensor", f, reads=[sb, k.sTb], writes=[bankb])
        rw, rwb = rowr.next()
        P.op("vector", lambda e: e.tensor_copy(out=rw[0:2, 0:W], in_=bank[0:2, 0:W]), reads=[bankb], writes=[rwb])
        if pend and auto_tail:
            pend.pop(0)()

        def tail():
            def f2(e):
                for j in range(NJ):
                    ins = mmc(e, bankT[:, 2 * j:2 * j + 2], lhsT=rw[0:2, j * 128:(j + 1) * 128], rhs=k.ident[0:2, 0:2], start=True, stop=True)
                return ins
            P.op("tensor", f2, reads=[rwb, k.constb], writes=[bankTb])
            psv = bankT[:, 0:2 * NJ].rearrange("p (j t) -> p j t", t=2)
            c0 = col0 // 128 + nt * NJ
            for t in range(2):
                P.op("vector", lambda e, t=t: e.tensor_tensor(out=k.mT[:, l, t, c0:c0 + NJ], in0=psv[:, :, t],
                                                              in1=k.adabT[:, l, c0:c0 + NJ], op=ALU.add),
                     reads=[bankTb, k.constb], writes=[mb])
        pend.append(tail)

    def finish():
        while pend:
            pend.pop(0)()
        ada_coefs(k, l, subs)
    steps = [(lambda nt=nt: step(nt)) for nt in range(ncols // W)]
    if not auto_tail:
        return steps, finish, pend
    return steps, finish


def norm_mod(k, st_rings, tiles, hT, hTb, l, s, g_only=None, dq="sync", split=False):
    P = k.P
    xin, sqr, tmpr, rsr = st_rings
    G = xin.t[0].shape[1]
    NQ = 16 // G

    def loads(t0, n):
        pend = []
        depth = len(xin.t)

        def issue(q):
            xt, xb = xin.next()
            qn_ = dq if isinstance(dq, str) else dq[q % len(dq)]
            P.dma(qn_, xt[:, :, 0:n], rows(k.xT[q * G * 128:(q + 1) * G * 128, t0:t0 + n]), writes=[xb], sembuf=xb)
            pend.append((xt, xb))
        for q in range(min(depth, NQ)):
            issue(q)
        for q in range(NQ):
            yield q, pend[q]
            if q + depth < NQ:
                issue(q + depth)

    def pass1(t0, n):
        pss, pssb = psum_next(k)
        for q, (xt, xb) in loads(t0, n):
            sq, sqb = sqr.next()
            P.op("scalar", lambda e, sq=sq, xt=xt: e.activation(out=sq[:, :, 0:n], in_=xt[:, :, 0:n], func=AF.Square),
                 reads=[xb], writes=[sqb])

            def f(e, sq=sq, q=q):
                for j in range(G):
                    ins = mmc(e, pss[:, 0:n], lhsT=k.ones_bf[:, :], rhs=sq[:, j, 0:n], start=(q == 0 and j == 0),
                              stop=(q == NQ - 1 and j == G - 1))
                return ins
            P.op("tensor", f, reads=[sqb, k.constb], writes=[pssb])
        rs, rsb = rsr.next()
        P.op("scalar", lambda e: e.activation(out=rs[:, 0:n], in_=pss[:, 0:n], func=AF.Sqrt, bias=k.epsc[:, 0:1], scale=1.0 / D),
             reads=[pssb, k.constb], writes=[rsb])
        P.op("vector", lambda e: e.reciprocal(out=rs[:, 0:n], in_=rs[:, 0:n]), reads=[rsb], writes=[rsb])
        return rs, rsb

    def pass2(ti, t0, n, typ, c0, rs, rsb):
        hTb1 = hTb[ti]
        for q, (xt, xb) in loads(t0, n):
            for j in range(G):
                kc = G * q + j
                tm, tmb = tmpr.next()
                P.op("vector", lambda e, tm=tm, xt=xt, j=j: e.tensor_tensor(
                    out=tm[:, 0:n], in0=xt[:, j, 0:n], in1=rs[:, 0:n], op=ALU.mult), reads=[xb, rsb], writes=[tmb])
                if g_only is None:
                    sc = k.coef[:, l, typ, 3 * s + 0, kc:kc + 1]
                    bi = k.coef[:, l, typ, 3 * s + 1, kc:kc + 1]
                    P.op("scalar", lambda e, tm=tm, kc=kc, sc=sc, bi=bi: e.activation(
                        out=hT[:, kc, c0:c0 + n], in_=tm[:, 0:n], func=AF.Identity, bias=bi, scale=sc),
                        reads=[tmb, k.coefb], writes=[hTb1])
                else:
                    sc = g_only[:, kc:kc + 1]
                    P.op("scalar", lambda e, tm=tm, kc=kc, sc=sc: e.activation(
                        out=hT[:, kc, c0:c0 + n], in_=tm[:, 0:n], func=AF.Identity, scale=sc),
                        reads=[tmb, k.constb], writes=[hTb1])

    if split:
        rss = [pass1(t0, n) for (t0, n, typ, c0) in tiles]
        for ti, (t0, n, typ, c0) in enumerate(tiles):
            pass2(ti, t0, n, typ, c0, *rss[ti])
    else:
        for ti, (t0, n, typ, c0) in enumerate(tiles):
            rs, rsb = pass1(t0, n)
            pass2(ti, t0, n, typ, c0, rs, rsb)


def mk_tiles(block):
    out = []
    c0 = 0
    for t0, n in block:
        out.append((t0, n, 1 if t0 < NCTX else 0, c0))
        c0 += n
    return out


FULL_BLOCKS = [[(0, 256), (256, 512)], [(768, 512), (1280, 256)], [(1536, 512), (2048, 256)]]
LAT_BLOCKS = [[(256, 512), (768, 256)], [(1024, 512), (1536, 256)], [(1792, 512)]]
XT_REGIONS = [(0, 256), (256, 512), (768, 512), (768, 256), (1024, 512), (1280, 256), (1536, 512), (1536, 256),
              (1792, 512), (2048, 256), (1280, 512)]


def ph_ffn(k, l, j, blocks, bg=False):
    nc, P = k.nc, k.P
    s = 0 if j == 0 else 2
    wg_d, wu_d, wd_d = k.ffn_wg[l, j], k.ffn_wu[l, j], k.ffn_wd[l, j]
    BT = 768
    with ExitStack() as st:
        hT = sbt(st, nc, "hT", [128, 16, BT], BF16)
        aT = sbt(st, nc, "aT", [128, 44, BT], BF16)
        wgu = Ring(st, nc, "wgu", 4, [128, 16, 256], BF16)
        wdr = Ring(st, nc, "wd", 2 if bg else 3, [128, 44, 128], BF16)
        xin = Ring(st, nc, "xin", 3, [128, 2, 512], F32)
        sqr = Ring(st, nc, "sq", 2, [128, 2, 512], BF16)
        tmpr = Ring(st, nc, "tmp", 2, [128, 512], F32)
        rsr = Ring(st, nc, "rs", 2, [128, 512], F32)
        sgr = Ring(st, nc, "sg", 2, [128, 512], F32)
        xor_ = Ring(st, nc, "xo", 3, [128, 512], F32)
        P.begin_phase()
        bg_steps, bg_finish = [], None
        if bg:
            k.ps_n = 6
            bg_steps, bg_finish = ada_bg_steps(k, l, st, k.ps[6], k.psb[6], W=256, nslab=2, col0=3 * D, ncols=6 * D, subs=(1, 2),
                                               bankT=k.ps[7], bankTb=k.psb[7])
        hTb = [Buf("hT%d" % i) for i in range(4)]
        aTbs = [Buf() for _ in range(44)]
        tl = [mk_tiles(b) for b in blocks]
        norm_mod(k, (xin, sqr, tmpr, rsr), tl[0], hT, hTb, l, s, dq=("sync", "scalar"))
        for bi, tiles in enumerate(tl):
            for fp in range(22):
                if bg_steps:
                    bg_steps.pop(0)()
                wg, wgb = wgu.next()
                P.dma("gpsimd", wg[:], rows(wg_d[:, fp * 256:(fp + 1) * 256]), writes=[wgb], sembuf=wgb)
                wu, wub = wgu.next()
                P.dma("gpsimd", wu[:], rows(wu_d[:, fp * 256:(fp + 1) * 256]), writes=[wub], sembuf=wub)
                for fs in range(2):
                    fc = fp * 2 + fs
                    for ti, (t0, n, typ, c0) in enumerate(tiles):
                        pg, pgb = psum_next(k)
                        pu, pub = psum_next(k)

                        def f(e, ps=pg, w=wg, fs=fs, c0=c0, n=n):
                            for kc in range(16):
                                ins = mmc(e, ps[:, 0:n], lhsT=w[:, kc, fs * 128:(fs + 1) * 128], rhs=hT[:, kc, c0:c0 + n],
                                               start=(kc == 0), stop=(kc == 15))
                            return ins
                        P.op("tensor", f, reads=[wgb, hTb[ti]], writes=[pgb])

                        def f2(e, ps=pu, w=wu, fs=fs, c0=c0, n=n):
                            for kc in range(16):
                                ins = mmc(e, ps[:, 0:n], lhsT=w[:, kc, fs * 128:(fs + 1) * 128], rhs=hT[:, kc, c0:c0 + n],
                                               start=(kc == 0), stop=(kc == 15))
                            return ins
                        P.op("tensor", f2, reads=[wub, hTb[ti]], writes=[pub])
                        sg, sgb = sgr.next()
                        P.op("scalar", lambda e, sg=sg, pg=pg, n=n: e.activation(out=sg[:, 0:n], in_=pg[:, 0:n], func=AF.Silu),
                             reads=[pgb], writes=[sgb])
                        P.op("vector", lambda e, sg=sg, pu=pu, fc=fc, c0=c0, n=n: e.tensor_tensor(
                            out=aT[:, fc, c0:c0 + n], in0=sg[:, 0:n], in1=pu[:, 0:n], op=ALU.mult),
                            reads=[sgb, pub], writes=[aTbs[fc]])
            for dc in range(16):
                if dc == 3 and bi + 1 < len(tl):
                    norm_mod(k, (xin, sqr, tmpr, rsr), tl[bi + 1], hT, hTb, l, s, dq="scalar", split=True)
                wd, wdb = wdr.next()
                P.dma("gpsimd", wd[:, 0:22, :], rows(wd_d[0:2816, dc * 128:(dc + 1) * 128]), writes=[wdb], sembuf=wdb)
                P.dma("gpsimd", wd[:, 22:44, :], rows(wd_d[2816:5632, dc * 128:(dc + 1) * 128]), writes=[wdb], sembuf=wdb)
                for (t0, n, typ, c0) in tiles:
                    ps, pb = psum_next(k)

                    def f(e, ps=ps, wd=wd, c0=c0, n=n):
                        for kc in range(44):
                            ins = mmc(e, ps[:, 0:n], lhsT=wd[:, kc, :], rhs=aT[:, kc, c0:c0 + n], start=(kc == 0), stop=(kc == 43))
                        return ins
                    P.op("tensor", f, reads=[wdb] + aTbs, writes=[pb])
                    xo, xob = xor_.next()
                    P.dma("sync", xo[:, 0:n], k.xT[dc * 128:(dc + 1) * 128, t0:t0 + n], writes=[xob], sembuf=xob)
                    gcol = k.coef[:, l, typ, 3 * s + 2, dc:dc + 1]
                    P.op("vector", lambda e, xo=xo, ps=ps, gcol=gcol, n=n: e.scalar_tensor_tensor(
                        out=xo[:, 0:n], in0=ps[:, 0:n], scalar=gcol, in1=xo[:, 0:n], op0=ALU.mult, op1=ALU.add),
                        reads=[pb, xob, k.coefb], writes=[xob])
                    P.dma("sync", k.xT[dc * 128:(dc + 1) * 128, t0:t0 + n], xo[:, 0:n], reads=[xob], sembuf=xob)
        while bg_steps:
            bg_steps.pop(0)()
        if bg_finish is not None:
            bg_finish()
        k.ps_n = 8
        P.barrier()
        P.emit()
        P.end_phase()


ALL_TILES = [(0, 256), (256, 512), (768, 512), (1280, 512), (1792, 512)]
LAT_TILES = ALL_TILES[1:]
MLA_SCALE = 192.0 ** -0.5
DIFF_SCALE = 0.125


def evac(P, idx, out, in_, reads, writes):
    if idx % 2 == 0:
        P.op("vector", lambda e: e.tensor_copy(out=out, in_=in_), reads=reads, writes=writes)
    else:
        P.op("scalar", lambda e: e.copy(out=out, in_=in_), reads=reads, writes=writes)


def mm_group(P, ps_ap, pairs, reads, writes):
    def f(e):
        n = len(pairs)
        for i, (a, b) in enumerate(pairs):
            ins = mmc(e, ps_ap, lhsT=a, rhs=b, start=(i == 0), stop=(i == n - 1))
        return ins
    return P.op("tensor", f, reads=reads, writes=writes)


def rope_combine(k, P, rings, psA, pAb, psB, pBb, t0, n, dst_ap, stg, stgb):
    t1r, t2r = rings
    t1, t1b = t1r.next()
    t2, t2b = t2r.next()
    P.op("vector", lambda e: e.tensor_tensor(out=t1[:, 0:n], in0=psA[:, 0:n], in1=k.cosT[:, t0:t0 + n], op=ALU.mult),
         reads=[pAb, k.ropeb], writes=[t1b])
    P.op("vector", lambda e: e.tensor_tensor(out=t2[:, 0:n], in0=psB[:, 0:n], in1=k.sinT[:, t0:t0 + n], op=ALU.mult),
         reads=[pBb, k.ropeb], writes=[t2b])
    P.op("vector", lambda e: e.tensor_tensor(out=stg[:, 0:n], in0=t1[:, 0:n], in1=t2[:, 0:n], op=ALU.add),
         reads=[t1b, t2b], writes=[stgb])
    P.dma("sync", dst_ap, stg[:, 0:n], reads=[stgb], sembuf=stgb)


def ph_proj(k, l):
    nc, P = k.nc, k.P
    last = (l == 1)
    w = k.w_in2[l]
    with ExitStack() as st:
        hT = sbt(st, nc, "hTp", [128, 16, NTOK], BF16)
        slabr = Ring(st, nc, "wsl", 4, [128, 16, 256], BF16)
        dvs = sbt(st, nc, "dvs", [128, 16, 512], BF16)
        dvsb = Buf()
        xin = Ring(st, nc, "xin", 3, [128, 2, 512], F32)
        sqr = Ring(st, nc, "sq", 2, [128, 2, 512], BF16)
        xar = Ring(st, nc, "xa", 2, [128, 512], BF16)
        rperm = sbt(st, nc, "rperm", [128, 128], BF16)
        rpb = Buf()
        tmpr = Ring(st, nc, "tmp", 3, [128, 512], F32)
        rsr = Ring(st, nc, "rs", 5, [128, 512], F32)
        rs2r = Ring(st, nc, "rs2", 1, [128, 512], F32)
        rawr = Ring(st, nc, "raw", 5, [128, 512], F32)
        sq2r = Ring(st, nc, "sq2", 4, [128, 512], BF16)
        t1r = Ring(st, nc, "t1", 2, [128, 512], F32)
        t2r = Ring(st, nc, "t2", 2, [128, 512], F32)
        stgr = Ring(st, nc, "stg", 2, [128, 512], BF16)
        k.cosT = sbt(st, nc, "cosT", [128, NTOK], F32)
        k.sinT = sbt(st, nc, "sinT", [128, NTOK], F32)
        k.ropeb = Buf()
        P.begin_phase()
        P.dma("sync", k.cosT[:], k.cosT_d, writes=[k.ropeb], sembuf=k.ropeb)
        P.dma("sync", k.sinT[:], k.sinT_d, writes=[k.ropeb], sembuf=k.ropeb)
        P.dma("gpsimd", rperm[:], k.rperm_d, writes=[rpb], sembuf=rpb)
        hTb = [Buf() for _ in range(5)]
        tiles = mk_tiles(ALL_TILES)
        norm_mod(k, (xin, sqr, tmpr, rsr), tiles, hT, hTb, l, 1, split=True, dq=("sync", "scalar"))
        qtiles = [(ti, t) for ti, t in enumerate(tiles) if not (last and ti == 0)]
        atiles = list(enumerate(tiles))
        slabs = {}

        def get_slab(si):
            sl, sb_ = slabr.next()
            P.dma("gpsimd", sl[:], rows(w[:, si * 256:(si + 1) * 256]), writes=[sb_], sembuf=sb_)
            return sl, sb_

        def proj_mm(sl, sb_, sub, ti, c0, n):
            ps, pb = psum_next(k)
            mm_group(P, ps[:, 0:n], [(sl[:, kc, sub * 128:(sub + 1) * 128], hT[:, kc, c0:c0 + n]) for kc in range(16)],
                     reads=[sb_, hTb[ti]], writes=[pb])
            return ps, pb

        def normed(slab_ids, nch, gcol, dst, tl):
            sls = [get_slab(si) for si in slab_ids]
            for ti, (t0, n, typ, c0) in tl:
                raws = []
                sqs = []
                for ch in range(nch):
                    sl, sb_ = sls[ch // 2]
                    ps, pb = proj_mm(sl, sb_, ch % 2, ti, c0, n)
                    rw, rwb = rawr.next()
                    P.op("vector", lambda e, rw=rw, ps=ps, n=n: e.tensor_copy(out=rw[:, 0:n], in_=ps[:, 0:n]), reads=[pb], writes=[rwb])
                    sq, sqb = sq2r.next()
                    P.op("scalar", lambda e, sq=sq, rw=rw, n=n: e.activation(out=sq[:, 0:n], in_=rw[:, 0:n], func=AF.Square),
                         reads=[rwb], writes=[sqb])
                    raws.append((rw, rwb))
                    sqs.append((sq, sqb))
                pss, pssb = psum_next(k)
                for ch in range(nch):
                    sq, sqb = sqs[ch]
                    P.op("tensor", lambda e, pss=pss, sq=sq, n=n, ch=ch: mmc(e, pss[:, 0:n], lhsT=k.ones_bf[:, :], rhs=sq[:, 0:n],
                                                                                 start=(ch == 0), stop=(ch == nch - 1)),
                         reads=[sqb, k.constb], writes=[pssb])
                rs, rsb = rs2r.next()
                P.op("scalar", lambda e, rs=rs, pss=pss, n=n: e.activation(out=rs[:, 0:n], in_=pss[:, 0:n], func=AF.Sqrt,
                                                                          bias=k.epsc[:, 0:1], scale=1.0 / (128 * nch)),
                     reads=[pssb, k.constb], writes=[rsb])
                P.op("vector", lambda e, rs=rs, n=n: e.reciprocal(out=rs[:, 0:n], in_=rs[:, 0:n]), reads=[rsb], writes=[rsb])
                for ch in range(nch):
                    rw, rwb = raws[ch]
                    P.op("vector", lambda e, rw=rw, rs=rs, n=n: e.tensor_tensor(out=rw[:, 0:n], in0=rw[:, 0:n], in1=rs[:, 0:n], op=ALU.mult),
                         reads=[rwb, rsb], writes=[rwb])
                    sg, sgb = stgr.next()
                    P.op("scalar", lambda e, sg=sg, rw=rw, ch=ch, n=n: e.activation(out=sg[:, 0:n], in_=rw[:, 0:n], func=AF.Identity,
                                                                                   scale=gcol[:, ch:ch + 1]),
                         reads=[rwb, k.constb], writes=[sgb])
                    P.dma("sync", dst[ch * 128:(ch + 1) * 128, t0:t0 + n], sg[:, 0:n], reads=[sgb], sembuf=sgb)

        pend_rope = []

        def roped(main_slab, main_sub, dst_rows, tl):
            for ti, (t0, n, typ, c0) in tl:
                psA, pAb = proj_mm(main_slab[0], main_slab[1], main_sub, ti, c0, n)
                xa, xab = xar.next()
                P.op("scalar", lambda e, xa=xa, psA=psA, n=n: e.copy(out=xa[:, 0:n], in_=psA[:, 0:n]), reads=[pAb], writes=[xab])
                if pend_rope:
                    pend_rope.pop(0)()

                def tail(psA=psA, pAb=pAb, xa=xa, xab=xab, t0=t0, n=n):
                    psB, pBb = psum_next(k)
                    P.op("tensor", lambda e: mmc(e, psB[:, 0:n], lhsT=rperm[:, :], rhs=xa[:, 0:n], start=True, stop=True),
                         reads=[xab, rpb], writes=[pBb])
                    sg, sgb = stgr.next()
                    rope_combine(k, P, (t1r, t2r), psA, pAb, psB, pBb, t0, n, dst_rows[:, t0:t0 + n], sg, sgb)
                pend_rope.append(tail)

        normed([0], 2, k.kvnT[:, l, :], k.ckvnT, atiles)
        s1 = get_slab(1)
        roped(s1, 0, k.krT, atiles)
        for hp in range(2):
            sm = get_slab(2 + hp)
            for sub in range(2):
                h = hp * 2 + sub
                roped(sm, sub, k.dkT[h * 128:(h + 1) * 128, :], atiles)
        while pend_rope:
            pend_rope.pop(0)()
        normed([6, 7], 4, k.qnormT[:, l, :], k.cqnT, qtiles)
        for hp in range(2):
            sm = get_slab(8 + hp)
            for sub in range(2):
                h = hp * 2 + sub
                roped(sm, sub, k.dqT[h * 128:(h + 1) * 128, :], qtiles)
        while pend_rope:
            pend_rope.pop(0)()
        cnt = 0
        for hp in range(2):
            sm = get_slab(12 + hp)
            for sub in range(2):
                g = hp * 2 + sub
                for ti, (t0, n, typ, c0) in qtiles:
                    ps, pb = proj_mm(sm[0], sm[1], sub, ti, c0, n)
                    sg, sgb = stgr.next()
                    evac(P, cnt, sg[:, 0:n], ps[:, 0:n], [pb], [sgb])
                    cnt += 1
                    P.dma("sync", k.uT[g * 128:(g + 1) * 128, t0:t0 + n], sg[:, 0:n], reads=[sgb], sembuf=sgb)
        P.dma("gpsimd", dvs[:], rows(w[:, 28 * 128:32 * 128]), writes=[dvsb], sembuf=dvsb)
        for tc in range(18):
            ti = 0 if tc < 2 else 1 + (tc - 2) // 4
            ps, pb = psum_next(k)
            mm_group(P, ps[:, :], [(hT[:, kc, tc * 128:(tc + 1) * 128], dvs[:, kc, :]) for kc in range(16)],
                     reads=[dvsb, hTb[ti]], writes=[pb])
            sg, sgb = stgr.next()
            evac(P, tc, sg[:, :], ps[:, :], [pb], [sgb])
            P.dma("sync", k.DVs[tc * 128:(tc + 1) * 128, :], sg[:, :], reads=[sgb], sembuf=sgb)
        P.barrier()
        P.emit()
        P.end_phase()


def ph_proj2(k, l):
    nc, P = k.nc, k.P
    last = (l == 1)
    with ExitStack() as st:
        ckv = sbt(st, nc, "ckv", [128, 2, NTOK], BF16)
        cq = sbt(st, nc, "cq", [128, 4, NTOK], BF16)
        wukv = sbt(st, nc, "wukv", [128, 2, 2048], BF16)
        wuq = sbt(st, nc, "wuq", [128, 4, 2048], BF16)
        t1r = Ring(st, nc, "t1", 2, [128, 512], F32)
        t2r = Ring(st, nc, "t2", 2, [128, 512], F32)
        stgr = Ring(st, nc, "stg", 4, [128, 512], BF16)
        k.cosT = sbt(st, nc, "cosT", [128, NTOK], F32)
        k.sinT = sbt(st, nc, "sinT", [128, NTOK], F32)
        k.ropeb = Buf()
        ckvb, cqb, wkb, wqb = Buf(), Buf(), Buf(), Buf()
        P.begin_phase()
        P.dma("sync", ckv[:], rows(k.ckvnT[:, :]), writes=[ckvb], sembuf=ckvb)
        P.dma("sync", cq[:], rows(k.cqnT[:, :]), writes=[cqb], sembuf=cqb)
        P.dma("sync", k.cosT[:], k.cosT_d, writes=[k.ropeb], sembuf=k.ropeb)
        P.dma("sync", k.sinT[:], k.sinT_d, writes=[k.ropeb], sembuf=k.ropeb)
        P.dma("gpsimd", wukv[:], rows(k.w_ukv2[l]), writes=[wkb], sembuf=wkb)
        P.dma("gpsimd", wuq[:], rows(k.w_uq2[l]), writes=[wqb], sembuf=wqb)
        tiles = mk_tiles(ALL_TILES)
        qtiles = tiles[1:] if last else tiles
        cnt = 0
        for h in range(8):
            for (t0, n, typ, c0) in tiles:
                ps, pb = psum_next(k)
                mm_group(P, ps[:, 0:n], [(wukv[:, kc, h * 128:(h + 1) * 128], ckv[:, kc, t0:t0 + n]) for kc in range(2)],
                         reads=[wkb, ckvb], writes=[pb])
                sg, sgb = stgr.next()
                evac(P, cnt, sg[:, 0:n], ps[:, 0:n], [pb], [sgb])
                cnt += 1
                P.dma("sync", k.knT[h * 128:(h + 1) * 128, t0:t0 + n], sg[:, 0:n], reads=[sgb], sembuf=sgb)
        for tc in range(18):
            for half in range(2):
                ps, pb = psum_next(k)
                mm_group(P, ps[:, :], [(ckv[:, kc, tc * 128:(tc + 1) * 128], wukv[:, kc, 1024 + half * 512:1024 + (half + 1) * 512])
                                      for kc in range(2)], reads=[wkb, ckvb], writes=[pb])
                sg, sgb = stgr.next()
                evac(P, cnt, sg[:, :], ps[:, :], [pb], [sgb])
                cnt += 1
                P.dma("sync", k.Vs[tc * 128:(tc + 1) * 128, half * 512:(half + 1) * 512], sg[:, :], reads=[sgb], sembuf=sgb)
        for h in range(8):
            for (t0, n, typ, c0) in qtiles:
                ps, pb = psum_next(k)
                mm_group(P, ps[:, 0:n], [(wuq[:, kc, h * 128:(h + 1) * 128], cq[:, kc, t0:t0 + n]) for kc in range(4)],
                         reads=[wqb, cqb], writes=[pb])
                sg, sgb = stgr.next()
                evac(P, cnt, sg[:, 0:n], ps[:, 0:n], [pb], [sgb])
                cnt += 1
                P.dma("sync", k.qnT[h * 128:(h + 1) * 128, t0:t0 + n], sg[:, 0:n], reads=[sgb], sembuf=sgb)
        for r in range(4):
            for (t0, n, typ, c0) in qtiles:
                psA, pAb = psum_next(k)
                mm_group(P, psA[:, 0:n], [(wuq[:, kc, 1024 + r * 128:1024 + (r + 1) * 128], cq[:, kc, t0:t0 + n]) for kc in range(4)],
                         reads=[wqb, cqb], writes=[pAb])
                psB, pBb = psum_next(k)
                mm_group(P, psB[:, 0:n], [(wuq[:, kc, 1536 + r * 128:1536 + (r + 1) * 128], cq[:, kc, t0:t0 + n]) for kc in range(4)],
                         reads=[wqb, cqb], writes=[pBb])
                sg, sgb = stgr.next()
                rope_combine(k, P, (t1r, t2r), psA, pAb, psB, pBb, t0, n, k.qrT[r * 128:(r + 1) * 128, t0:t0 + n], sg, sgb)
        P.barrier()
        P.emit()
        P.end_phase()


def ph_mla(k, l, bg_layer=None):
    nc, P = k.nc, k.P
    with ExitStack() as st:
        kn = sbt(st, nc, "kn", [128, 8, NTOK], BF16)
        V = sbt(st, nc, "V", [128, 18, 1024], BF16)
        kr = sbt(st, nc, "kr", [128, NTOK], BF16)
        qnr = Ring(st, nc, "qn", 2, [128, 8, 512], BF16)
        qrr = Ring(st, nc, "qr", 2, [128, 8, 512], BF16)
        Er = Ring(st, nc, "E", 4, [128, 512], BF16)
        rzr = Ring(st, nc, "rz", 2, [128, 512], F32)
        stgr = Ring(st, nc, "stg", 3, [128, 512], BF16)
        knb, Vb, krb = Buf(), Buf(), Buf()
        P.begin_phase()
        bg_steps, bg_finish, bg_pend = [], None, []
        if bg_layer is not None:
            bg_steps, bg_finish, bg_pend = ada_bg_steps(k, bg_layer, st, k.ps[3], k.psb[3], auto_tail=False)
        NS = 3 if bg_layer is not None else 4
        for t_, b_ in zip(qrr.t, qrr.b):
            P.op("vector", lambda e, t_=t_: e.memset(t_[:], 0.0), writes=[b_])
        P.dma("sync", kn[:], rows(k.knT[:, :]), writes=[knb], sembuf=knb)
        P.dma("sync", kr[:], k.krT[:, :], writes=[krb], sembuf=krb)
        qt = [((256 + i * 512), 512, list(range(18))) for i in range(4)]
        if l == 0:
            qt = [(0, 256, [0, 1])] + qt
        si = 0
        hh = 0
        def load_q(t0, n):
            qn, qnb = qnr.next()
            qr, qrb = qrr.next()
            P.dma("sync", qn[:, :, 0:n], rows(k.qnT[:, t0:t0 + n]), writes=[qnb], sembuf=qnb)
            for h_ in range(8):
                hp_ = 64 * (h_ % 2)
                r0_ = (h_ // 2) * 128 + hp_
                P.dma("sync", qr[hp_:hp_ + 64, h_, 0:n], k.qrT[r0_:r0_ + 64, t0:t0 + n], writes=[qrb], sembuf=qrb)
            return qn, qnb, qr, qrb
        qloads = [load_q(qt[0][0], qt[0][1])]
        P.dma("sync", V[:], rows(k.Vs[:, :]), writes=[Vb], sembuf=Vb)
        for qi, (t0, n, keys) in enumerate(qt):
            if qi + 1 < len(qt):
                qloads.append(load_q(qt[qi + 1][0], qt[qi + 1][1]))
            qn, qnb, qr, qrb = qloads[qi]

            def do_head(h, t0, n, keys, qn, qnb, qr, qrb, O, Ob, Z, Zb):
                nonlocal si
                hp = 64 * (h % 2)

                def s_mm(j):
                    nonlocal si
                    S, Sb = k.ps[si % NS], k.psb[si % NS]
                    si += 1
                    mm_group(P, S[:, 0:n], [(kn[:, h, j * 128:(j + 1) * 128], qn[:, h, 0:n]),
                                            (kr[:, j * 128:(j + 1) * 128], qr[:, h, 0:n])],
                             reads=[knb, krb, qnb, qrb], writes=[Sb])
                    return S, Sb
                cur = s_mm(keys[0])
                for ji, j in enumerate(keys):
                    S, Sb = cur
                    if ji + 1 < len(keys):
                        cur = s_mm(keys[ji + 1])
                    E, Eb = Er.next()
                    P.op("scalar", lambda e, E=E, S=S: e.activation(out=E[:, 0:n], in_=S[:, 0:n], func=AF.Exp, scale=MLA_SCALE),
                         reads=[Sb], writes=[Eb])

                    def pv(e, E=E, j=j, ji=ji, O=O, Z=Z):
                        mmc(e, O[:, 0:n], lhsT=V[:, j, h * 128:(h + 1) * 128], rhs=E[:, 0:n], start=(ji == 0), stop=(ji == len(keys) - 1))
                        return mmc(e, Z[:, 0:n], lhsT=k.ones_bf[:, :], rhs=E[:, 0:n], start=(ji == 0), stop=(ji == len(keys) - 1))
                    P.op("tensor", pv, reads=[Eb, Vb, k.constb], writes=[Ob, Zb])
                    if bg_pend and (ji == 9 or ji == len(keys) - 1):
                        bg_pend.pop(0)()
                rz, rzb = rzr.next()
                P.op("vector", lambda e, rz=rz, Z=Z: e.reciprocal(out=rz[:, 0:n], in_=Z[:, 0:n]), reads=[Zb], writes=[rzb])
                sg, sgb = stgr.next()
                P.op("vector", lambda e, sg=sg, O=O, rz=rz: e.tensor_tensor(out=sg[:, 0:n], in0=O[:, 0:n], in1=rz[:, 0:n], op=ALU.mult),
                     reads=[Ob, rzb], writes=[sgb])
                P.dma("sync", k.oT[h * 128:(h + 1) * 128, t0:t0 + n], sg[:, 0:n], reads=[sgb], sembuf=sgb)
            for h in range(8):
                if bg_steps:
                    bg_steps.pop(0)()
                do_head(h, t0, n, keys, qn, qnb, qr, qrb, k.ps[4 + hh % 2], k.psb[4 + hh % 2], k.ps[6 + hh % 2], k.psb[6 + hh % 2])
                hh += 1
        while bg_steps:
            bg_steps.pop(0)()
            while bg_pend:
                bg_pend.pop(0)()
        if bg_finish is not None:
            bg_finish()
        P.barrier()
        P.emit()
        P.end_phase()


def ph_diff(k, l):
    nc, P = k.nc, k.P
    lam_init = 0.8 - 0.6 * math.exp(-0.3 * l)
    cfac = 1.0 - lam_init
    with ExitStack() as st:
        dk = sbt(st, nc, "dk", [128, 4, NTOK], BF16)
        DV = sbt(st, nc, "DV", [128, 18, 512], BF16)
        dqr = Ring(st, nc, "dq", 2, [128, 4, 512], BF16)
        Er = Ring(st, nc, "E", 6, [128, 512], BF16)
        fr = Ring(st, nc, "f", 12, [128, 512], F32)
        sqr = Ring(st, nc, "sq", 2, [128, 512], BF16)
        stgr = Ring(st, nc, "stg", 3, [128, 512], BF16)
        pending = []
        lt = sbt(st, nc, "lt", [128, 2, 64], F32)
        lr = sbt(st, nc, "lr", [128, 4], F32)
        dkb, DVb, lb = Buf(), Buf(), Buf()
        P.begin_phase()
        P.dma("sync", dk[:], rows(k.dkT[:, :]), writes=[dkb], sembuf=dkb)
        lam = k.lamrep
        P.op("vector", lambda e: e.tensor_tensor(out=lt[:, 0, :], in0=lam[:, l, 0, :], in1=lam[:, l, 1, :], op=ALU.mult), reads=[k.constb], writes=[lb])
        P.op("vector", lambda e: e.tensor_tensor(out=lt[:, 1, :], in0=lam[:, l, 2, :], in1=lam[:, l, 3, :], op=ALU.mult), reads=[k.constb, lb], writes=[lb])
        P.op("vector", lambda e: e.reduce_sum(out=lr[:, 0:2], in_=lt[:, :, :], axis=mybir.AxisListType.X), reads=[lb], writes=[lb])
        P.op("scalar", lambda e: e.activation(out=lr[:, 0:2], in_=lr[:, 0:2], func=AF.Exp), reads=[lb], writes=[lb])
        P.op("vector", lambda e: e.tensor_tensor(out=lr[:, 2:3], in0=lr[:, 1:2], in1=lr[:, 0:1], op=ALU.subtract), reads=[lb], writes=[lb])
        P.op("vector", lambda e: e.tensor_scalar(out=lr[:, 2:3], in0=lr[:, 2:3], scalar1=-lam_init, scalar2=None, op0=ALU.add), reads=[lb], writes=[lb])
        P.op("vector", lambda e: e.memset(lr[:, 3:4], EPS / (cfac * cfac)), reads=[lb], writes=[lb])
        neglam = lr[:, 2:3]
        epsc2 = lr[:, 3:4]
        qt = [((256 + i * 512), 512, list(range(18))) for i in range(4)]
        if l == 0:
            qt = [(0, 256, [0, 1])] + qt
        si = 0
        def load_q(t0, n):
            dq, dqb = dqr.next()
            P.dma("sync", dq[:, :, 0:n], rows(k.dqT[:, t0:t0 + n]), writes=[dqb], sembuf=dqb)
            return dq, dqb
        qloads = [load_q(qt[0][0], qt[0][1])]
        P.dma("sync", DV[:], rows(k.DVs[:, :]), writes=[DVb], sembuf=DVb)
        for qi, (t0, n, keys) in enumerate(qt):
            if qi + 1 < len(qt):
                qloads.append(load_q(qt[qi + 1][0], qt[qi + 1][1]))
            dq, dqb = qloads[qi]

            def do_head(h, t0, n, keys, dq, dqb):
                nonlocal si
                acc = [(k.ps[4 + i], k.psb[4 + i]) for i in range(4)]

                def s_mm(j):
                    nonlocal si
                    S1, S1b = k.ps[(2 * si) % 4], k.psb[(2 * si) % 4]
                    S2, S2b = k.ps[(2 * si) % 4 + 1], k.psb[(2 * si) % 4 + 1]
                    si += 1
                    P.op("tensor", lambda e: mmc(e, S1[:, 0:n], lhsT=dk[0:64, h, j * 128:(j + 1) * 128], rhs=dq[0:64, h, 0:n], start=True, stop=True),
                         reads=[dkb, dqb], writes=[S1b])
                    P.op("tensor", lambda e: mmc(e, S2[:, 0:n], lhsT=dk[64:128, h, j * 128:(j + 1) * 128], rhs=dq[64:128, h, 0:n], start=True, stop=True),
                         reads=[dkb, dqb], writes=[S2b])
                    return (S1, S1b, S2, S2b)
                cur = s_mm(keys[0])
                for ji, j in enumerate(keys):
                    S1, S1b, S2, S2b = cur
                    if ji + 1 < len(keys):
                        cur = s_mm(keys[ji + 1])
                    first, lastj = (ji == 0), (ji == len(keys) - 1)
                    for (S, Sb, (O, Ob), (Z, Zb)) in ((S1, S1b, acc[0], acc[1]), (S2, S2b, acc[2], acc[3])):
                        E, Eb = Er.next()
                        P.op("scalar", lambda e, E=E, S=S: e.activation(out=E[:, 0:n], in_=S[:, 0:n], func=AF.Exp, scale=DIFF_SCALE),
                             reads=[Sb], writes=[Eb])

                        def pv(e, E=E, O=O, Z=Z, j=j, first=first, lastj=lastj):
                            mmc(e, O[:, 0:n], lhsT=DV[:, j, h * 128:(h + 1) * 128], rhs=E[:, 0:n], start=first, stop=lastj)
                            return mmc(e, Z[:, 0:n], lhsT=k.ones_bf[:, :], rhs=E[:, 0:n], start=first, stop=lastj)
                        P.op("tensor", pv, reads=[Eb, DVb, k.constb], writes=[Ob, Zb])
                    if pending and (ji == 8 or lastj):
                        pending.pop(0)()
                cps = []
                for (A, Ab) in acc:
                    c, cb = fr.next()
                    P.op("vector", lambda e, c=c, A=A: e.tensor_copy(out=c[:, 0:n], in_=A[:, 0:n]), reads=[Ab], writes=[cb])
                    cps.append((c, cb))
                (o1, o1b), (z1, z1b), (o2, o2b), (z2, z2b) = cps
                P.op("vector", lambda e: e.reciprocal(out=z1[:, 0:n], in_=z1[:, 0:n]), reads=[z1b], writes=[z1b])
                P.op("vector", lambda e: e.reciprocal(out=z2[:, 0:n], in_=z2[:, 0:n]), reads=[z2b], writes=[z2b])
                P.op("vector", lambda e: e.tensor_tensor(out=o1[:, 0:n], in0=o1[:, 0:n], in1=z1[:, 0:n], op=ALU.mult), reads=[o1b, z1b], writes=[o1b])
                P.op("vector", lambda e: e.tensor_tensor(out=o2[:, 0:n], in0=o2[:, 0:n], in1=z2[:, 0:n], op=ALU.mult), reads=[o2b, z2b], writes=[o2b])
                o, ob = fr.next()
                P.op("vector", lambda e: e.scalar_tensor_tensor(out=o[:, 0:n], in0=o2[:, 0:n], scalar=neglam, in1=o1[:, 0:n],
                                                                op0=ALU.mult, op1=ALU.add), reads=[o1b, o2b, lb], writes=[ob])
                sq, sqb = sqr.next()
                P.op("vector", lambda e: e.tensor_tensor(out=sq[:, 0:n], in0=o[:, 0:n], in1=o[:, 0:n], op=ALU.mult), reads=[ob], writes=[sqb])

                def finish():
                    nonlocal si
                    SS, SSb = k.ps[(2 * si) % 4], k.psb[(2 * si) % 4]
                    P.op("tensor", lambda e: mmc(e, SS[:, 0:n], lhsT=k.ones_bf[:, :], rhs=sq[:, 0:n], start=True, stop=True),
                         reads=[sqb, k.constb], writes=[SSb])
                    rs, rsb = fr.next()
                    P.op("scalar", lambda e: e.activation(out=rs[:, 0:n], in_=SS[:, 0:n], func=AF.Ln, bias=epsc2,
                                                          scale=1.0 / (128.0 * cfac * cfac)), reads=[SSb, lb], writes=[rsb])
                    P.op("scalar", lambda e: e.activation(out=rs[:, 0:n], in_=rs[:, 0:n], func=AF.Exp, scale=-0.5), reads=[rsb], writes=[rsb])
                    P.op("vector", lambda e: e.tensor_tensor(out=o[:, 0:n], in0=o[:, 0:n], in1=rs[:, 0:n], op=ALU.mult),
                         reads=[ob, rsb], writes=[ob])
                    sg, sgb = stgr.next()
                    P.op("scalar", lambda e: e.activation(out=sg[:, 0:n], in_=o[:, 0:n], func=AF.Identity, scale=k.sublnT[:, l:l + 1]),
                         reads=[ob, k.constb], writes=[sgb])
                    P.dma("sync", k.oT[1024 + h * 128:1024 + (h + 1) * 128, t0:t0 + n], sg[:, 0:n], reads=[sgb], sembuf=sgb)
                pending.append(finish)
            for h in range(4):
                do_head(h, t0, n, keys, dq, dqb)
        while pending:
            pending.pop(0)()
        P.barrier()
        P.emit()
        P.end_phase()


def ph_fourier(k, l):
    nc, P = k.nc, k.P
    with ExitStack() as st:
        u = sbt(st, nc, "u", [128, 4, NTOK], BF16)
        ccsc = sbt(st, nc, "ccsc", [128, 256], BF16)
        AB = sbt(st, nc, "AB", [128, 16, 4, 256], BF16)
        ABc = sbt(st, nc, "ABc", [128, 2, 4, 256], BF16)
        csr = Ring(st, nc, "cs", 2, [128, 16, 512], BF16)
        ssr = Ring(st, nc, "ss", 2, [128, 16, 512], BF16)
        c256 = sbt(st, nc, "c256", [128, 2, 256], BF16)
        s256 = sbt(st, nc, "s256", [128, 2, 256], BF16)
        stgr = Ring(st, nc, "stg", 3, [128, 512], BF16)
        ub, cb_, c2b, s2b = Buf(), Buf(), Buf(), Buf()
        ABb = [Buf() for _ in range(16)]
        ABcb = [Buf() for _ in range(2)]
        P.begin_phase()
        P.dma("sync", u[:], rows(k.uT[:, :]), writes=[ub], sembuf=ub)
        P.dma("gpsimd", ccsc[:], k.ccsc_d, writes=[cb_], sembuf=cb_)
        cnt = 0

        def step1(tok0, ABt, ABtb, tc):
            nonlocal cnt
            for gp in range(2):
                ps, pb = psum_next(k)

                def f(e, ps=ps, gp=gp):
                    for gs in range(2):
                        ins = mmc(e, ps[:, gs * 256:(gs + 1) * 256], lhsT=u[:, gp * 2 + gs, tok0:tok0 + 128], rhs=ccsc[:, :], start=True, stop=True)
                    return ins
                P.op("tensor", f, reads=[ub, cb_], writes=[pb])
                evac(P, cnt, ABt[:, tc, gp * 2:gp * 2 + 2, :].rearrange("p a b -> p (a b)"), ps[:, :], [pb], [ABtb[tc]])
                cnt += 1
        for tc in range(16):
            step1(256 + tc * 128, AB, ABb, tc)
        for stile in range(4):
            cs, csb = csr.next()
            ss, ssb = ssr.next()
            P.dma("gpsimd", cs[:], rows(k.cs2048[:, stile * 512:(stile + 1) * 512]), writes=[csb], sembuf=csb)
            P.dma("gpsimd", ss[:], rows(k.ss2048[:, stile * 512:(stile + 1) * 512]), writes=[ssb], sembuf=ssb)
            for g in range(4):
                ps, pb = psum_next(k)
                pairs = []
                for sc in range(16):
                    pairs.append((AB[:, sc, g, 0:128], cs[:, sc, :]))
                    pairs.append((AB[:, sc, g, 128:256], ss[:, sc, :]))
                mm_group(P, ps[:, :], pairs, reads=ABb + [csb, ssb], writes=[pb])
                sg, sgb = stgr.next()
                evac(P, cnt, sg[:, :], ps[:, :], [pb], [sgb])
                cnt += 1
                P.dma("sync", k.oT[1536 + g * 128:1536 + (g + 1) * 128, 256 + stile * 512:256 + (stile + 1) * 512], sg[:, :],
                      reads=[sgb], sembuf=sgb)
        if l == 0:
            P.dma("gpsimd", c256[:], rows(k.c256_d), writes=[c2b], sembuf=c2b)
            P.dma("gpsimd", s256[:], rows(k.s256_d), writes=[s2b], sembuf=s2b)
            for tc in range(2):
                step1(tc * 128, ABc, ABcb, tc)
            for g in range(4):
                ps, pb = psum_next(k)
                pairs = []
                for sc in range(2):
                    pairs.append((ABc[:, sc, g, 0:128], c256[:, sc, :]))
                    pairs.append((ABc[:, sc, g, 128:256], s256[:, sc, :]))
                mm_group(P, ps[:, 0:256], pairs, reads=ABcb + [c2b, s2b], writes=[pb])
                sg, sgb = stgr.next()
                evac(P, cnt, sg[:, 0:256], ps[:, 0:256], [pb], [sgb])
                cnt += 1
                P.dma("sync", k.oT[1536 + g * 128:1536 + (g + 1) * 128, 0:256], sg[:, 0:256], reads=[sgb], sembuf=sgb)
        P.barrier()
        P.emit()
        P.end_phase()


def ph_wout(k, l):
    nc, P = k.nc, k.P
    tiles = mk_tiles(ALL_TILES)
    if l == 1:
        tiles = tiles[1:]
    with ExitStack() as st:
        o = sbt(st, nc, "o", [128, 16, NTOK], BF16)
        wor = Ring(st, nc, "wo", 3, [128, 16, 256], BF16)
        xor_ = Ring(st, nc, "xo", 3, [128, 512], F32)
        ob = [Buf() for _ in range(16)]
        P.begin_phase()
        for kc in range(16):
            P.dma("sync", o[:, kc, :], k.oT[kc * 128:(kc + 1) * 128, :], writes=[ob[kc]], sembuf=ob[kc])
        for dp in range(8):
            wo, wob = wor.next()
            P.dma("gpsimd", wo[:], rows(k.w_out[l, :, dp * 256:(dp + 1) * 256]), writes=[wob], sembuf=wob)
            for ds in range(2):
                dc = dp * 2 + ds
                for (t0, n, typ, c0) in tiles:
                    ps, pb = psum_next(k)
                    mm_group(P, ps[:, 0:n], [(wo[:, kc, ds * 128:(ds + 1) * 128], o[:, kc, t0:t0 + n]) for kc in range(16)],
                             reads=[wob] + ob, writes=[pb])
                    xo, xob = xor_.next()
                    P.dma("scalar", xo[:, 0:n], k.xT[dc * 128:(dc + 1) * 128, t0:t0 + n], writes=[xob], sembuf=xob)
                    gcol = k.coef[:, l, typ, 5, dc:dc + 1]
                    P.op("vector", lambda e, xo=xo, ps=ps, gcol=gcol, n=n: e.scalar_tensor_tensor(
                        out=xo[:, 0:n], in0=ps[:, 0:n], scalar=gcol, in1=xo[:, 0:n], op0=ALU.mult, op1=ALU.add),
                        reads=[pb, xob, k.coefb], writes=[xob])
                    P.dma("sync", k.xT[dc * 128:(dc + 1) * 128, t0:t0 + n], xo[:, 0:n], reads=[xob], sembuf=xob)
        P.barrier()
        P.emit()
        P.end_phase()


def ph_final(k):
    nc, P = k.nc, k.P
    with ExitStack() as st:
        yr = Ring(st, nc, "y", 2, [128, 16, 512], F32)
        osr = Ring(st, nc, "os", 2, [128, D], F32)
        xin = Ring(st, nc, "xin", 3, [128, 2, 512], F32)
        sqr = Ring(st, nc, "sq", 2, [128, 2, 512], BF16)
        tmpr = Ring(st, nc, "tmp", 2, [128, 512], F32)
        rsr = Ring(st, nc, "rs", 2, [128, 512], F32)
        P.begin_phase()
        cnt = 0
        for (t0, n) in LAT_TILES:
            y, yb = yr.next()
            norm_mod(k, (xin, sqr, tmpr, rsr), [(t0, n, 0, 0)], y, [yb], 0, 0, g_only=k.fnT, dq=("sync", "scalar"))
            for tc in range(4):
                os_, osb = osr.next()
                for q in range(4):
                    ps, pb = psum_next(k)

                    def f(e, ps=ps, y=y, tc=tc, q=q):
                        for j in range(4):
                            kc = q * 4 + j
                            ins = mmc(e, ps[:, j * 128:(j + 1) * 128], lhsT=y[:, kc, tc * 128:(tc + 1) * 128], rhs=k.ident[:, :],
                                           start=True, stop=True)
                        return ins
                    P.op("tensor", f, reads=[yb, k.constb], writes=[pb])
                    evac(P, cnt, os_[:, q * 512:(q + 1) * 512], ps[:, :], [pb], [osb])
                    cnt += 1
                r0 = t0 - NCTX + tc * 128
                P.dma("sync", k.out[r0:r0 + 128, :], os_[:, :], reads=[osb], sembuf=osb)
        P.barrier()
        P.emit()
        P.end_phase()


SCRATCH = [("ckvnT", [256, NTOK]), ("cqnT", [512, NTOK]), ("knT", [1024, NTOK]), ("Vs", [NTOK, 1024]), ("krT", [128, NTOK]),
           ("dkT", [512, NTOK]), ("DVs", [NTOK, 512]), ("qnT", [1024, NTOK]), ("qrT", [512, NTOK]), ("dqT", [512, NTOK]),
           ("uT", [512, NTOK]), ("oT", [2048, NTOK])]


NB2 = 3
SW_WINDOW = 5
DBG_MODE = ""


def build(stop_after=None, debug=False, skip=()):
    nc = bass.Bass("TRN2", target_bir_lowering=False)
    k = K()
    k.nc = nc
    MM_COUNT[0] = 0

    def din(name, shape, dt=F32):
        return nc.dram_tensor(name, list(shape), dt, kind="ExternalInput").ap()

    def dscratch(name, shape, dt):
        if debug:
            return nc.dram_tensor(name, list(shape), dt, kind="ExternalOutput").ap()
        return nc.dram_tensor(name, list(shape), dt).ap()

    k.x = din("x", [NLAT, D])
    k.ctx = din("ctx", [NCTX, D])
    cT_d = din("cT", [128, 32])
    k.ada_w = din("ada_w", [2, D, 9 * D])
    adabT_d = din("adabT", [128, 2 * 144])
    gT_d = din("gT", [128, 2 * 3 * 16])
    k.ffn_wg = din("ffn_wg", [2, 2, D, DFF])
    k.ffn_wu = din("ffn_wu", [2, 2, D, DFF])
    k.ffn_wd = din("ffn_wd", [2, 2, DFF, D])
    ident_d = din("ident", [128, 128])
    k.w_in2 = din("w_in2", [2, D, 4096])
    k.w_ukv2 = din("w_ukv2", [2, 256, 2048])
    k.w_uq2 = din("w_uq2", [2, 512, 2048])
    k.w_out = din("w_out", [2, D, D])
    k.cosT_d = din("cosT", [128, NTOK])
    k.sinT_d = din("sinT", [128, NTOK])
    kvnT_d = din("kvnT", [128, 4])
    qnormT_d = din("qnormT", [128, 8])
    sublnT_d = din("sublnT", [128, 2])
    lamrep_d = din("lamrep", [128, 512])
    fnT_d = din("fnT", [128, 16])
    k.rperm_d = din("rperm", [128, 128])
    k.ccsc_d = din("ccsc", [128, 256])
    k.cs2048 = din("cs2048", [2048, 2048])
    k.ss2048 = din("ss2048", [2048, 2048])
    k.c256_d = din("c256", [256, 256])
    k.s256_d = din("s256", [256, 256])
    k.out = nc.dram_tensor("out", [NLAT, D], F32, kind="ExternalOutput").ap()
    k.xT = dscratch("xT", [D, NTOK], F32)
    for name, shape in SCRATCH:
        setattr(k, name, dscratch(name, shape, BF16))

    with ExitStack() as st:
        P = Prog(nc)
        P.open(st)
        k.P = P
        k.ps = [st.enter_context(nc.psum_tensor("ps%d" % i, [128, 512], F32)) for i in range(8)]
        k.psb = [Buf("ps%d" % i) for i in range(8)]
        k.ps_i = 0
        k.ps_n = 8
        sb = lambda name, shape, dt: st.enter_context(nc.sbuf_tensor("s_" + name, shape, dt))
        k.ident = sb("ident", [128, 128], F32)
        k.ones_bf = sb("ones_bf", [128, 128], BF16)
        k.epsc = sb("epsc", [128, 1], F32)
        k.cT = sb("cT", [128, 16, 2], F32)
        k.sT = sb("sT", [128, 16, 2], BF16)
        k.adabT = sb("adabT", [128, 2, 144], F32)
        k.gT = sb("gT", [128, 2, 3, 16], F32)
        k.mT = sb("mT", [128, 2, 2, 144], F32)
        k.coef = sb("coef", [128, 2, 2, 9, 16], F32)
        k.kvnT = sb("kvnT", [128, 2, 2], F32)
        k.qnormT = sb("qnormT", [128, 2, 4], F32)
        k.sublnT = sb("sublnT", [128, 2], F32)
        k.lamrep = sb("lamrep", [128, 2, 4, 64], F32)
        k.fnT = sb("fnT", [128, 16], F32)
        k.constb = Buf("const")
        k.sTb = Buf("sT")
        k.coefb = Buf("coef")

        P.begin_phase()
        cb = k.constb
        P.dma("sync", k.ident[:], ident_d, writes=[cb], sembuf=cb)
        P.dma("sync", k.cT[:].rearrange("p a b -> p (a b)"), cT_d, writes=[cb], sembuf=cb)
        P.dma("sync", k.adabT[:].rearrange("p a b -> p (a b)"), adabT_d, writes=[cb], sembuf=cb)
        P.dma("sync", k.gT[:].rearrange("p a b c -> p (a b c)"), gT_d, writes=[cb], sembuf=cb)
        P.dma("sync", k.kvnT[:].rearrange("p a b -> p (a b)"), kvnT_d, writes=[cb], sembuf=cb)
        P.dma("sync", k.qnormT[:].rearrange("p a b -> p (a b)"), qnormT_d, writes=[cb], sembuf=cb)
        P.dma("sync", k.sublnT[:], sublnT_d, writes=[cb], sembuf=cb)
        P.dma("sync", k.lamrep[:].rearrange("p a b c -> p (a b c)"), lamrep_d, writes=[cb], sembuf=cb)
        P.dma("sync", k.fnT[:], fnT_d, writes=[cb], sembuf=cb)
        P.op("vector", lambda e: e.memset(k.ones_bf[:], 1.0), writes=[cb])
        P.op("vector", lambda e: e.memset(k.epsc[:], EPS), writes=[cb])
        P.barrier()
        P.emit()
        P.end_phase()

        stages = [("tin", lambda: ph_transpose_in(k))]
        for l in range(2):
            if l == 0:
                stages.append(("ada%d" % l, lambda l=l: ph_ada(k, l, ntiles=12, subs=(0,))))
            stages.append(("ffn%d_0" % l, lambda l=l: ph_ffn(k, l, 0, FULL_BLOCKS, bg=(l == 0))))
            stages.append(("proj%d" % l, lambda l=l: ph_proj(k, l)))
            stages.append(("proj2_%d" % l, lambda l=l: ph_proj2(k, l)))
            stages.append(("mla%d" % l, lambda l=l: ph_mla(k, l, bg_layer=(1 if l == 0 else None))))
            stages.append(("diff%d" % l, lambda l=l: ph_diff(k, l)))
            stages.append(("four%d" % l, lambda l=l: ph_fourier(k, l)))
            stages.append(("wout%d" % l, lambda l=l: ph_wout(k, l)))
            stages.append(("ffn%d_1" % l, lambda l=l: ph_ffn(k, l, 1, (FULL_BLOCKS if l == 0 else LAT_BLOCKS)[:NB2])))
        stages.append(("final", lambda: ph_final(k)))
        k.stage_mm = []
        for name, fn in stages:
            if name not in skip:
                fn()
            k.stage_mm.append((name, MM_COUNT[0]))
            if stop_after == name:
                break
        P.begin_phase()
        P.wait("sync")
        P.emit()
        P.end_phase()
        k.n_inst = P.n_inst
    return nc, k


_CONST_CACHE = {}


def host_consts():
    if _CONST_CACHE:
        return _CONST_CACHE
    f = np.float32
    s = np.arange(NLAT)
    pos = [s // 64, s % 64]
    cosT = np.ones((64, NTOK), np.float64)
    sinT = np.zeros((64, NTOK), np.float64)
    for i in range(64):
        jj = i % 16
        inv = np.float32(10000.0) ** np.float32(-2.0 * jj / 32.0)
        ang = pos[i // 32].astype(np.float32) * np.float32(inv)
        cosT[i, NCTX:] = np.cos(ang.astype(np.float64))
        sn = np.sin(ang.astype(np.float64))
        sinT[i, NCTX:] = -sn if (i % 32) < 16 else sn
    _CONST_CACHE["cosT"] = np.ascontiguousarray(np.concatenate([cosT, cosT], 0).astype(f))
    _CONST_CACHE["sinT"] = np.ascontiguousarray(np.concatenate([sinT, sinT], 0).astype(f))
    c = np.arange(128)
    ang = 2 * np.pi * ((c[:, None] * c[None, :]) % 128) / 128.0
    _CONST_CACHE["ccsc"] = np.ascontiguousarray(np.concatenate([np.cos(ang), -np.sin(ang)], 1).astype(f))
    s2 = np.arange(2048, dtype=np.int64)
    ang = 2 * np.pi * ((s2[:, None] * s2[None, :]) % 2048) / 2048.0
    _CONST_CACHE["cs2048"] = np.ascontiguousarray((np.cos(ang) / 512.0).astype(f))
    _CONST_CACHE["ss2048"] = np.ascontiguousarray((np.sin(ang) / 512.0).astype(f))
    s3 = np.arange(256, dtype=np.int64)
    ang = 2 * np.pi * ((s3[:, None] * s3[None, :]) % 256) / 256.0
    nrm = math.sqrt(256.0 * 128.0)
    _CONST_CACHE["c256"] = np.ascontiguousarray((np.cos(ang) / nrm).astype(f))
    _CONST_CACHE["s256"] = np.ascontiguousarray((np.sin(ang) / nrm).astype(f))
    _CONST_CACHE["ident"] = np.eye(128, dtype=f)
    rp = np.zeros((128, 128), f)
    rp[np.arange(128) ^ 16, np.arange(128)] = 1.0
    _CONST_CACHE["rperm"] = rp
    return _CONST_CACHE


def swap64(a):
    m = a.shape[1] // 64
    idx = (np.arange(m)[:, None] * 64 + (np.arange(64) ^ 16)[None, :]).reshape(-1)
    return a[:, idx]


def host_shared(inp):
    f = np.float32
    w2 = []
    for l in range(2):
        wi = np.asarray(inp["w_in"][l], f)
        ckv, kr, dk, dv = wi[:, 0:256], wi[:, 256:320], wi[:, 320:832], wi[:, 832:1344]
        cq, dq, u = wi[:, 1344:1856], wi[:, 1856:2368], wi[:, 2368:2880]
        w2.append(np.concatenate([ckv, kr, kr, swap64(kr), swap64(kr), dk, swap64(dk), cq, dq, swap64(dq), u, dv], axis=1))
    ukv = []
    uq = []
    for l in range(2):
        r = np.asarray(inp["mla_w_ukv"][l], f).reshape(256, 8, 256)
        ukv.append(np.concatenate([r[:, :, :128].reshape(256, 1024), r[:, :, 128:].reshape(256, 1024)], 1))
        r = np.asarray(inp["mla_w_uq"][l], f).reshape(512, 8, 192)
        rope = r[:, :, 128:].reshape(512, 512)
        uq.append(np.concatenate([r[:, :, :128].reshape(512, 1024), rope, swap64(rope)], 1))
    g = np.asarray(inp["norm_g"], f)
    sh = {
        "ada_w": np.asarray(inp["ada_w"], f),
        "adabT": np.ascontiguousarray(np.stack([np.asarray(inp["ada_b"][l], f).reshape(144, 128).T for l in range(2)], axis=1).reshape(128, 288)),
        "gT": np.ascontiguousarray(g.reshape(2, 3, 16, 128).transpose(3, 0, 1, 2).reshape(128, 96)),
        "ffn_wg": np.asarray(inp["ffn_wg"], f),
        "ffn_wu": np.asarray(inp["ffn_wu"], f),
        "ffn_wd": np.asarray(inp["ffn_wd"], f),
        "w_in2": np.ascontiguousarray(np.stack(w2, 0)),
        "w_ukv2": np.ascontiguousarray(np.stack(ukv, 0)),
        "w_uq2": np.ascontiguousarray(np.stack(uq, 0)),
        "w_out": np.asarray(inp["w_out"], f),
        "kvnT": np.ascontiguousarray(np.asarray(inp["mla_kv_norm"], f).reshape(2, 2, 128).transpose(2, 0, 1).reshape(128, 4)),
        "qnormT": np.ascontiguousarray(np.asarray(inp["mla_q_norm"], f).reshape(2, 4, 128).transpose(2, 0, 1).reshape(128, 8)),
        "sublnT": np.ascontiguousarray(np.asarray(inp["diff_subln"], f).T),
        "lamrep": np.ascontiguousarray(np.broadcast_to(np.asarray(inp["diff_lambda"], f).reshape(1, 512), (128, 512))),
        "fnT": np.ascontiguousarray(np.asarray(inp["final_norm"], f).reshape(16, 128).T),
    }
    sh.update(host_consts())
    return sh


def host_inputs(inp, b, shared=None):
    f = np.float32
    if shared is None:
        shared = host_shared(inp)
    c = np.asarray(inp["c"][b], f)
    cc = np.asarray(inp["c_ctx"], f)
    cT = np.stack([c.reshape(16, 128).T, cc.reshape(16, 128).T], axis=-1).reshape(128, 32)
    d = dict(shared)
    d["x"] = np.ascontiguousarray(inp["x"][b], dtype=f)
    d["ctx"] = np.ascontiguousarray(inp["ctx"][b], dtype=f)
    d["cT"] = np.ascontiguousarray(cT)
    return d


def kernel(**inputs):
    nc, k = build()
    shared = host_shared(inputs)
    in_maps = [host_inputs(inputs, b, shared) for b in range(8)]
    res = run_bass_kernel_spmd(nc, in_maps, core_ids=list(range(8)))
    return np.stack([np.asarray(r["out"], dtype=np.float32) for r in res.results], axis=0)
```
